# Optimizing a Trainium2 kernel written in Bass

```python
import jax, jax.numpy as jnp
from jax import lax
import numpy as np

D_MODEL = 2048
BATCH = 2
SEQ = 16384
DEPTH = 2

N_MIXERS = 2
N_LAYERS_A = (DEPTH + 1) // 2
N_LAYERS_B = DEPTH // 2
D_FF = 4 * D_MODEL
PLE_DIM = 256
ROPE_THETA = 500000.0
EPS = 1e-6

GM_WIDTH = D_MODEL
GM_CHUNK = 128
GM_GROUPS = 16
GM_GROUP_DIM = GM_WIDTH // GM_GROUPS

NSA_HEADS = 16
NSA_KV_GROUPS = 4
NSA_HPG = NSA_HEADS // NSA_KV_GROUPS
HEAD_DIM = 128
ROT_DIM = HEAD_DIM // 4
CMP_LEN = 32
CMP_STRIDE = 16
CMP_HIDDEN = 4 * HEAD_DIM
SEL_BLOCK = 64
SEL_TOPK = 16
WINDOW = 512
Q_BLOCK = 128
NSA_IN_DIM = NSA_HEADS * HEAD_DIM + 6 * NSA_KV_GROUPS * HEAD_DIM + 3 * NSA_HEADS
NEG_INF = -1e30
FORCE_SCORE = 1e9

kernel_name = "hybrid_gmlp_nsa_trunk"


def rmsnorm(x, g):
    xf = x.astype(jnp.float32)
    y = xf * lax.rsqrt(jnp.mean(xf * xf, axis=-1, keepdims=True) + EPS)
    return (y * g.astype(jnp.float32)).astype(x.dtype)


def layernorm(x, g, b):
    xf = x.astype(jnp.float32)
    mu = jnp.mean(xf, axis=-1, keepdims=True)
    var = jnp.mean(jnp.square(xf - mu), axis=-1, keepdims=True)
    y = (xf - mu) * lax.rsqrt(var + EPS)
    return (y * g.astype(jnp.float32) + b.astype(jnp.float32)).astype(x.dtype)


def rope(x, pos):
    half = ROT_DIM // 2
    inv = jnp.power(jnp.float32(ROPE_THETA), -jnp.arange(half, dtype=jnp.float32) * 2.0 / ROT_DIM)
    ang = pos.astype(jnp.float32)[:, None] * inv[None, :]
    cos = jnp.cos(ang)[:, None, :]
    sin = jnp.sin(ang)[:, None, :]
    xr = x[..., :ROT_DIM].astype(jnp.float32)
    x1, x2 = xr[..., :half], xr[..., half:]
    rot = jnp.concatenate([x1 * cos - x2 * sin, x2 * cos + x1 * sin], axis=-1).astype(x.dtype)
    return jnp.concatenate([rot, x[..., ROT_DIM:]], axis=-1)


def masked_softmax(s, mask):
    s = jnp.where(mask, s, NEG_INF)
    p = jax.nn.softmax(s, axis=-1)
    return jnp.where(mask, p, 0.0)


def chunked_gmlp(h, w_in, ln_g, ln_b, w_s, b_s, w_out):
    B, T, _ = h.shape
    z = jax.nn.gelu(h @ w_in)
    u, v = jnp.split(z, 2, axis=-1)
    v = layernorm(v, ln_g, ln_b)
    v = v.reshape(B, T // GM_CHUNK, GM_CHUNK, GM_GROUPS, GM_GROUP_DIM)
    causal = jnp.tril(jnp.ones((GM_CHUNK, GM_CHUNK), dtype=bool))
    ws = jnp.where(causal[None], w_s, jnp.zeros((), w_s.dtype))
    sv = jnp.einsum('gts,bcsgd->bctgd', ws, v) + b_s.T[None, None, :, :, None]
    return (u * sv.reshape(B, T, GM_WIDTH)) @ w_out


def compress(k, pe, w1, w2):
    B, T, G, dk = k.shape
    nc = (T - CMP_LEN) // CMP_STRIDE + 1
    idx = np.arange(nc)[:, None] * CMP_STRIDE + np.arange(CMP_LEN)[None, :]
    blk = k[:, idx] + pe[None, None, :, None, :]
    blk = blk.transpose(0, 1, 3, 2, 4).reshape(B, nc, G, CMP_LEN * dk)
    return jax.nn.gelu(blk @ w1) @ w2


def native_sparse_attention(h, w_in, kc_pe, kc_w1, kc_w2, vc_pe, vc_w1, vc_w2, w_out):
    B, T, _ = h.shape
    G, N, Dh, H = NSA_KV_GROUPS, NSA_HPG, HEAD_DIM, NSA_HEADS
    sizes = [H * Dh] + [G * Dh] * 6 + [3 * H]
    splits = [int(s) for s in np.cumsum(sizes)[:-1]]
    q, kc, vc, ks, vs, kw, vw, gl = jnp.split(h @ w_in, splits, axis=-1)
    pos = jnp.arange(T)
    q = (rope(q.reshape(B, T, H, Dh), pos) * (HEAD_DIM ** -0.5)).reshape(B, T, G, N, Dh)
    nc = (T - CMP_LEN) // CMP_STRIDE + 1
    cmp_end_np = np.arange(nc) * CMP_STRIDE + CMP_LEN - 1
    cmp_end = jnp.asarray(cmp_end_np, dtype=jnp.int32)
    kc = rope(compress(kc.reshape(B, T, G, Dh), kc_pe, kc_w1, kc_w2), cmp_end)
    vc = compress(vc.reshape(B, T, G, Dh), vc_pe, vc_w1, vc_w2)
    ks = rope(ks.reshape(B, T, G, Dh), pos)
    kw = rope(kw.reshape(B, T, G, Dh), pos)
    ns = T // SEL_BLOCK
    sel_k = min(SEL_TOPK, ns)
    ks_b = ks.reshape(B, ns, SEL_BLOCK, G, Dh).transpose(0, 3, 1, 2, 4)
    vs_b = vs.reshape(B, ns, SEL_BLOCK, G, Dh).transpose(0, 3, 1, 2, 4)
    pad = ((0, 0), (WINDOW, 0), (0, 0), (0, 0))
    kw_pad = jnp.pad(kw, pad)
    vw_pad = jnp.pad(vw.reshape(B, T, G, Dh), pad)
    gates = jax.nn.sigmoid(gl).reshape(B, T, 3, G, N, 1)
    cs = np.arange(nc)[:, None] * CMP_STRIDE
    ss = np.arange(ns)[None, :] * SEL_BLOCK
    ov = np.clip(np.minimum(cs + CMP_LEN, ss + SEL_BLOCK) - np.maximum(cs, ss), 0, None) / CMP_STRIDE
    cmp_to_sel = jnp.asarray(ov, dtype=jnp.float32)
    bi = jnp.arange(B)[:, None, None, None]
    gi = jnp.arange(G)[None, :, None, None]
    blk_id = jnp.arange(ns)

    def query_block(qb):
        t0 = qb * Q_BLOCK
        tpos = t0 + jnp.arange(Q_BLOCK)
        qblk = lax.dynamic_slice_in_dim(q, t0, Q_BLOCK, axis=1)
        s_c = jnp.einsum('bqgnd,bcgd->bgnqc', qblk, kc, preferred_element_type=jnp.float32)
        p_c = masked_softmax(s_c, cmp_end[None, :] <= tpos[:, None])
        o_c = jnp.einsum('bgnqc,bcgd->bqgnd', p_c.astype(vc.dtype), vc)
        imp = jnp.einsum('bgnqc,cs->bgqs', p_c, cmp_to_sel)
        cur = tpos // SEL_BLOCK
        valid = blk_id[None, :] <= cur[:, None]
        forced = (blk_id[None, :] == 0) | (blk_id[None, :] == cur[:, None]) | (blk_id[None, :] == cur[:, None] - 1)
        score = jnp.where(valid, jnp.where(forced, FORCE_SCORE, imp), NEG_INF)
        top_s, top_i = lax.top_k(score, sel_k)
        kg = ks_b[bi, gi, top_i]
        vg = vs_b[bi, gi, top_i]
        s_s = jnp.einsum('bqgnd,bgqkjd->bgnqkj', qblk, kg, preferred_element_type=jnp.float32)
        kpos = top_i[..., None] * SEL_BLOCK + jnp.arange(SEL_BLOCK)
        m_s = (kpos <= tpos[None, None, :, None, None]) & (top_s > NEG_INF * 0.5)[..., None]
        p_s = masked_softmax(s_s.reshape(B, G, N, Q_BLOCK, -1),
                             m_s.reshape(B, G, 1, Q_BLOCK, -1)).reshape(s_s.shape)
        o_s = jnp.einsum('bgnqkj,bgqkjd->bqgnd', p_s.astype(vg.dtype), vg)
        kwin = lax.dynamic_slice_in_dim(kw_pad, t0, WINDOW + Q_BLOCK, axis=1)
        vwin = lax.dynamic_slice_in_dim(vw_pad, t0, WINDOW + Q_BLOCK, axis=1)
        kp = t0 - WINDOW + jnp.arange(WINDOW + Q_BLOCK)
        m_w = (kp[None, :] <= tpos[:, None]) & (kp[None, :] > tpos[:, None] - WINDOW) & (kp[None, :] >= 0)
        s_w = jnp.einsum('bqgnd,bkgd->bgnqk', qblk, kwin, preferred_element_type=jnp.float32)
        p_w = masked_softmax(s_w, m_w)
        o_w = jnp.einsum('bgnqk,bkgd->bqgnd', p_w.astype(vwin.dtype), vwin)
        g = lax.dynamic_slice_in_dim(gates, t0, Q_BLOCK, axis=1)
        o = g[:, :, 0] * o_c + g[:, :, 1] * o_s + g[:, :, 2] * o_w
        return o.reshape(B, Q_BLOCK, H * Dh)

    out = lax.map(query_block, jnp.arange(T // Q_BLOCK))
    out = out.transpose(1, 0, 2, 3).reshape(B, T, H * Dh)
    return out @ w_out


def sqrelu_mlp(h, w_up, w_down):
    return jnp.square(jax.nn.relu(h @ w_up)) @ w_down


def setup_inputs(seed: int = 0) -> dict:
    key = jax.random.key(seed)
    ks = jax.random.split(key, 32)
    f32 = jnp.float32

    def nrm(k, shape, scale):
        return jax.random.normal(k, shape, f32) * scale

    def gain(k, shape):
        return 1.0 + 0.05 * jax.random.normal(k, shape, f32)

    return {
        "x": nrm(ks[0], (BATCH, SEQ, D_MODEL), 1.0),
        "p": nrm(ks[1], (DEPTH, BATCH, SEQ, PLE_DIM), 1.0),
        "norm_mix": gain(ks[2], (DEPTH, D_MODEL)),
        "norm_ffn": gain(ks[3], (DEPTH, D_MODEL)),
        "norm_ple": gain(ks[4], (DEPTH, D_MODEL)),
        "ffn_up": nrm(ks[5], (DEPTH, D_MODEL, D_FF), D_MODEL ** -0.5),
        "ffn_down": nrm(ks[6], (DEPTH, D_FF, D_MODEL), D_FF ** -0.5),
        "ple_proj": nrm(ks[7], (DEPTH, PLE_DIM, D_MODEL), PLE_DIM ** -0.5),
        "ple_gate": nrm(ks[8], (DEPTH, D_MODEL, D_MODEL), D_MODEL ** -0.5),
        "gm_in": nrm(ks[9], (N_LAYERS_A, D_MODEL, 2 * GM_WIDTH), D_MODEL ** -0.5),
        "gm_ln_g": gain(ks[10], (N_LAYERS_A, GM_WIDTH)),
        "gm_ln_b": nrm(ks[11], (N_LAYERS_A, GM_WIDTH), 0.02),
        "gm_ws": nrm(ks[12], (N_LAYERS_A, GM_GROUPS, GM_CHUNK, GM_CHUNK), GM_CHUNK ** -0.5),
        "gm_bs": 1.0 + nrm(ks[13], (N_LAYERS_A, GM_GROUPS, GM_CHUNK), 0.1),
        "gm_out": nrm(ks[14], (N_LAYERS_A, GM_WIDTH, D_MODEL), GM_WIDTH ** -0.5),
        "nsa_in": nrm(ks[15], (N_LAYERS_B, D_MODEL, NSA_IN_DIM), D_MODEL ** -0.5),
        "nsa_kc_pe": nrm(ks[16], (N_LAYERS_B, CMP_LEN, HEAD_DIM), 0.1),
        "nsa_kc_w1": nrm(ks[17], (N_LAYERS_B, CMP_LEN * HEAD_DIM, CMP_HIDDEN), (CMP_LEN * HEAD_DIM) ** -0.5),
        "nsa_kc_w2": nrm(ks[18], (N_LAYERS_B, CMP_HIDDEN, HEAD_DIM), CMP_HIDDEN ** -0.5),
        "nsa_vc_pe": nrm(ks[19], (N_LAYERS_B, CMP_LEN, HEAD_DIM), 0.1),
        "nsa_vc_w1": nrm(ks[20], (N_LAYERS_B, CMP_LEN * HEAD_DIM, CMP_HIDDEN), (CMP_LEN * HEAD_DIM) ** -0.5),
        "nsa_vc_w2": nrm(ks[21], (N_LAYERS_B, CMP_HIDDEN, HEAD_DIM), CMP_HIDDEN ** -0.5),
        "nsa_out": nrm(ks[22], (N_LAYERS_B, NSA_HEADS * HEAD_DIM, D_MODEL), (NSA_HEADS * HEAD_DIM) ** -0.5),
        "final_norm": gain(ks[23], (D_MODEL,)),
    }


def reference(x, p, norm_mix, norm_ffn, norm_ple, ffn_up, ffn_down, ple_proj, ple_gate,
              gm_in, gm_ln_g, gm_ln_b, gm_ws, gm_bs, gm_out,
              nsa_in, nsa_kc_pe, nsa_kc_w1, nsa_kc_w2, nsa_vc_pe, nsa_vc_w1, nsa_vc_w2, nsa_out,
              final_norm):
    for i in range(DEPTH):
        h = rmsnorm(x, norm_mix[i])
        j = i // N_MIXERS
        if i % N_MIXERS == 0:
            mix = chunked_gmlp(h, gm_in[j], gm_ln_g[j], gm_ln_b[j], gm_ws[j], gm_bs[j], gm_out[j])
        else:
            mix = native_sparse_attention(h, nsa_in[j], nsa_kc_pe[j], nsa_kc_w1[j], nsa_kc_w2[j],
                                          nsa_vc_pe[j], nsa_vc_w1[j], nsa_vc_w2[j], nsa_out[j])
        x = x + mix
        x = x + sqrelu_mlp(rmsnorm(x, norm_ffn[i]), ffn_up[i], ffn_down[i])
        gate = jax.nn.sigmoid(rmsnorm(x, norm_ple[i]) @ ple_gate[i])
        x = x + gate * (p[i] @ ple_proj[i])
    return rmsnorm(x, final_norm)
```

```python
import bisect
import contextlib
import numpy as np
import ml_dtypes
import concourse.bass as bass
import concourse.mybir as mybir
from concourse.bass_utils import run_bass_kernel_spmd

F32 = mybir.dt.float32
BF16 = mybir.dt.bfloat16
AF = mybir.ActivationFunctionType
ALU = mybir.AluOpType
NPBF = ml_dtypes.bfloat16

D = 2048
DFF = 8192
T = 16384
B = 2
NCORES = 8
EPS = 1e-6
SEM_LIMIT = 1000


class Buf:
    __slots__ = ("writer", "readers", "const")

    def __init__(self):
        self.writer = None
        self.readers = {}
        self.const = False


class Tracker:
    def __init__(self, nc, es):
        self.nc = nc
        self.es = es
        self.engs = {"pe": nc.tensor, "act": nc.scalar, "dve": nc.vector, "pool": nc.gpsimd, "sp": nc.sync}
        self.sems = {k: [] for k in self.engs}
        self.seq = {k: 0 for k in self.engs}
        self.last = {k: None for k in self.engs}
        self.sig_seqs = {k: [] for k in self.engs}
        self.known = {}
        self.dma_sems = {}
        self.dma_uses = {}
        self.dma_rr = {}
        for q, n in (("sp", 20), ("pool", 12), ("act", 6)):
            self.dma_sems[q] = [es.enter_context(nc.semaphore(f"dq_{q}_{i}")) for i in range(n)]
            self.dma_uses[q] = [0] * n
            self.dma_rr[q] = 0
        self.nwaits = 0
        import os
        self.dummy = [es.enter_context(nc.semaphore(f"dummy{i}")) for i in range(int(os.environ.get("DUMMY_SEMS", "0")))]

    def _eng_sem(self, e, epoch):
        while len(self.sems[e]) <= epoch:
            self.sems[e].append(self.es.enter_context(self.nc.semaphore(f"es_{e}_{len(self.sems[e])}")))
        return self.sems[e][epoch]

    def _wait(self, waiter, ev):
        if ev is None:
            return
        if ev[0] == "dma":
            _, q, si, val = ev
            key = (waiter, "dma", q, si)
            if self.known.get(key, 0) >= val:
                return
            self.engs[waiter].wait_ge(self.dma_sems[q][si], val)
            self.known[key] = val
            self.nwaits += 1
            return
        e, seq = ev
        sigs = self.sig_seqs[e]
        i = bisect.bisect_left(sigs, seq)
        if i == len(sigs):
            lseq, lins = self.last[e]
            assert lseq >= seq
            k = len(sigs)
            lins.then_inc(self._eng_sem(e, k // SEM_LIMIT), 1)
            sigs.append(lseq)
        k = i
        epoch, val = k // SEM_LIMIT, k % SEM_LIMIT + 1
        key = (waiter, e, epoch)
        if self.known.get(key, 0) >= val:
            return
        self.engs[waiter].wait_ge(self._eng_sem(e, epoch), val)
        self.known[key] = val
        for ep in range(epoch):
            self.known[(waiter, e, ep)] = SEM_LIMIT
        self.nwaits += 1

    def _deps(self, eng, reads, writes):
        deps = {}

        def add(ev, is_write_dep):
            if ev is None:
                return
            if ev[0] == "dma":
                deps[ev] = ev
            else:
                e, seq = ev
                if e == "pe" and eng == "pe" and is_write_dep:
                    return
                if deps.get(e, (e, 0))[1] < seq:
                    deps[e] = ev

        for b in reads:
            add(b.writer, False)
        for b in writes:
            add(b.writer, True)
            for r in b.readers.values():
                if not (r[0] == eng and eng == "pe" and False):
                    add(r, False)
        return list(deps.values())

    def _record(self, ev, reads, writes):
        for b in reads:
            if b.const:
                continue
            if ev[0] == "dma":
                b.readers[ev] = ev
            else:
                b.readers[ev[0]] = ev
        for b in writes:
            b.writer = ev
            b.readers = {}

    def op(self, eng, fn, reads=(), writes=()):
        for d in self._deps(eng, reads, writes):
            if d[0] == eng and eng == "pe":
                continue
            self._wait(eng, d)
        ins = fn(self.engs[eng])
        self.seq[eng] += 1
        ev = (eng, self.seq[eng])
        self.last[eng] = (self.seq[eng], ins)
        self._record(ev, reads, writes)
        return ins

    def dma(self, q, out, in_, reads=(), writes=(), **kw):
        for d in self._deps(q, reads, writes):
            self._wait(q, d)
        n = len(self.dma_sems[q])
        si = self.dma_rr[q]
        self.dma_rr[q] = (si + 1) % n
        uses = self.dma_uses[q][si]
        if uses > 0:
            self._wait(q, ("dma", q, si, 16 * uses))
        ins = self.engs[q].dma_start(out=out, in_=in_, **kw)
        ins.then_inc(self.dma_sems[q][si], 16)
        self.dma_uses[q][si] = uses + 1
        ev = ("dma", q, si, 16 * (uses + 1))
        self._record(ev, reads, writes)
        return ev

    def finish(self, out_events):
        for ev in out_events:
            self._wait("sp", ev)


TT = 512
NSUB = 4
NTILE = 4096 // TT
WSLOT = 8192


class Dense:
    def __init__(self, mode, ntiles=NTILE):
        self.mode = mode
        self.ntiles = ntiles
        nc = self.nc = bass.Bass("TRN2", target_bir_lowering=False)
        self.es = contextlib.ExitStack()

    def dram_in(self, name, shape, dt):
        return self.nc.dram_tensor(name, list(shape), dt, kind="ExternalInput").ap()

    def dram_out(self, name, shape, dt):
        return self.nc.dram_tensor(name, list(shape), dt, kind="ExternalOutput").ap()

    def sb(self, name, shape, dt):
        return self.es.enter_context(self.nc.sbuf_tensor(name, list(shape), dt))

    def build(self):
        nc, es = self.nc, self.es
        with es:
            self.tr = Tracker(nc, es)
            self._build()
        return nc

    def wpiece(self, wap, r0, nk, c0, ncols):
        assert nk * ncols <= WSLOT
        i = self.wrr
        self.wrr = (i + 1) % len(self.wring)
        t, b = self.wring[i]
        dst = t[:, 0:nk * ncols].rearrange("p (k n) -> p k n", k=nk)
        src = wap[r0 * 128:(r0 + nk) * 128, c0:c0 + ncols].rearrange("(k p) n -> p k n", p=128)
        q = "pool"
        self.wq += 1
        self.tr.dma(q, dst, src, writes=[b])
        return dst, b

    def bcast_load(self, vec_ap):
        i = self.grr
        self.grr = (i + 1) % len(self.gbc)
        t, b = self.gbc[i]
        self.tr.dma("sp", t[:], vec_ap.partition_broadcast(128), writes=[b])
        return t, b

    def next_ps(self):
        i = self.prr
        self.prr = (i + 1) % 8
        return self.ps[i]

    def rmsnorm_T(self, gvec_ap):
        tr = self.tr
        g_t, g_b = self.bcast_load(gvec_ap)
        for s in range(NSUB):
            xs = self.xres[:, s, :]
            xb = self.xres_b[s]
            junk, jb = self.xn_tm[s % 2]
            ss, ssb = self.small[s % 2]
            tr.op("act", lambda e: e.activation(out=junk[:], in_=xs, func=AF.Square, accum_out=ss[:, 0:1]),
                  reads=[xb], writes=[jb, ssb])
            tr.op("dve", lambda e: e.tensor_scalar(out=ss[:, 1:2], in0=ss[:, 0:1], scalar1=1.0 / D, scalar2=EPS,
                                                   op0=ALU.mult, op1=ALU.add), reads=[ssb], writes=[ssb])
            tr.op("act", lambda e: e.activation(out=ss[:, 2:3], in_=ss[:, 1:2], func=AF.Sqrt), reads=[ssb], writes=[ssb])
            tr.op("dve", lambda e: e.reciprocal(out=ss[:, 3:4], in_=ss[:, 2:3]), reads=[ssb], writes=[ssb])
            tr.op("dve", lambda e: e.scalar_tensor_tensor(out=junk[:], in0=xs, scalar=ss[:, 3:4], in1=g_t[:],
                                                          op0=ALU.mult, op1=ALU.mult),
                  reads=[xb, ssb, g_b], writes=[jb])
            self.transpose_into(junk, jb, 16, self.xnT, self.xnT_b, s)

    def transpose_into(self, src, srcb, nk, dstT, dstb, s):
        tr = self.tr
        for k0 in range(0, nk, 8):
            n = min(8, nk - k0)
            pt, pb = self.next_ps()
            pv = pt[:].bitcast(BF16)
            for j in range(n):
                k = k0 + j
                tr.op("pe", lambda e: e.transpose(out=pv[:, j * 128:(j + 1) * 128], in_=src[:, k * 128:(k + 1) * 128],
                                                  identity=self.ident[:]),
                      reads=[srcb, self.ident_b], writes=[pb])
            eng = "act" if (k0 // 8) % 2 == 0 else "dve"
            o = dstT[:, k0:k0 + n, s * 128:(s + 1) * 128]
            i = pv[:, 0:n * 128].rearrange("p (k t) -> p k t", k=n)
            if eng == "act":
                tr.op("act", lambda e: e.copy(out=o, in_=i), reads=[pb], writes=[dstb])
            else:
                tr.op("dve", lambda e: e.tensor_copy(out=o, in_=i), reads=[pb], writes=[dstb])

    def proj_fm(self, wap, r0, nk, c0, ncols, srcT, srcb, evac):
        tr = self.tr
        pcols = WSLOT // nk
        for p0 in range(0, ncols, pcols):
            pc = min(pcols, ncols - p0)
            w, wb = self.wpiece(wap, r0, nk, c0 + p0, pc)
            for f in range(pc // 128):
                pt, pb = self.next_ps()
                for k in range(nk):
                    tr.op("pe", lambda e: e.matmul(pt[:], lhsT=w[:, k, f * 128:(f + 1) * 128], rhs=srcT[:, k, :],
                                                   start=(k == 0), stop=(k == nk - 1)),
                          reads=[wb] + list(srcb), writes=[pb])
                evac((p0 // 128) + f, pt, pb)

    def proj_tm(self, wap, r0, nk, c0, ncols, srcT, srcb, evac, blk=512):
        tr = self.tr
        kper = max(1, WSLOT // blk)
        half = 0
        for cb, cc in enumerate(range(0, ncols, blk)):
            w_ = min(blk, ncols - cc)
            pss = [self.ps[(half * 4 + s)] for s in range(NSUB)]
            half ^= 1
            for k0 in range(0, nk, kper):
                kn = min(kper, nk - k0)
                w, wb = self.wpiece(wap, r0 + k0, kn, c0 + cc, w_)
                for kk in range(kn):
                    k = k0 + kk
                    for s in range(NSUB):
                        pt, pb = pss[s]
                        tr.op("pe", lambda e: e.matmul(pt[:, 0:w_], lhsT=srcT[:, k, s * 128:(s + 1) * 128], rhs=w[:, kk, :],
                                                       start=(k == 0), stop=(k == nk - 1)),
                              reads=[wb] + list(srcb), writes=[pb])
            for s in range(NSUB):
                pt, pb = pss[s]
                evac(cb, s, pt, pb, w_)

    def resid_add(self, cb, s, pt, pb, w_):
        xs = self.xres[:, s, cb * 512:cb * 512 + w_]
        self.tr.op("dve", lambda e: e.tensor_tensor(out=xs, in0=xs, in1=pt[:, 0:w_], op=ALU.add),
                   reads=[pb, self.xres_b[s]], writes=[self.xres_b[s]])

    def ffn(self, li):
        tr = self.tr
        self.rmsnorm_T(self.a["norm_ffn"][li:li + 1, :])
        for h in range(2):
            def evac_up(f, pt, pb):
                tmp, tb = self.tmpf[f % 2]
                tr.op("act", lambda e: e.activation(out=tmp[:], in_=pt[:], func=AF.Relu), reads=[pb], writes=[tb])
                eng = "dve"
                tr.op(eng, lambda e: e.tensor_tensor(out=self.hT[:, f, :], in0=tmp[:], in1=tmp[:], op=ALU.mult),
                      reads=[tb], writes=[self.hT_b, self.hT_b2])
            self.proj_fm(self.a["ffn_up"][li], 0, 16, h * 4096, 4096, self.xnT, [self.xnT_b], evac_up)
            self.proj_tm(self.a["ffn_down"][li], h * 32, 32, 0, 2048, self.hT, [self.hT_b, self.hT_b2], self.resid_add)

    def ple(self, li, tok0):
        tr = self.tr
        self.rmsnorm_T(self.a["norm_ple"][li:li + 1, :])
        pf, pfb = self.p_f
        tr.dma("sp", pf[:], self.a["p"][tok0:tok0 + TT, :].rearrange("(s p) c -> p s c", p=128), writes=[pfb])
        for s in range(NSUB):
            pbf, pbb = self.p_bf[s % 2]
            tr.op("dve", lambda e: e.tensor_copy(out=pbf[:], in_=pf[:, s, :]), reads=[pfb], writes=[pbb])
            self.transpose_into(pbf, pbb, 2, self.pT, self.pT_b, s)
        for cb in range(4):
            pg = [self.ps[s] for s in range(4)]
            pp = [self.ps[4 + s] for s in range(4)]
            w, wb = self.wpiece(self.a["ple_gate"][li], 0, 16, cb * 512, 512)
            wple, wpb = self.wpiece(self.a["ple_proj"][li], 0, 2, cb * 512, 512)
            for k in range(16):
                for s in range(NSUB):
                    pt, pb = pg[s]
                    tr.op("pe", lambda e: e.matmul(pt[:], lhsT=self.xnT[:, k, s * 128:(s + 1) * 128], rhs=w[:, k, :],
                                                   start=(k == 0), stop=(k == 15)), reads=[wb, self.xnT_b], writes=[pb])
            for k in range(2):
                for s in range(NSUB):
                    pt, pb = pp[s]
                    tr.op("pe", lambda e: e.matmul(pt[:], lhsT=self.pT[:, k, s * 128:(s + 1) * 128],
                                                   rhs=wple[:, k, :],
                                                   start=(k == 0), stop=(k == 1)), reads=[wpb, self.pT_b], writes=[pb])
            for s in range(NSUB):
                tmp, tb = self.tmpf[s % 2]
                tr.op("act", lambda e: e.activation(out=tmp[:], in_=pg[s][0][:], func=AF.Sigmoid), reads=[pg[s][1]], writes=[tb])
                tr.op("dve", lambda e: e.tensor_tensor(out=tmp[:], in0=tmp[:], in1=pp[s][0][:], op=ALU.mult),
                      reads=[tb, pp[s][1]], writes=[tb])
                xs = self.xres[:, s, cb * 512:(cb + 1) * 512]
                tr.op("dve", lambda e: e.tensor_tensor(out=xs, in0=xs, in1=tmp[:], op=ALU.add),
                      reads=[tb, self.xres_b[s]], writes=[self.xres_b[s]])

    def gmlp(self):
        tr = self.tr
        a = self.a
        self.rmsnorm_T(a["norm_mix"][0:1, :])
        tr.dma("sp", self.hT[:, 16:24, :].rearrange("p k t -> p (k t)").bitcast(F32), a["bsT"].partition_broadcast(128),
               writes=[self.hT_b2])

        def evac_u(f, pt, pb):
            tr.op("act", lambda e: e.activation(out=self.hT[:, f, :], in_=pt[:], func=AF.Gelu_apprx_tanh),
                  reads=[pb], writes=[self.hT_b])
        self.proj_fm(a["gm_in"], 0, 16, 0, 2048, self.xnT, [self.xnT_b], evac_u)

        def evac_v(cb, s, pt, pb, w_):
            vs = self.v_f[:, s, cb * 512:(cb + 1) * 512]
            tr.op("act", lambda e: e.activation(out=vs, in_=pt[:], func=AF.Gelu_apprx_tanh), reads=[pb], writes=[self.v_fb[s]])
            tr.op("dve", lambda e: e.bn_stats(out=self.stats[:, s, cb * 6:(cb + 1) * 6], in_=vs), reads=[self.v_fb[s]],
                  writes=[self.stats_b[s]])
        self.proj_tm(a["gm_in"], 0, 16, 2048, 2048, self.xnT, [self.xnT_b], evac_v)
        lg_t, lg_b = self.bcast_load(a["gm_ln_g"])
        lb_t, lb_b = self.bcast_load(a["gm_ln_b"])
        for s in range(NSUB):
            mv, mvb = self.small[s % 2]
            tr.op("dve", lambda e: e.bn_aggr(out=mv[:, 0:2], in_=self.stats[:, s, :]), reads=[self.stats_b[s]], writes=[mvb])
            tr.op("dve", lambda e: e.tensor_scalar(out=mv[:, 2:3], in0=mv[:, 1:2], scalar1=EPS, scalar2=1.0, op0=ALU.add, op1=ALU.mult),
                  reads=[mvb], writes=[mvb])
            tr.op("act", lambda e: e.activation(out=mv[:, 3:4], in_=mv[:, 2:3], func=AF.Sqrt), reads=[mvb], writes=[mvb])
            tr.op("dve", lambda e: e.reciprocal(out=mv[:, 4:5], in_=mv[:, 3:4]), reads=[mvb], writes=[mvb])
            vs = self.v_f[:, s, :]
            tr.op("dve", lambda e: e.tensor_scalar(out=vs, in0=vs, scalar1=mv[:, 0:1], scalar2=mv[:, 4:5],
                                                   op0=ALU.subtract, op1=ALU.mult), reads=[mvb, self.v_fb[s]], writes=[self.v_fb[s]])
            tr.op("dve", lambda e: e.tensor_tensor(out=vs, in0=vs, in1=lg_t[:], op=ALU.mult), reads=[self.v_fb[s], lg_b],
                  writes=[self.v_fb[s]])
            vl, vlb = self.xn_tm[s % 2]
            tr.op("dve", lambda e: e.tensor_tensor(out=vl[:], in0=vs, in1=lb_t[:], op=ALU.add), reads=[self.v_fb[s], lb_b],
                  writes=[vlb])
            for g4 in range(4):
                pt, pb = self.next_ps()
                for gg in range(4):
                    g = g4 * 4 + gg
                    tr.op("pe", lambda e: e.matmul(pt[:, gg * 128:(gg + 1) * 128], lhsT=vl[:, g * 128:(g + 1) * 128],
                                                   rhs=self.wsT[:, g, :], start=True, stop=True),
                          reads=[vlb, self.wsT_b], writes=[pb])
                tmp, tb = self.tmpf[g4 % 2]
                tv = tmp[:].rearrange("p (g t) -> p g t", g=4)
                tr.op("dve", lambda e: e.tensor_tensor(out=tv, in0=pt[:].rearrange("p (g t) -> p g t", g=4),
                                                       in1=self.bsT[:, g4 * 4:(g4 + 1) * 4, :], op=ALU.add),
                      reads=[pb, self.bsT_b], writes=[tb])
                u = self.hT[:, g4 * 4:(g4 + 1) * 4, s * 128:(s + 1) * 128]
                tr.op("dve", lambda e: e.tensor_tensor(out=u, in0=tv, in1=u, op=ALU.mult), reads=[tb, self.hT_b],
                      writes=[self.hT_b])
        mT = self.hT[:, 0:16, :]
        self.proj_tm(a["gm_out"], 0, 16, 0, 2048, mT, [self.hT_b], self.resid_add)

    def nsa_proj(self, tok0, pos0):
        tr = self.tr
        a = self.a
        self.rmsnorm_T(a["norm_mix"][1:2, :])
        cs, csb = self.cs
        tr.dma("sp", cs[:], a["rope"][tok0:tok0 + TT, :].rearrange("(s p) c -> p s c", p=128), writes=[csb])
        out_evs = self.out_evs

        def evac(cb, s, pt, pb, w_):
            if cb == 10:
                g, gb = self.small[2 + s % 2]
                tr.op("act", lambda e: e.activation(out=g[:, 0:48], in_=pt[:, 0:48], func=AF.Sigmoid), reads=[pb], writes=[gb])
                out_evs.append(tr.dma("sp", a["gates_out"][tok0 + s * 128:tok0 + (s + 1) * 128, :], g[:, 0:48], reads=[gb]))
                return
            tmp, tb = self.tmpf[s % 2]
            is_q = cb < 4
            rope = is_q or cb in (6, 8)
            ob, obb = self.obf[(cb * 4 + s) % 2]
            if not rope:
                tr.op("act", lambda e: e.copy(out=ob[:], in_=pt[:]), reads=[pb], writes=[obb])
            else:
                sc = (128.0 ** -0.5) if is_q else 1.0
                tr.op("act", lambda e: e.mul(out=tmp[:], in_=pt[:], mul=sc), reads=[pb], writes=[tb])
                v = tmp[:].rearrange("p (h d) -> p h d", h=4)
                x1, x2 = v[:, :, 0:16], v[:, :, 16:32]
                cos = cs[:, s, 0:16].unsqueeze(1).to_broadcast([128, 4, 16])
                sin = cs[:, s, 16:32].unsqueeze(1).to_broadcast([128, 4, 16])
                r, rb = self.ropet[s % 2]
                rv = r[:].rearrange("p (j h d) -> p j h d", j=4, h=4)
                tr.op("dve", lambda e: e.tensor_tensor(out=rv[:, 0], in0=x1, in1=cos, op=ALU.mult), reads=[tb, csb], writes=[rb])
                tr.op("dve", lambda e: e.tensor_tensor(out=rv[:, 1], in0=x2, in1=sin, op=ALU.mult), reads=[tb, csb], writes=[rb])
                tr.op("dve", lambda e: e.tensor_tensor(out=rv[:, 2], in0=x2, in1=cos, op=ALU.mult), reads=[tb, csb], writes=[rb])
                tr.op("dve", lambda e: e.tensor_tensor(out=rv[:, 3], in0=x1, in1=sin, op=ALU.mult), reads=[tb, csb], writes=[rb])
                tr.op("dve", lambda e: e.tensor_tensor(out=x1, in0=rv[:, 0], in1=rv[:, 1], op=ALU.subtract), reads=[rb, tb], writes=[tb])
                tr.op("dve", lambda e: e.tensor_tensor(out=x2, in0=rv[:, 2], in1=rv[:, 3], op=ALU.add), reads=[rb, tb], writes=[tb])
                tr.op("act", lambda e: e.copy(out=ob[:], in_=tmp[:]), reads=[tb], writes=[obb])
            out_evs.append(tr.dma("sp", a["qkv_out"][tok0 + s * 128:tok0 + (s + 1) * 128, cb * 512:(cb + 1) * 512], ob[:], reads=[obb]))
        self.proj_tm(a["nsa_in"], 0, 16, 0, 5168, self.xnT, [self.xnT_b], evac)

    def attn_out(self, tok0):
        tr = self.tr
        a = self.a
        for s in range(NSUB):
            ot, otb = self.xn_tm[s % 2]
            tr.dma("sp", ot[:], a["o_in"][tok0 + s * 128:tok0 + (s + 1) * 128, :], writes=[otb])
            self.transpose_into(ot, otb, 16, self.xnT, self.xnT_b, s)
        self.proj_tm(a["nsa_out"], 0, 16, 0, 2048, self.xnT, [self.xnT_b], self.resid_add)

    def final_norm(self, tok0):
        tr = self.tr
        g_t, g_b = self.bcast_load(self.a["final_norm"])
        for s in range(NSUB):
            xs = self.xres[:, s, :]
            xb = self.xres_b[s]
            junk, jb = self.xn_tm[s % 2]
            ss, ssb = self.small[s % 2]
            tr.op("act", lambda e: e.activation(out=junk[:], in_=xs, func=AF.Square, accum_out=ss[:, 0:1]),
                  reads=[xb], writes=[jb, ssb])
            tr.op("dve", lambda e: e.tensor_scalar(out=ss[:, 1:2], in0=ss[:, 0:1], scalar1=1.0 / D, scalar2=EPS,
                                                   op0=ALU.mult, op1=ALU.add), reads=[ssb], writes=[ssb])
            tr.op("act", lambda e: e.activation(out=ss[:, 2:3], in_=ss[:, 1:2], func=AF.Sqrt), reads=[ssb], writes=[ssb])
            tr.op("dve", lambda e: e.reciprocal(out=ss[:, 3:4], in_=ss[:, 2:3]), reads=[ssb], writes=[ssb])
            tr.op("dve", lambda e: e.scalar_tensor_tensor(out=xs, in0=xs, scalar=ss[:, 3:4], in1=g_t[:],
                                                          op0=ALU.mult, op1=ALU.mult),
                  reads=[xb, ssb, g_b], writes=[xb])
            self.out_evs.append(tr.dma("sp", self.a["x_out"][tok0 + s * 128:tok0 + (s + 1) * 128, :], xs, reads=[xb]))

    def _build(self):
        nc, tr = self.nc, self.tr
        mode = self.mode
        ntok = self.ntiles * TT
        a = self.a = {}
        a["x"] = self.dram_in("x", [ntok, D], F32)
        a["p"] = self.dram_in("p", [ntok, 256], F32)
        a["ident"] = self.dram_in("ident", [128, 128], BF16)
        for nm in ("norm_mix", "norm_ffn", "norm_ple"):
            a[nm] = self.dram_in(nm, [2, D], F32)
        a["ffn_up"] = [self.dram_in(f"ffn_up{mode}", [D, DFF], F32)] * 2
        a["ffn_down"] = [self.dram_in(f"ffn_down{mode}", [DFF, D], F32)] * 2
        a["ple_proj"] = [self.dram_in(f"ple_proj{mode}", [256, D], F32)] * 2
        a["ple_gate"] = [self.dram_in(f"ple_gate{mode}", [D, D], F32)] * 2
        li = 0 if mode == 1 else 1
        if mode == 1:
            a["gm_in"] = self.dram_in("gm_in", [D, 4096], F32)
            a["gm_out"] = self.dram_in("gm_out", [D, D], F32)
            a["gm_ln_g"] = self.dram_in("gm_ln_g", [1, D], F32)
            a["gm_ln_b"] = self.dram_in("gm_ln_b", [1, D], F32)
            a["wsT"] = self.dram_in("wsT", [128, 16, 128], F32)
            a["cmask"] = self.dram_in("cmask", [128, 128], F32)
            a["bsT"] = self.dram_in("bsT", [1, 2048], F32)
            a["nsa_in"] = self.dram_in("nsa_in", [D, 5168], F32)
            a["rope"] = self.dram_in("rope", [ntok, 32], F32)
            a["x_out"] = self.dram_out("x_out", [ntok, D], F32)
            a["qkv_out"] = self.dram_out("qkv_out", [ntok, 5120], BF16)
            a["gates_out"] = self.dram_out("gates_out", [ntok, 48], F32)
        else:
            a["o_in"] = self.dram_in("o_in", [ntok, D], BF16)
            a["nsa_out"] = self.dram_in("nsa_out", [D, D], F32)
            a["final_norm"] = self.dram_in("final_norm", [1, D], F32)
            a["x_out"] = self.dram_out("x_out", [ntok, D], F32)

        def mk(name, shape, dt):
            return self.sb(name, shape, dt), Buf()
        self.xres = self.sb("xres", [128, NSUB, D], F32)
        self.xres_b = [Buf() for _ in range(NSUB)]
        self.xn_tm = [mk(f"xn_tm{i}", [128, D], BF16) for i in range(2)]
        self.xnT = self.sb("xnT", [128, 16, TT], BF16)
        self.xnT_b = Buf()
        self.hT = self.sb("hT", [128, 32, TT], BF16)
        self.hT_b = Buf()
        self.hT_b2 = Buf()
        self.wring = [mk(f"wr{i}", [128, WSLOT], BF16) for i in range(3)]
        self.wrr = 0
        self.wq = 0
        self.gbc = [mk(f"gbc{i}", [128, D], F32) for i in range(2)]
        self.grr = 0
        self.small = [mk(f"small{i}", [128, 64], F32) for i in range(4)]
        self.tmpf = [mk(f"tmpf{i}", [128, 512], F32) for i in range(2)]
        self.ident, self.ident_b = mk("ident_sb", [128, 128], BF16)
        self.p_f = mk("p_f", [128, NSUB, 256], F32)
        self.p_bf = [mk(f"p_bf{i}", [128, 256], BF16) for i in range(2)]
        self.pT = self.sb("pT", [128, 2, TT], BF16)
        self.pT_b = Buf()
        self.ps = [(self.es.enter_context(nc.psum_tensor(f"ps{i}", [128, 512], F32)), Buf()) for i in range(8)]
        self.prr = 0
        self.out_evs = []
        tr.dma("sp", self.ident[:], a["ident"], writes=[self.ident_b])
        self.ident_b.const = True
        if mode == 1:
            self.v_f = self.sb("v_f", [128, NSUB, D], F32)
            self.v_fb = [Buf() for _ in range(NSUB)]
            self.stats = self.sb("stats", [128, NSUB, 24], F32)
            self.stats_b = [Buf() for _ in range(NSUB)]
            self.wsT, self.wsT_b = mk("wsT_sb", [128, 16, 128], BF16)
            self.bsT = self.hT[:, 16:24, :].rearrange("p k t -> p (k t)").bitcast(F32).rearrange("p (g t) -> p g t", g=16)
            self.bsT_b = self.hT_b2
            wsf0, wsfb = self.gbc[0]
            cm0, cmb = self.gbc[1]
            wsf = wsf0[:].rearrange("p (g t) -> p g t", g=16)
            cm = cm0[:, 0:128]
            tr.dma("sp", wsf, a["wsT"], writes=[wsfb])
            tr.dma("sp", cm, a["cmask"], writes=[cmb])
            tr.op("dve", lambda e: e.tensor_tensor(out=self.wsT[:], in0=wsf, in1=cm.unsqueeze(1).to_broadcast([128, 16, 128]),
                                                   op=ALU.mult), reads=[wsfb, cmb], writes=[self.wsT_b])
            self.wsT_b.const = True
            self.cs = mk("cs", [128, NSUB, 32], F32)
            self.obf = [mk(f"obf{i}", [128, 512], BF16) for i in range(2)]
            self.ropet = [mk(f"ropet{i}", [128, 256], F32) for i in range(2)]

        for ti in range(self.ntiles):
            tok0 = ti * TT
            for s in range(NSUB):
                tr.dma("sp", self.xres[:, s, :], a["x"][tok0 + s * 128:tok0 + (s + 1) * 128, :], writes=[self.xres_b[s]])
            if mode == 1:
                import os
                st = os.environ.get("STAGES", "gmlp,ffn,ple").split(",")
                if "gmlp" in st:
                    self.gmlp()
                if "ffn" in st:
                    self.ffn(0)
                if "ple" in st:
                    self.ple(0, tok0)
                for s in range(NSUB):
                    self.out_evs.append(tr.dma("sp", a["x_out"][tok0 + s * 128:tok0 + (s + 1) * 128, :], self.xres[:, s, :],
                                               reads=[self.xres_b[s]]))
                self.nsa_proj(tok0, 0)
            else:
                self.attn_out(tok0)
                self.ffn(1)
                self.ple(1, tok0)
                self.final_norm(tok0)
        tr.finish(self.out_evs)


QT = 256
NCMP = 1023


def nsa_consts():
    kl = np.arange(128)[:, None]
    ql = np.arange(QT)[None, :]
    masks = np.zeros((128, 13, QT), np.float32)
    masks[:, 0] = (kl <= ql)
    masks[:, 1] = (128 + kl <= ql)
    masks[:, 2] = (kl > ql)
    masks[:, 3] = (kl + 128 > ql)
    for r in range(9):
        masks[:, 4 + r] = (16 * kl + 31 <= 256 * r + ql)
    c = np.arange(1024)[:, None]
    s = np.arange(256)[None, :]
    ov = np.clip(np.minimum(c * 16 + 32, s * 64 + 64) - np.maximum(c * 16, s * 64), 0, None) / 16.0
    ov[1023] = 0
    M = ov.reshape(8, 128, 256).transpose(1, 0, 2)
    R = np.zeros((128, 64, 128), np.float32)
    for j in range(64):
        R[2 * j, j, :64] = 1
        R[2 * j + 1, j, 64:] = 1
    F = np.zeros((128, 512), np.float32)
    qq = np.arange(128)[:, None]
    rel = np.arange(512)[None, :] - 256
    cur = (qq >= 64).astype(np.int64)
    F[(rel > cur)] = -1e30
    F[(rel == cur) | (rel == cur - 1)] = 1e9
    pos = (np.arange(1024) * 16 + 31).astype(np.float32)
    inv = np.power(np.float32(500000.0), -np.arange(16, dtype=np.float32) * 2.0 / 32)
    ang = pos[:, None] * inv[None, :]
    crope = np.concatenate([np.cos(ang), np.sin(ang)], axis=1).astype(np.float32).reshape(8, 128, 32).transpose(1, 0, 2)
    return dict(masks=masks.astype(NPBF), Mtab=np.ascontiguousarray(M).astype(NPBF), Rtab=R.astype(NPBF), Fbase=F,
                crope=np.ascontiguousarray(crope), ident=np.eye(128, dtype=np.float32).astype(NPBF))


class NSA:
    def __init__(self, nq=T // QT, Tk=T):
        self.nq = nq
        self.Tk = Tk
        self.nc = bass.Bass("TRN2", target_bir_lowering=False)
        self.es = contextlib.ExitStack()

    def dram_in(self, name, shape, dt):
        return self.nc.dram_tensor(name, list(shape), dt, kind="ExternalInput").ap()

    def sb(self, name, shape, dt):
        return self.es.enter_context(self.nc.sbuf_tensor(name, list(shape), dt))

    def mk(self, name, shape, dt):
        return self.sb(name, shape, dt), Buf()

    def build(self):
        with self.es:
            self.tr = Tracker(self.nc, self.es)
            self._build()
        return self.nc

    @staticmethod
    def _alias(b):
        n = Buf()
        n.writer = b.writer
        n.readers = dict(b.readers)
        return n

    def load_T(self, src, dstT, dstb, nchunks):
        tr = self.tr
        for c0 in range(0, nchunks, 8):
            n = min(8, nchunks - c0)
            st, stb = self.stage[(c0 // 8) % 2]
            tr.dma("sp", st[:, 0:n, :], src[c0 * 128:(c0 + n) * 128, :].rearrange("(c p) d -> p c d", p=128), writes=[stb])
            pt, pb = self.ps[7] if (c0 // 8) % 2 == 0 else self.ps[6]
            pv = pt[:].bitcast(BF16)
            for j in range(n):
                tr.op("pe", lambda e: e.transpose(out=pv[:, j * 128:(j + 1) * 128], in_=st[:, j, :], identity=self.ident[:]),
                      reads=[stb, self.ident_b], writes=[pb])
            eng = "act" if (c0 // 8) % 2 == 0 else "dve"
            o = dstT[:, c0 * 128:(c0 + n) * 128]
            if eng == "act":
                tr.op("act", lambda e: e.copy(out=o, in_=pv[:, 0:n * 128]), reads=[pb], writes=[dstb])
            else:
                tr.op("dve", lambda e: e.tensor_copy(out=o, in_=pv[:, 0:n * 128]), reads=[pb], writes=[dstb])

    def compress(self, src, w1, w2, peT, is_k):
        tr = self.tr
        nch = self.Tk // 128
        ncmp = (self.Tk - 32) // 16 + 1
        R1, R1b = self.R1
        self.load_T(src, R1, R1b, nch)
        w1s, w1b = self.R2
        w1v = w1s[:].rearrange("p (l h) -> p l h", l=32)
        for l0 in range(0, 32, 8):
            tr.dma("pool", w1v[:, l0:l0 + 8, :], w1[l0 * 128:(l0 + 8) * 128, :].rearrange("(l p) h -> p l h", p=128), writes=[w1b])
        w2s, w2b = self.w2s
        tr.dma("pool", w2s[:], w2.rearrange("(c p) d -> p c d", p=128), writes=[w2b])
        pef, pefb = self.pef
        tr.dma("sp", pef[:], peT, writes=[pefb])
        peb, pebb = self.peb
        tr.op("dve", lambda e: e.tensor_copy(out=peb[:], in_=pef[:]), reads=[pefb], writes=[pebb])
        hid, hidb = self.hid
        tr.op("pool", lambda e: e.memset(hid[:], 0.0), writes=[hidb])
        bias, biasb = self.small[0]
        for hc in range(4):
            pt, pb = self.ps[hc % 2]
            for l in range(32):
                tr.op("pe", lambda e: e.matmul(pt[:, 0:1], lhsT=w1v[:, l, hc * 128:(hc + 1) * 128], rhs=peb[:, l:l + 1],
                                               start=(l == 0), stop=(l == 31)), reads=[w1b, pebb], writes=[pb])
            tr.op("dve", lambda e: e.tensor_copy(out=bias[:, hc:hc + 1], in_=pt[:, 0:1]), reads=[pb], writes=[biasb])
        for hc in range(4):
            for cb in range(0, ncmp, 512):
                n = min(512, ncmp - cb)
                pt, pb = self.ps[2 + ((hc * 2 + cb // 512) % 2)]
                for l in range(32):
                    rhs = R1[:, l + 16 * cb:l + 16 * cb + 16 * (n - 1) + 1:16]
                    tr.op("pe", lambda e: e.matmul(pt[:, 0:n], lhsT=w1v[:, l, hc * 128:(hc + 1) * 128], rhs=rhs,
                                                   start=(l == 0), stop=(l == 31)), reads=[w1b, R1b], writes=[pb])
                tr.op("act", lambda e: e.activation(out=hid[:, hc, cb:cb + n], in_=pt[:, 0:n], func=AF.Gelu_apprx_tanh,
                                                    bias=bias[:, hc:hc + 1]), reads=[pb, biasb], writes=[hidb])
        ncc = (ncmp + 127) // 128
        for j in range(ncc):
            pt, pb = self.ps[4 + j % 2]
            for hc in range(4):
                tr.op("pe", lambda e: e.matmul(pt[:, 0:128], lhsT=hid[:, hc, j * 128:(j + 1) * 128], rhs=w2s[:, hc, :],
                                               start=(hc == 0), stop=(hc == 3)), reads=[hidb, w2b], writes=[pb])
            if not is_k:
                tr.op("act", lambda e: e.copy(out=self.vcmp1[:, j, 0:128], in_=pt[:, 0:128]), reads=[pb], writes=[self.vcmp1_b])
            else:
                tmp, tb = self.tmpc[j % 2]
                tr.op("act", lambda e: e.copy(out=tmp[:], in_=pt[:, 0:128]), reads=[pb], writes=[tb])
                x1, x2 = tmp[:, 0:16], tmp[:, 16:32]
                cos, sin = self.crope[:, j, 0:16], self.crope[:, j, 16:32]
                r, rb = self.small[1]
                tr.op("dve", lambda e: e.tensor_tensor(out=r[:, 0:16], in0=x1, in1=cos, op=ALU.mult), reads=[tb, self.crope_b], writes=[rb])
                tr.op("dve", lambda e: e.tensor_tensor(out=r[:, 16:32], in0=x2, in1=sin, op=ALU.mult), reads=[tb, self.crope_b], writes=[rb])
                tr.op("dve", lambda e: e.tensor_tensor(out=r[:, 32:48], in0=x2, in1=cos, op=ALU.mult), reads=[tb, self.crope_b], writes=[rb])
                tr.op("dve", lambda e: e.tensor_tensor(out=r[:, 48:64], in0=x1, in1=sin, op=ALU.mult), reads=[tb, self.crope_b], writes=[rb])
                tr.op("dve", lambda e: e.tensor_tensor(out=x1, in0=r[:, 0:16], in1=r[:, 16:32], op=ALU.subtract), reads=[rb, tb], writes=[tb])
                tr.op("dve", lambda e: e.tensor_tensor(out=x2, in0=r[:, 32:48], in1=r[:, 48:64], op=ALU.add), reads=[rb, tb], writes=[tb])
                tb16, tb16b = self.tmpc16[j % 2]
                tr.op("dve", lambda e: e.tensor_copy(out=tb16[:], in_=tmp[:]), reads=[tb], writes=[tb16b])
                p2, p2b = self.ps[6 + j % 2]
                pv = p2[:].bitcast(BF16)
                tr.op("pe", lambda e: e.transpose(out=pv[:, 0:128], in_=tb16[:], identity=self.ident[:]),
                      reads=[tb16b, self.ident_b], writes=[p2b])
                tr.op("act", lambda e: e.copy(out=self.kcmpT[:, j * 128:(j + 1) * 128], in_=pv[:, 0:128]), reads=[p2b],
                      writes=[self.kcmpT_b])

    def run_phase(self, batches, accs):
        tr = self.tr
        fib = set()
        prev = None
        for b in list(batches) + [None]:
            cur = None
            if b is not None:
                bi = self.bcount
                self.bcount += 1
                S, Sb = self.sbig[bi % self.nsbuf]
                P, Pb = self.pbig[bi % 2]
                nu = len(b["units"])
                if "pre" in b:
                    b["pre"]()
                for u, (kT, kb, h) in enumerate(b["units"]):
                    neg = b.get("neg")
                    tr.op("pe", lambda e: e.matmul(S[:, u * QT:(u + 1) * QT], lhsT=kT, rhs=self.QTt[:, h, :], start=True, stop=(neg is None)),
                          reads=list(kb) + [self.QT_b], writes=[Sb])
                    if neg is not None:
                        tr.op("pe", lambda e: e.matmul(S[:, u * QT:(u + 1) * QT], lhsT=neg[0], rhs=neg[1], start=False, stop=True),
                              reads=list(neg[2]), writes=[Sb])
                n = nu * QT
                if "post" in b:
                    b["post"]()
                tr.op("act", lambda e: e.activation(out=P[:, 0:n], in_=S[:, 0:n], func=AF.Exp), reads=[Sb], writes=[Pb])
                pv3 = P[:, 0:n].rearrange("p (u q) -> p u q", u=nu)
                for m, mb in b["masks"]:
                    tr.op("dve", lambda e: e.tensor_tensor(out=pv3, in0=pv3, in1=m.unsqueeze(1).to_broadcast([128, nu, QT]), op=ALU.mult),
                          reads=[Pb] + list(mb), writes=[Pb])
                cur = (P, Pb, b)
            if prev is not None:
                P, Pb, pb_ = prev
                v = pb_["v"]
                ncol = v.shape[-1]
                for u, (kT, kb, h) in enumerate(pb_["units"]):
                    for sub in range(2):
                        acc, accb, bank = accs[h][sub]
                        st_flag = bank not in fib
                        fib.add(bank)
                        tr.op("pe", lambda e: e.matmul(acc[:, 0:ncol], lhsT=P[:, u * QT + sub * 128:u * QT + (sub + 1) * 128], rhs=v,
                                                       start=st_flag, stop=True, skip_group_check=True),
                              reads=[Pb] + list(pb_["vb"]), writes=[accb])
            prev = cur

    def _build(self):
        import os
        nc, tr = self.nc, self.tr
        Tk = self.Tk
        nch = Tk // 128
        a = {}
        a["q"] = self.dram_in("q", [Tk, 512], BF16)
        for nm in ("kc", "vc", "ks", "vs", "kw", "vw"):
            a[nm] = self.dram_in(nm, [Tk, 128], BF16)
        a["gates"] = self.dram_in("gates", [Tk, 12], F32)
        for nm in ("kc_w1", "vc_w1"):
            a[nm] = self.dram_in(nm, [4096, 512], F32)
        for nm in ("kc_w2", "vc_w2"):
            a[nm] = self.dram_in(nm, [512, 128], F32)
        for nm in ("kc_peT", "vc_peT"):
            a[nm] = self.dram_in(nm, [128, 32], F32)
        a["masks"] = self.dram_in("masks", [128, 13, QT], BF16)
        a["Mtab"] = self.dram_in("Mtab", [128, 8, 256], BF16)
        a["Rtab"] = self.dram_in("Rtab", [128, 64, 128], BF16)
        a["Fbase"] = self.dram_in("Fbase", [128, 512], F32)
        a["crope"] = self.dram_in("crope", [128, 8, 32], F32)
        a["ident"] = self.dram_in("ident", [128, 128], BF16)
        o_out = self.nc.dram_tensor("o_out", [self.nq * QT, 512], BF16, kind="ExternalOutput").ap()
        mk = self.mk
        self.R1 = mk("R1", [128, Tk], BF16)
        self.R2 = mk("R2", [128, max(Tk, 16384)], BF16)
        self.vs1, self.vs1_b = mk("vs1", [128, nch, 129], BF16)
        self.vw1, self.vw1_b = mk("vw1", [128, nch, 129], BF16)
        self.hid = mk("hid", [128, 4, 1024], BF16)
        self.w2s = mk("w2s", [128, 4, 128], BF16)
        self.pef = mk("pef", [128, 32], F32)
        self.peb = mk("peb", [128, 32], BF16)
        self.stage = [mk(f"stage{i}", [128, 8, 128], BF16) for i in range(2)]
        self.small = [mk(f"small{i}", [128, 64], F32) for i in range(4)]
        self.tmpc = [mk(f"tmpc{i}", [128, 128], F32) for i in range(2)]
        self.tmpc16 = [mk(f"tmpc16{i}", [128, 128], BF16) for i in range(2)]
        self.kcmpT, self.kcmpT_b = mk("kcmpT", [128, 1024], BF16)
        self.vcmp1, self.vcmp1_b = mk("vcmp1", [128, 8, 385], BF16)
        self.ident, self.ident_b = mk("ident_sb", [128, 128], BF16)
        self.masks, self.masks_b = mk("masks_sb", [128, 13, QT], BF16)
        self.Rtab, self.Rtab_b = mk("Rtab_sb", [128, 64, 128], BF16)
        self.Fbase, self.Fbase_b = mk("Fbase_sb", [128, 512], F32)
        self.crope, self.crope_b = mk("crope_sb", [128, 8, 32], F32)
        big0 = self.es.enter_context(nc.psum_tensor("psbig0", [128, 1024], F32))
        big1 = self.es.enter_context(nc.psum_tensor("psbig1", [128, 1024], F32))
        b0, b1 = Buf(), Buf()
        self.ps = [(big0[:, 0:512], b0), (big0[:, 512:1024], b0), (big1[:, 0:512], b1), (big1[:, 512:1024], b1)]
        for i in range(4, 8):
            self.ps.append((self.es.enter_context(nc.psum_tensor(f"ps{i}", [128, 512], F32))[:], Buf()))
        self.sbig = [(big0[:], b0), (big1[:], b1)]
        self.nsbuf = 2
        self.bcount = 0
        for t_, b_, src in ((self.ident, self.ident_b, a["ident"]), (self.masks, self.masks_b, a["masks"]),
                            (self.Rtab, self.Rtab_b, a["Rtab"]), (self.Fbase, self.Fbase_b, a["Fbase"]),
                            (self.crope, self.crope_b, a["crope"])):
            tr.dma("sp", t_[:], src, writes=[b_])
            b_.const = True
        for _ in range(int(os.environ.get("ACT_DUMMY", "0"))):
            tr.op("act", lambda e: e.copy(out=self.small[3][0][:, 0:8], in_=self.small[3][0][:, 8:16]), writes=[self.small[3][1]])
        for _ in range(int(os.environ.get("DVE_DUMMY", "0"))):
            tr.op("dve", lambda e: e.tensor_copy(out=self.small[3][0][:, 16:24], in_=self.small[3][0][:, 24:32]), writes=[self.small[3][1]])
        tr.op("pool", lambda e: e.memset(self.kcmpT[:], 0.0), writes=[self.kcmpT_b])
        tr.op("pool", lambda e: e.memset(self.vcmp1[:], 0.0), writes=[self.vcmp1_b])
        self.compress(a["kc"], a["kc_w1"], a["kc_w2"], a["kc_peT"], True)
        self.compress(a["vc"], a["vc_w1"], a["vc_w2"], a["vc_peT"], False)
        tr.op("pool", lambda e: e.memset(self.vcmp1[:, :, 128:129], 1.0), reads=[], writes=[self.vcmp1_b])
        tr.dma("sp", self.vcmp1[:, :, 129:385], a["Mtab"], writes=[self.vcmp1_b])
        STOP = os.environ.get("NSA_STOP", "")
        if STOP == "compress":
            tr.finish([]); return
        ksT, ksT_b = self.R1
        kwT, kwT_b = self.R2
        self.load_T(a["ks"], ksT, ksT_b, nch)
        self.load_T(a["kw"], kwT, kwT_b, nch)
        tr.op("pool", lambda e: e.memset(self.vs1[:, :, 128:129], 1.0), writes=[self.vs1_b])
        tr.op("pool", lambda e: e.memset(self.vw1[:, :, 128:129], 1.0), writes=[self.vw1_b])
        for c0 in range(0, nch, 8):
            n = min(8, nch - c0)
            tr.dma("sp", self.vs1[:, c0:c0 + n, 0:128], a["vs"][c0 * 128:(c0 + n) * 128, :].rearrange("(c p) d -> p c d", p=128),
                   writes=[self.vs1_b])
            tr.dma("sp", self.vw1[:, c0:c0 + n, 0:128], a["vw"][c0 * 128:(c0 + n) * 128, :].rearrange("(c p) d -> p c d", p=128),
                   writes=[self.vw1_b])
        if STOP == "kv":
            tr.finish([]); return
        q_tm = [mk(f"q_tm{i}", [128, 2, 512], BF16) for i in range(2)]
        self.QTt, self.QT_b = mk("QTt", [128, 4, QT], BF16)
        self.pbig = [mk(f"pbig{i}", [128, 4 * QT], BF16) for i in range(2)]
        o_acc, o_accb = mk("o_acc", [128, 2, 512], F32)
        o_bf = [mk(f"o_bf{i}", [128, 2, 512], BF16) for i in range(2)]
        imp = [mk(f"imp{i}", [128, 256], F32) for i in range(2)]
        score = [mk(f"score{i}", [128, 256], F32) for i in range(2)]
        selb = [mk(f"selb{i}", [128, 256], BF16) for i in range(2)]
        selT, selT_b = mk("selT", [128, 2, QT], BF16)
        gts = [mk(f"gts{i}", [128, 2, 12], F32) for i in range(2)]
        mslot = []
        for i in range(2):
            pt, pbk = self.ps[7]
            mslot.append((pt[:, i * 256:(i + 1) * 256], self._alias(pbk)))
        pb7s = [self.ps[7][1], mslot[0][1], mslot[1][1]]
        msk = [mk(f"msk{i}", [128, QT], BF16) for i in range(2)]
        acc8 = [[None, None] for _ in range(4)]
        for h in range(4):
            for sub in range(2):
                idx = h * 2 + sub
                pt, pbk = self.ps[4 + idx // 3]
                acc8[h][sub] = (pt[:, (idx % 3) * 129:(idx % 3) * 129 + 129], pbk, 4 + idx // 3)
        out_evs = []
        I0 = int(os.environ.get("NSA_I0", "0"))
        for i in range(I0, self.nq):
            t0 = i * QT
            qt_, qtb = q_tm[i % 2]
            tr.dma("sp", qt_[:], a["q"][t0:t0 + QT, :].rearrange("(s p) c -> p s c", p=128), writes=[qtb])
            g, gb = gts[i % 2]
            tr.dma("sp", g[:], a["gates"][t0:t0 + QT, :].rearrange("(s p) c -> p s c", p=128), writes=[gb])
            pt, pb = self.ps[7]
            pv = pt[:].bitcast(BF16)
            for sub in range(2):
                for h in range(4):
                    tr.op("pe", lambda e: e.transpose(out=pv[:, (sub * 4 + h) * 128:(sub * 4 + h + 1) * 128],
                                                      in_=qt_[:, sub, h * 128:(h + 1) * 128], identity=self.ident[:]),
                          reads=[qtb, self.ident_b], writes=pb7s)
            for sub in range(2):
                tr.op("act" if sub == 0 else "dve",
                      (lambda e: e.copy(out=self.QTt[:, :, sub * 128:(sub + 1) * 128],
                                        in_=pv[:, sub * 512:(sub + 1) * 512].rearrange("p (h q) -> p h q", h=4))) if sub == 0 else
                      (lambda e: e.tensor_copy(out=self.QTt[:, :, sub * 128:(sub + 1) * 128],
                                               in_=pv[:, sub * 512:(sub + 1) * 512].rearrange("p (h q) -> p h q", h=4))),
                      reads=pb7s, writes=[self.QT_b])
            jmax = min((16 * i + 14) // 128, 7)
            for hp in range(2):
                accs = {}
                cbanks = [3, 4, 5, 6]
                for hh in range(2):
                    h = hp * 2 + hh
                    accs[h] = []
                    for sub in range(2):
                        bk = cbanks[hh * 2 + sub]
                        pt_, pb_ = self.ps[bk]
                        accs[h].append((pt_, pb_, bk))
                batches = []
                for j in range(jmax + 1):
                    r = i - 8 * j
                    masks = [(self.masks[:, 4 + r, :], [self.masks_b])] if r <= 8 else []
                    batches.append(dict(units=[(self.kcmpT[:, j * 128:(j + 1) * 128], [self.kcmpT_b], hp * 2 + hh) for hh in range(2)],
                                        v=self.vcmp1[:, j, :], vb=[self.vcmp1_b], masks=masks))
                self.nsbuf = 1
                self.run_phase(batches, accs)
                self.nsbuf = 2
                for hh in range(2):
                    h = hp * 2 + hh
                    for sub in range(2):
                        acc, accb, _ = accs[h][sub]
                        sm, smb = self.small[sub]
                        c0 = h * 4
                        tr.op("dve", lambda e: e.tensor_scalar(out=sm[:, c0:c0 + 1], in0=acc[:, 128:129], scalar1=1e-30, scalar2=1.0,
                                                               op0=ALU.max, op1=ALU.mult), reads=[accb], writes=[smb])
                        tr.op("dve", lambda e: e.reciprocal(out=sm[:, c0 + 1:c0 + 2], in_=sm[:, c0:c0 + 1]), reads=[smb], writes=[smb])
                        tr.op("dve", lambda e: e.tensor_tensor(out=sm[:, c0 + 2:c0 + 3], in0=sm[:, c0 + 1:c0 + 2], in1=g[:, sub, h:h + 1],
                                                               op=ALU.mult), reads=[smb, gb], writes=[smb])
                        tr.op("dve", lambda e: e.tensor_scalar(out=o_acc[:, sub, h * 128:(h + 1) * 128], in0=acc[:, 0:128],
                                                               scalar1=sm[:, c0 + 2:c0 + 3], scalar2=1.0, op0=ALU.mult, op1=ALU.mult),
                              reads=[accb, smb], writes=[o_accb])
                        im, imb = imp[sub]
                        if h == 0:
                            tr.op("dve", lambda e: e.tensor_scalar(out=im[:], in0=acc[:, 129:385], scalar1=sm[:, c0 + 1:c0 + 2], scalar2=1.0,
                                                                   op0=ALU.mult, op1=ALU.mult), reads=[accb, smb], writes=[imb])
                        else:
                            tr.op("dve", lambda e: e.scalar_tensor_tensor(out=im[:], in0=acc[:, 129:385], scalar=sm[:, c0 + 1:c0 + 2],
                                                                          in1=im[:], op0=ALU.mult, op1=ALU.add),
                                  reads=[accb, smb, imb], writes=[imb])
            if STOP == "cmp":
                continue
            for sub in range(2):
                qt128 = 2 * i + sub
                im, imb = imp[sub]
                sc, scb = score[sub]
                off = 256 - 2 * qt128
                tr.op("dve", lambda e: e.tensor_tensor(out=sc[:], in0=im[:], in1=self.Fbase[:, off:off + 256], op=ALU.add),
                      reads=[imb, self.Fbase_b], writes=[scb])
                tr.op("dve", lambda e: e.memset(sc[:, 0:1], 1e9), writes=[scb], reads=[scb])
                m8, m8b = self.small[2 + sub]
                tr.op("dve", lambda e: e.max(out=m8[:, 0:8], in_=sc[:]), reads=[scb], writes=[m8b])
                tr.op("dve", lambda e: e.match_replace(out=im[:], in_to_replace=m8[:, 0:8], in_values=sc[:], imm_value=-3e38),
                      reads=[scb, m8b], writes=[imb])
                tr.op("dve", lambda e: e.max(out=m8[:, 8:16], in_=im[:]), reads=[imb], writes=[m8b])
                sb_, sbb = selb[sub]
                tr.op("dve", lambda e: e.tensor_scalar(out=sb_[:], in0=sc[:], scalar1=m8[:, 15:16], scalar2=1.0, op0=ALU.is_ge, op1=ALU.mult),
                      reads=[scb, m8b], writes=[sbb])
                pt, pb = self.ps[7]
                pv = pt[:].bitcast(BF16)
                for sc_ in range(2):
                    tr.op("pe", lambda e: e.transpose(out=pv[:, sc_ * 128:(sc_ + 1) * 128], in_=sb_[:, sc_ * 128:(sc_ + 1) * 128],
                                                      identity=self.ident[:]), reads=[sbb, self.ident_b], writes=pb7s)
                tr.op("dve", lambda e: e.tensor_scalar(out=selT[:, :, sub * 128:(sub + 1) * 128],
                                                       in0=pv[:, 0:256].rearrange("p (c q) -> p c q", c=2),
                                                       scalar1=30000.0, scalar2=-30000.0, op0=ALU.mult, op1=ALU.add),
                      reads=pb7s, writes=[selT_b])
            if STOP == "topk":
                continue
            accs = {h: acc8[h] for h in range(4)}
            batches = []
            for kc in range(0, 2 * i + 2):
                r = kc - 2 * i
                masks = [(self.masks[:, r, :], [self.masks_b])] if r >= 0 else []
                batches.append(dict(units=[(ksT[:, kc * 128:(kc + 1) * 128], [ksT_b], h) for h in range(4)],
                                    v=self.vs1[:, kc, :], vb=[self.vs1_b], masks=masks,
                                    neg=(self.Rtab[:, kc % 64, :], selT[:, kc // 64, :], [self.Rtab_b, selT_b])))
            self.run_phase(batches, accs)
            self.evac_branch(acc8, g, gb, 1, o_acc, o_accb)
            if STOP == "sel":
                continue
            batches = []
            for kc in range(max(0, 2 * i - 4), 2 * i + 2):
                r = kc - 2 * i
                mi = {-4: 2, -3: 3, 0: 0, 1: 1}.get(r)
                masks = [(self.masks[:, mi, :], [self.masks_b])] if mi is not None else []
                batches.append(dict(units=[(kwT[:, kc * 128:(kc + 1) * 128], [kwT_b], h) for h in range(4)],
                                    v=self.vw1[:, kc, :], vb=[self.vw1_b], masks=masks))
            self.run_phase(batches, accs)
            if STOP == "win":
                continue
            self.evac_branch(acc8, g, gb, 2, o_acc, o_accb)
            if STOP == "winevac":
                continue
            ob, obb = o_bf[i % 2]
            tr.op("dve", lambda e: e.tensor_copy(out=ob[:], in_=o_acc[:]), reads=[o_accb], writes=[obb])
            for sub in range(2):
                out_evs.append(tr.dma(os.environ.get("NSA_OUTQ", "sp"), o_out[t0 + sub * 128:t0 + (sub + 1) * 128, :], ob[:, sub, :], reads=[obb]))
        tr.finish(out_evs)

    def evac_branch(self, acc8, g, gb, br, o_acc, o_accb):
        tr = self.tr
        for h in range(4):
            for sub in range(2):
                acc, accb, _ = acc8[h][sub]
                sm, smb = self.small[sub]
                c0 = h * 4
                tr.op("dve", lambda e: e.tensor_scalar(out=sm[:, c0:c0 + 1], in0=acc[:, 128:129], scalar1=1e-30, scalar2=1.0,
                                                       op0=ALU.max, op1=ALU.mult), reads=[accb], writes=[smb])
                tr.op("dve", lambda e: e.reciprocal(out=sm[:, c0 + 1:c0 + 2], in_=sm[:, c0:c0 + 1]), reads=[smb], writes=[smb])
                tr.op("dve", lambda e: e.tensor_tensor(out=sm[:, c0 + 2:c0 + 3], in0=sm[:, c0 + 1:c0 + 2],
                                                       in1=g[:, sub, br * 4 + h:br * 4 + h + 1], op=ALU.mult), reads=[smb, gb], writes=[smb])
                o = o_acc[:, sub, h * 128:(h + 1) * 128]
                tr.op("dve", lambda e: e.scalar_tensor_tensor(out=o, in0=acc[:, 0:128], scalar=sm[:, c0 + 2:c0 + 3], in1=o,
                                                              op0=ALU.mult, op1=ALU.add), reads=[accb, smb, o_accb], writes=[o_accb])


_CACHE = {}


def _prog(key, fn):
    if key not in _CACHE:
        _CACHE[key] = fn()
    return _CACHE[key]


def kernel(x, p, norm_mix, norm_ffn, norm_ple, ffn_up, ffn_down, ple_proj, ple_gate,
           gm_in, gm_ln_g, gm_ln_b, gm_ws, gm_bs, gm_out,
           nsa_in, nsa_kc_pe, nsa_kc_w1, nsa_kc_w2, nsa_vc_pe, nsa_vc_w1, nsa_vc_w2, nsa_out, final_norm):
    f32 = np.float32
    A = lambda v: np.ascontiguousarray(np.asarray(v, dtype=f32))
    x = A(x).reshape(B * T, D)
    p = A(p).reshape(2, B * T, 256)
    norm_mix, norm_ffn, norm_ple = A(norm_mix), A(norm_ffn), A(norm_ple)
    ident = np.eye(128, dtype=f32).astype(NPBF)
    NTOK = B * T // NCORES
    pos = np.arange(T, dtype=f32)
    inv = np.power(f32(500000.0), -np.arange(16, dtype=f32) * f32(2.0) / f32(32))
    ang = pos[:, None] * inv[None, :]
    rope = np.concatenate([np.cos(ang), np.sin(ang)], axis=1).astype(f32)
    rope = np.concatenate([rope, rope], axis=0)
    cores = list(range(NCORES))
    nc1 = Dense(1).build()
    common1 = dict(ident=ident, norm_mix=norm_mix, norm_ffn=norm_ffn, norm_ple=norm_ple,
                   ffn_up1=A(ffn_up[0]), ffn_down1=A(ffn_down[0]), ple_proj1=A(ple_proj[0]), ple_gate1=A(ple_gate[0]),
                   gm_in=A(gm_in[0]), gm_out=A(gm_out[0]), gm_ln_g=A(gm_ln_g).reshape(1, D), gm_ln_b=A(gm_ln_b).reshape(1, D),
                   wsT=np.ascontiguousarray(A(gm_ws[0]).transpose(2, 0, 1)), cmask=np.triu(np.ones((128, 128), f32)),
                   bsT=A(gm_bs[0]).reshape(1, 2048), nsa_in=A(nsa_in[0]))
    maps = []
    for c in cores:
        sl = slice(c * NTOK, (c + 1) * NTOK)
        maps.append(dict(common1, x=x[sl], p=p[0, sl], rope=rope[sl]))
    r1 = run_bass_kernel_spmd(nc1, maps, core_ids=cores).results
    x1 = np.concatenate([r["x_out"] for r in r1], axis=0)
    qkv = np.concatenate([r["qkv_out"] for r in r1], axis=0)
    gates = np.concatenate([r["gates_out"] for r in r1], axis=0)
    del r1, maps
    nc2 = NSA().build()
    C = nsa_consts()
    common2 = dict(C, kc_w1=A(nsa_kc_w1[0]), vc_w1=A(nsa_vc_w1[0]), kc_w2=A(nsa_kc_w2[0]), vc_w2=A(nsa_vc_w2[0]),
                   kc_peT=np.ascontiguousarray(A(nsa_kc_pe[0]).T), vc_peT=np.ascontiguousarray(A(nsa_vc_pe[0]).T))
    maps = []
    for c in cores:
        b, g = c // 4, c % 4
        rows = slice(b * T, (b + 1) * T)
        m = dict(common2)
        m["q"] = np.ascontiguousarray(qkv[rows, g * 512:(g + 1) * 512])
        for i, nm in enumerate(("kc", "vc", "ks", "vs", "kw", "vw")):
            m[nm] = np.ascontiguousarray(qkv[rows, 2048 + i * 512 + g * 128:2048 + i * 512 + (g + 1) * 128])
        m["gates"] = np.ascontiguousarray(np.concatenate([gates[rows, br * 16 + g * 4:br * 16 + g * 4 + 4] for br in range(3)], axis=1))
        maps.append(m)
    r2 = run_bass_kernel_spmd(nc2, maps, core_ids=cores).results
    o = np.empty((B * T, D), dtype=NPBF)
    for c in cores:
        b, g = c // 4, c % 4
        o[b * T:(b + 1) * T, g * 512:(g + 1) * 512] = r2[c]["o_out"]
    del r2, maps, qkv
    nc3 = Dense(3).build()
    common3 = dict(ident=ident, norm_mix=norm_mix, norm_ffn=norm_ffn, norm_ple=norm_ple,
                   ffn_up3=A(ffn_up[1]), ffn_down3=A(ffn_down[1]), ple_proj3=A(ple_proj[1]), ple_gate3=A(ple_gate[1]),
                   nsa_out=A(nsa_out[0]), final_norm=A(final_norm).reshape(1, D))
    maps = []
    for c in cores:
        sl = slice(c * NTOK, (c + 1) * NTOK)
        maps.append(dict(common3, x=x1[sl], p=p[1, sl], o_in=np.ascontiguousarray(o[sl])))
    r3 = run_bass_kernel_spmd(nc3, maps, core_ids=cores).results
    out = np.concatenate([r["x_out"] for r in r3], axis=0).reshape(B, T, D).astype(f32)
    return out
```

```python
import bisect
import contextlib
import numpy as np
import ml_dtypes
import concourse.bass as bass
import concourse.mybir as mybir
from concourse.bass_utils import run_bass_kernel_spmd

F32 = mybir.dt.float32
BF16 = mybir.dt.bfloat16
AF = mybir.ActivationFunctionType
ALU = mybir.AluOpType
NPBF = ml_dtypes.bfloat16

D = 2048
DFF = 8192
T = 16384
B = 2
NCORES = 8
EPS = 1e-6
SEM_LIMIT = 1000


class Buf:
    __slots__ = ("writer", "readers", "const")

    def __init__(self):
        self.writer = None
        self.readers = {}
        self.const = False


class Tracker:
    def __init__(self, nc, es):
        self.nc = nc
        self.es = es
        self.engs = {"pe": nc.tensor, "act": nc.scalar, "dve": nc.vector, "pool": nc.gpsimd, "sp": nc.sync}
        self.sems = {k: [] for k in self.engs}
        self.seq = {k: 0 for k in self.engs}
        self.last = {k: None for k in self.engs}
        self.sig_seqs = {k: [] for k in self.engs}
        self.known = {}
        self.dma_sems = {}
        self.dma_uses = {}
        self.dma_rr = {}
        for q, n in (("sp", 20), ("pool", 12), ("act", 6)):
            self.dma_sems[q] = [es.enter_context(nc.semaphore(f"dq_{q}_{i}")) for i in range(n)]
            self.dma_uses[q] = [0] * n
            self.dma_rr[q] = 0
        self.nwaits = 0
        import os
        self.dummy = [es.enter_context(nc.semaphore(f"dummy{i}")) for i in range(int(os.environ.get("DUMMY_SEMS", "0")))]

    def _eng_sem(self, e, epoch):
        while len(self.sems[e]) <= epoch:
            self.sems[e].append(self.es.enter_context(self.nc.semaphore(f"es_{e}_{len(self.sems[e])}")))
        return self.sems[e][epoch]

    def _wait(self, waiter, ev):
        if ev is None:
            return
        if ev[0] == "dma":
            _, q, si, val = ev
            key = (waiter, "dma", q, si)
            if self.known.get(key, 0) >= val:
                return
            self.engs[waiter].wait_ge(self.dma_sems[q][si], val)
            self.known[key] = val
            self.nwaits += 1
            return
        e, seq = ev
        sigs = self.sig_seqs[e]
        i = bisect.bisect_left(sigs, seq)
        if i == len(sigs):
            lseq, lins = self.last[e]
            assert lseq >= seq
            k = len(sigs)
            lins.then_inc(self._eng_sem(e, k // SEM_LIMIT), 1)
            sigs.append(lseq)
        k = i
        epoch, val = k // SEM_LIMIT, k % SEM_LIMIT + 1
        key = (waiter, e, epoch)
        if self.known.get(key, 0) >= val:
            return
        self.engs[waiter].wait_ge(self._eng_sem(e, epoch), val)
        self.known[key] = val
        for ep in range(epoch):
            self.known[(waiter, e, ep)] = SEM_LIMIT
        self.nwaits += 1

    def _deps(self, eng, reads, writes):
        deps = {}

        def add(ev, is_write_dep):
            if ev is None:
                return
            if ev[0] == "dma":
                deps[ev] = ev
            else:
                e, seq = ev
                if e == "pe" and eng == "pe" and is_write_dep:
                    return
                if deps.get(e, (e, 0))[1] < seq:
                    deps[e] = ev

        for b in reads:
            add(b.writer, False)
        for b in writes:
            add(b.writer, True)
            for r in b.readers.values():
                if not (r[0] == eng and eng == "pe" and False):
                    add(r, False)
        return list(deps.values())

    def _record(self, ev, reads, writes):
        for b in reads:
            if b.const:
                continue
            if ev[0] == "dma":
                b.readers[ev] = ev
            else:
                b.readers[ev[0]] = ev
        for b in writes:
            b.writer = ev
            b.readers = {}

    def op(self, eng, fn, reads=(), writes=()):
        for d in self._deps(eng, reads, writes):
            if d[0] == eng and eng == "pe":
                continue
            if False and d[0] == eng and self.seq[eng] - d[1] >= 3:
                continue
            self._wait(eng, d)
        ins = fn(self.engs[eng])
        self.seq[eng] += 1
        ev = (eng, self.seq[eng])
        self.last[eng] = (self.seq[eng], ins)
        self._record(ev, reads, writes)
        return ins

    def dma(self, q, out, in_, reads=(), writes=(), **kw):
        for d in self._deps(q, reads, writes):
            self._wait(q, d)
        n = len(self.dma_sems[q])
        si = self.dma_rr[q]
        self.dma_rr[q] = (si + 1) % n
        uses = self.dma_uses[q][si]
        if uses > 0:
            self._wait(q, ("dma", q, si, 16 * uses))
        ins = self.engs[q].dma_start(out=out, in_=in_, **kw)
        ins.then_inc(self.dma_sems[q][si], 16)
        self.dma_uses[q][si] = uses + 1
        ev = ("dma", q, si, 16 * (uses + 1))
        self._record(ev, reads, writes)
        return ev

    def finish(self, out_events):
        for ev in out_events:
            self._wait("sp", ev)


TT = 512
NSUB = 4
NTILE = 4096 // TT
WSLOT = 8192


class Dense:
    def __init__(self, mode, ntiles=NTILE):
        self.mode = mode
        self.ntiles = ntiles
        nc = self.nc = bass.Bass("TRN2", target_bir_lowering=False)
        self.es = contextlib.ExitStack()

    def dram_in(self, name, shape, dt):
        return self.nc.dram_tensor(name, list(shape), dt, kind="ExternalInput").ap()

    def dram_out(self, name, shape, dt):
        return self.nc.dram_tensor(name, list(shape), dt, kind="ExternalOutput").ap()

    def sb(self, name, shape, dt):
        return self.es.enter_context(self.nc.sbuf_tensor(name, list(shape), dt))

    def build(self):
        nc, es = self.nc, self.es
        with es:
            self.tr = Tracker(nc, es)
            self._build()
        return nc

    def wpiece(self, wap, r0, nk, c0, ncols):
        assert nk * ncols <= WSLOT
        i = self.wrr
        self.wrr = (i + 1) % len(self.wring)
        t, b = self.wring[i]
        dst = t[:, 0:nk * ncols].rearrange("p (k n) -> p k n", k=nk)
        src = wap[r0 * 128:(r0 + nk) * 128, c0:c0 + ncols].rearrange("(k p) n -> p k n", p=128)
        q = "pool"
        self.wq += 1
        self.tr.dma(q, dst, src, writes=[b])
        return dst, b

    def bcast_load(self, vec_ap):
        i = self.grr
        self.grr = (i + 1) % len(self.gbc)
        t, b = self.gbc[i]
        self.tr.dma("sp", t[:], vec_ap.partition_broadcast(128), writes=[b])
        return t, b

    def next_ps(self):
        i = self.prr
        self.prr = (i + 1) % 8
        return self.ps[i]

    def rmsnorm_T(self, gvec_ap):
        tr = self.tr
        g_t, g_b = self.bcast_load(gvec_ap)
        for s in range(NSUB):
            xs = self.xres[:, s, :]
            xb = self.xres_b[s]
            junk, jb = self.xn_tm[s % 2]
            ss, ssb = self.small[s % 2]
            tr.op("act", lambda e: e.activation(out=junk[:], in_=xs, func=AF.Square, accum_out=ss[:, 0:1]),
                  reads=[xb], writes=[jb, ssb])
            tr.op("dve", lambda e: e.tensor_scalar(out=ss[:, 1:2], in0=ss[:, 0:1], scalar1=1.0 / D, scalar2=EPS,
                                                   op0=ALU.mult, op1=ALU.add), reads=[ssb], writes=[ssb])
            tr.op("act", lambda e: e.activation(out=ss[:, 2:3], in_=ss[:, 1:2], func=AF.Sqrt), reads=[ssb], writes=[ssb])
            tr.op("dve", lambda e: e.reciprocal(out=ss[:, 3:4], in_=ss[:, 2:3]), reads=[ssb], writes=[ssb])
            tr.op("dve", lambda e: e.scalar_tensor_tensor(out=junk[:], in0=xs, scalar=ss[:, 3:4], in1=g_t[:],
                                                          op0=ALU.mult, op1=ALU.mult),
                  reads=[xb, ssb, g_b], writes=[jb])
            self.transpose_into(junk, jb, 16, self.xnT, self.xnT_b, s)

    def transpose_into(self, src, srcb, nk, dstT, dstb, s):
        tr = self.tr
        for k0 in range(0, nk, 8):
            n = min(8, nk - k0)
            pt, pb = self.next_ps()
            pv = pt[:].bitcast(BF16)
            for j in range(n):
                k = k0 + j
                tr.op("pe", lambda e: e.transpose(out=pv[:, j * 128:(j + 1) * 128], in_=src[:, k * 128:(k + 1) * 128],
                                                  identity=self.ident[:]),
                      reads=[srcb, self.ident_b], writes=[pb])
            eng = "act" if (k0 // 8) % 2 == 0 else "dve"
            o = dstT[:, k0:k0 + n, s * 128:(s + 1) * 128]
            i = pv[:, 0:n * 128].rearrange("p (k t) -> p k t", k=n)
            if eng == "act":
                tr.op("act", lambda e: e.copy(out=o, in_=i), reads=[pb], writes=[dstb])
            else:
                tr.op("dve", lambda e: e.tensor_copy(out=o, in_=i), reads=[pb], writes=[dstb])

    def proj_fm(self, wap, r0, nk, c0, ncols, srcT, srcb, evac):
        tr = self.tr
        pcols = WSLOT // nk
        for p0 in range(0, ncols, pcols):
            pc = min(pcols, ncols - p0)
            w, wb = self.wpiece(wap, r0, nk, c0 + p0, pc)
            for f in range(pc // 128):
                pt, pb = self.next_ps()
                for k in range(nk):
                    tr.op("pe", lambda e: e.matmul(pt[:], lhsT=w[:, k, f * 128:(f + 1) * 128], rhs=srcT[:, k, :],
                                                   start=(k == 0), stop=(k == nk - 1)),
                          reads=[wb] + list(srcb), writes=[pb])
                evac((p0 // 128) + f, pt, pb)

    def proj_tm(self, wap, r0, nk, c0, ncols, srcT, srcb, evac, blk=512):
        tr = self.tr
        kper = max(1, WSLOT // blk)
        half = 0
        for cb, cc in enumerate(range(0, ncols, blk)):
            w_ = min(blk, ncols - cc)
            pss = [self.ps[(half * 4 + s)] for s in range(NSUB)]
            half ^= 1
            for k0 in range(0, nk, kper):
                kn = min(kper, nk - k0)
                w, wb = self.wpiece(wap, r0 + k0, kn, c0 + cc, w_)
                for kk in range(kn):
                    k = k0 + kk
                    for s in range(NSUB):
                        pt, pb = pss[s]
                        tr.op("pe", lambda e: e.matmul(pt[:, 0:w_], lhsT=srcT[:, k, s * 128:(s + 1) * 128], rhs=w[:, kk, :],
                                                       start=(k == 0), stop=(k == nk - 1)),
                              reads=[wb] + list(srcb), writes=[pb])
            for s in range(NSUB):
                pt, pb = pss[s]
                evac(cb, s, pt, pb, w_)

    def resid_add(self, cb, s, pt, pb, w_):
        xs = self.xres[:, s, cb * 512:cb * 512 + w_]
        self.tr.op("dve", lambda e: e.tensor_tensor(out=xs, in0=xs, in1=pt[:, 0:w_], op=ALU.add),
                   reads=[pb, self.xres_b[s]], writes=[self.xres_b[s]])

    def ffn(self, li):
        tr = self.tr
        self.rmsnorm_T(self.a["norm_ffn"][li:li + 1, :])
        for h in range(2):
            def evac_up(f, pt, pb):
                tmp, tb = self.tmpf[f % 2]
                tr.op("act", lambda e: e.activation(out=tmp[:], in_=pt[:], func=AF.Relu), reads=[pb], writes=[tb])
                eng = "dve"
                tr.op(eng, lambda e: e.tensor_tensor(out=self.hT[:, f, :], in0=tmp[:], in1=tmp[:], op=ALU.mult),
                      reads=[tb], writes=[self.hT_b, self.hT_b2])
            self.proj_fm(self.a["ffn_up"][li], 0, 16, h * 4096, 4096, self.xnT, [self.xnT_b], evac_up)
            self.proj_tm(self.a["ffn_down"][li], h * 32, 32, 0, 2048, self.hT, [self.hT_b, self.hT_b2], self.resid_add)

    def ple(self, li, tok0):
        tr = self.tr
        self.rmsnorm_T(self.a["norm_ple"][li:li + 1, :])
        pf, pfb = self.p_f
        tr.dma("sp", pf[:], self.a["p"][tok0:tok0 + TT, :].rearrange("(s p) c -> p s c", p=128), writes=[pfb])
        for s in range(NSUB):
            pbf, pbb = self.p_bf[s % 2]
            tr.op("dve", lambda e: e.tensor_copy(out=pbf[:], in_=pf[:, s, :]), reads=[pfb], writes=[pbb])
            self.transpose_into(pbf, pbb, 2, self.pT, self.pT_b, s)
        for cb in range(4):
            pg = [self.ps[s] for s in range(4)]
            pp = [self.ps[4 + s] for s in range(4)]
            w, wb = self.wpiece(self.a["ple_gate"][li], 0, 16, cb * 512, 512)
            wple, wpb = self.wpiece(self.a["ple_proj"][li], 0, 2, cb * 512, 512)
            for k in range(16):
                for s in range(NSUB):
                    pt, pb = pg[s]
                    tr.op("pe", lambda e: e.matmul(pt[:], lhsT=self.xnT[:, k, s * 128:(s + 1) * 128], rhs=w[:, k, :],
                                                   start=(k == 0), stop=(k == 15)), reads=[wb, self.xnT_b], writes=[pb])
            for k in range(2):
                for s in range(NSUB):
                    pt, pb = pp[s]
                    tr.op("pe", lambda e: e.matmul(pt[:], lhsT=self.pT[:, k, s * 128:(s + 1) * 128],
                                                   rhs=wple[:, k, :],
                                                   start=(k == 0), stop=(k == 1)), reads=[wpb, self.pT_b], writes=[pb])
            for s in range(NSUB):
                tmp, tb = self.tmpf[s % 2]
                tr.op("act", lambda e: e.activation(out=tmp[:], in_=pg[s][0][:], func=AF.Sigmoid), reads=[pg[s][1]], writes=[tb])
                tr.op("dve", lambda e: e.tensor_tensor(out=tmp[:], in0=tmp[:], in1=pp[s][0][:], op=ALU.mult),
                      reads=[tb, pp[s][1]], writes=[tb])
                xs = self.xres[:, s, cb * 512:(cb + 1) * 512]
                tr.op("dve", lambda e: e.tensor_tensor(out=xs, in0=xs, in1=tmp[:], op=ALU.add),
                      reads=[tb, self.xres_b[s]], writes=[self.xres_b[s]])

    def gmlp(self):
        tr = self.tr
        a = self.a
        self.rmsnorm_T(a["norm_mix"][0:1, :])
        tr.dma("sp", self.hT[:, 16:24, :].rearrange("p k t -> p (k t)").bitcast(F32), a["bsT"].partition_broadcast(128),
               writes=[self.hT_b2])

        def evac_u(f, pt, pb):
            tr.op("act", lambda e: e.activation(out=self.hT[:, f, :], in_=pt[:], func=AF.Gelu_apprx_tanh),
                  reads=[pb], writes=[self.hT_b])
        self.proj_fm(a["gm_in"], 0, 16, 0, 2048, self.xnT, [self.xnT_b], evac_u)

        def evac_v(cb, s, pt, pb, w_):
            vs = self.v_f[:, s, cb * 512:(cb + 1) * 512]
            tr.op("act", lambda e: e.activation(out=vs, in_=pt[:], func=AF.Gelu_apprx_tanh), reads=[pb], writes=[self.v_fb[s]])
            tr.op("dve", lambda e: e.bn_stats(out=self.stats[:, s, cb * 6:(cb + 1) * 6], in_=vs), reads=[self.v_fb[s]],
                  writes=[self.stats_b[s]])
        self.proj_tm(a["gm_in"], 0, 16, 2048, 2048, self.xnT, [self.xnT_b], evac_v)
        lg_t, lg_b = self.bcast_load(a["gm_ln_g"])
        lb_t, lb_b = self.bcast_load(a["gm_ln_b"])
        for s in range(NSUB):
            mv, mvb = self.small[s % 2]
            tr.op("dve", lambda e: e.bn_aggr(out=mv[:, 0:2], in_=self.stats[:, s, :]), reads=[self.stats_b[s]], writes=[mvb])
            tr.op("dve", lambda e: e.tensor_scalar(out=mv[:, 2:3], in0=mv[:, 1:2], scalar1=EPS, scalar2=1.0, op0=ALU.add, op1=ALU.mult),
                  reads=[mvb], writes=[mvb])
            tr.op("act", lambda e: e.activation(out=mv[:, 3:4], in_=mv[:, 2:3], func=AF.Sqrt), reads=[mvb], writes=[mvb])
            tr.op("dve", lambda e: e.reciprocal(out=mv[:, 4:5], in_=mv[:, 3:4]), reads=[mvb], writes=[mvb])
            vs = self.v_f[:, s, :]
            tr.op("dve", lambda e: e.tensor_scalar(out=vs, in0=vs, scalar1=mv[:, 0:1], scalar2=mv[:, 4:5],
                                                   op0=ALU.subtract, op1=ALU.mult), reads=[mvb, self.v_fb[s]], writes=[self.v_fb[s]])
            tr.op("dve", lambda e: e.tensor_tensor(out=vs, in0=vs, in1=lg_t[:], op=ALU.mult), reads=[self.v_fb[s], lg_b],
                  writes=[self.v_fb[s]])
            vl, vlb = self.xn_tm[s % 2]
            tr.op("dve", lambda e: e.tensor_tensor(out=vl[:], in0=vs, in1=lb_t[:], op=ALU.add), reads=[self.v_fb[s], lb_b],
                  writes=[vlb])
            for g4 in range(4):
                pt, pb = self.next_ps()
                for gg in range(4):
                    g = g4 * 4 + gg
                    tr.op("pe", lambda e: e.matmul(pt[:, gg * 128:(gg + 1) * 128], lhsT=vl[:, g * 128:(g + 1) * 128],
                                                   rhs=self.wsT[:, g, :], start=True, stop=True),
                          reads=[vlb, self.wsT_b], writes=[pb])
                tmp, tb = self.tmpf[g4 % 2]
                tv = tmp[:].rearrange("p (g t) -> p g t", g=4)
                tr.op("dve", lambda e: e.tensor_tensor(out=tv, in0=pt[:].rearrange("p (g t) -> p g t", g=4),
                                                       in1=self.bsT[:, g4 * 4:(g4 + 1) * 4, :], op=ALU.add),
                      reads=[pb, self.bsT_b], writes=[tb])
                u = self.hT[:, g4 * 4:(g4 + 1) * 4, s * 128:(s + 1) * 128]
                tr.op("dve", lambda e: e.tensor_tensor(out=u, in0=tv, in1=u, op=ALU.mult), reads=[tb, self.hT_b],
                      writes=[self.hT_b])
        mT = self.hT[:, 0:16, :]
        self.proj_tm(a["gm_out"], 0, 16, 0, 2048, mT, [self.hT_b], self.resid_add)

    def nsa_proj(self, tok0, pos0):
        tr = self.tr
        a = self.a
        self.rmsnorm_T(a["norm_mix"][1:2, :])
        cs, csb = self.cs
        tr.dma("sp", cs[:], a["rope"][tok0:tok0 + TT, :].rearrange("(s p) c -> p s c", p=128), writes=[csb])
        out_evs = self.out_evs

        def evac(cb, s, pt, pb, w_):
            if cb == 10:
                g, gb = self.small[2 + s % 2]
                tr.op("act", lambda e: e.activation(out=g[:, 0:48], in_=pt[:, 0:48], func=AF.Sigmoid), reads=[pb], writes=[gb])
                out_evs.append(tr.dma("sp", a["gates_out"][tok0 + s * 128:tok0 + (s + 1) * 128, :], g[:, 0:48], reads=[gb]))
                return
            tmp, tb = self.tmpf[s % 2]
            is_q = cb < 4
            rope = is_q or cb in (6, 8)
            ob, obb = self.obf[(cb * 4 + s) % 2]
            if not rope:
                tr.op("act", lambda e: e.copy(out=ob[:], in_=pt[:]), reads=[pb], writes=[obb])
            else:
                sc = (128.0 ** -0.5) if is_q else 1.0
                tr.op("act", lambda e: e.mul(out=tmp[:], in_=pt[:], mul=sc), reads=[pb], writes=[tb])
                v = tmp[:].rearrange("p (h d) -> p h d", h=4)
                x1, x2 = v[:, :, 0:16], v[:, :, 16:32]
                cos = cs[:, s, 0:16].unsqueeze(1).to_broadcast([128, 4, 16])
                sin = cs[:, s, 16:32].unsqueeze(1).to_broadcast([128, 4, 16])
                r, rb = self.ropet[s % 2]
                rv = r[:].rearrange("p (j h d) -> p j h d", j=4, h=4)
                tr.op("dve", lambda e: e.tensor_tensor(out=rv[:, 0], in0=x1, in1=cos, op=ALU.mult), reads=[tb, csb], writes=[rb])
                tr.op("dve", lambda e: e.tensor_tensor(out=rv[:, 1], in0=x2, in1=sin, op=ALU.mult), reads=[tb, csb], writes=[rb])
                tr.op("dve", lambda e: e.tensor_tensor(out=rv[:, 2], in0=x2, in1=cos, op=ALU.mult), reads=[tb, csb], writes=[rb])
                tr.op("dve", lambda e: e.tensor_tensor(out=rv[:, 3], in0=x1, in1=sin, op=ALU.mult), reads=[tb, csb], writes=[rb])
                tr.op("dve", lambda e: e.tensor_tensor(out=x1, in0=rv[:, 0], in1=rv[:, 1], op=ALU.subtract), reads=[rb, tb], writes=[tb])
                tr.op("dve", lambda e: e.tensor_tensor(out=x2, in0=rv[:, 2], in1=rv[:, 3], op=ALU.add), reads=[rb, tb], writes=[tb])
                tr.op("act", lambda e: e.copy(out=ob[:], in_=tmp[:]), reads=[tb], writes=[obb])
            out_evs.append(tr.dma("sp", a["qkv_out"][tok0 + s * 128:tok0 + (s + 1) * 128, cb * 512:(cb + 1) * 512], ob[:], reads=[obb]))
        self.proj_tm(a["nsa_in"], 0, 16, 0, 5168, self.xnT, [self.xnT_b], evac)

    def attn_out(self, tok0):
        tr = self.tr
        a = self.a
        for s in range(NSUB):
            ot, otb = self.xn_tm[s % 2]
            tr.dma("sp", ot[:], a["o_in"][tok0 + s * 128:tok0 + (s + 1) * 128, :], writes=[otb])
            self.transpose_into(ot, otb, 16, self.xnT, self.xnT_b, s)
        self.proj_tm(a["nsa_out"], 0, 16, 0, 2048, self.xnT, [self.xnT_b], self.resid_add)

    def final_norm(self, tok0):
        tr = self.tr
        g_t, g_b = self.bcast_load(self.a["final_norm"])
        for s in range(NSUB):
            xs = self.xres[:, s, :]
            xb = self.xres_b[s]
            junk, jb = self.xn_tm[s % 2]
            ss, ssb = self.small[s % 2]
            tr.op("act", lambda e: e.activation(out=junk[:], in_=xs, func=AF.Square, accum_out=ss[:, 0:1]),
                  reads=[xb], writes=[jb, ssb])
            tr.op("dve", lambda e: e.tensor_scalar(out=ss[:, 1:2], in0=ss[:, 0:1], scalar1=1.0 / D, scalar2=EPS,
                                                   op0=ALU.mult, op1=ALU.add), reads=[ssb], writes=[ssb])
            tr.op("act", lambda e: e.activation(out=ss[:, 2:3], in_=ss[:, 1:2], func=AF.Sqrt), reads=[ssb], writes=[ssb])
            tr.op("dve", lambda e: e.reciprocal(out=ss[:, 3:4], in_=ss[:, 2:3]), reads=[ssb], writes=[ssb])
            tr.op("dve", lambda e: e.scalar_tensor_tensor(out=xs, in0=xs, scalar=ss[:, 3:4], in1=g_t[:],
                                                          op0=ALU.mult, op1=ALU.mult),
                  reads=[xb, ssb, g_b], writes=[xb])
            self.out_evs.append(tr.dma("sp", self.a["x_out"][tok0 + s * 128:tok0 + (s + 1) * 128, :], xs, reads=[xb]))

    def _build(self):
        nc, tr = self.nc, self.tr
        mode = self.mode
        ntok = self.ntiles * TT
        a = self.a = {}
        a["x"] = self.dram_in("x", [ntok, D], F32)
        a["p"] = self.dram_in("p", [ntok, 256], F32)
        a["ident"] = self.dram_in("ident", [128, 128], BF16)
        for nm in ("norm_mix", "norm_ffn", "norm_ple"):
            a[nm] = self.dram_in(nm, [2, D], F32)
        a["ffn_up"] = [self.dram_in(f"ffn_up{mode}", [D, DFF], F32)] * 2
        a["ffn_down"] = [self.dram_in(f"ffn_down{mode}", [DFF, D], F32)] * 2
        a["ple_proj"] = [self.dram_in(f"ple_proj{mode}", [256, D], F32)] * 2
        a["ple_gate"] = [self.dram_in(f"ple_gate{mode}", [D, D], F32)] * 2
        li = 0 if mode == 1 else 1
        if mode == 1:
            a["gm_in"] = self.dram_in("gm_in", [D, 4096], F32)
            a["gm_out"] = self.dram_in("gm_out", [D, D], F32)
            a["gm_ln_g"] = self.dram_in("gm_ln_g", [1, D], F32)
            a["gm_ln_b"] = self.dram_in("gm_ln_b", [1, D], F32)
            a["wsT"] = self.dram_in("wsT", [128, 16, 128], F32)
            a["cmask"] = self.dram_in("cmask", [128, 128], F32)
            a["bsT"] = self.dram_in("bsT", [1, 2048], F32)
            a["nsa_in"] = self.dram_in("nsa_in", [D, 5168], F32)
            a["rope"] = self.dram_in("rope", [ntok, 32], F32)
            a["x_out"] = self.dram_out("x_out", [ntok, D], F32)
            a["qkv_out"] = self.dram_out("qkv_out", [ntok, 5120], BF16)
            a["gates_out"] = self.dram_out("gates_out", [ntok, 48], F32)
        else:
            a["o_in"] = self.dram_in("o_in", [ntok, D], BF16)
            a["nsa_out"] = self.dram_in("nsa_out", [D, D], F32)
            a["final_norm"] = self.dram_in("final_norm", [1, D], F32)
            a["x_out"] = self.dram_out("x_out", [ntok, D], F32)

        def mk(name, shape, dt):
            return self.sb(name, shape, dt), Buf()
        self.xres = self.sb("xres", [128, NSUB, D], F32)
        self.xres_b = [Buf() for _ in range(NSUB)]
        self.xn_tm = [mk(f"xn_tm{i}", [128, D], BF16) for i in range(2)]
        self.xnT = self.sb("xnT", [128, 16, TT], BF16)
        self.xnT_b = Buf()
        self.hT = self.sb("hT", [128, 32, TT], BF16)
        self.hT_b = Buf()
        self.hT_b2 = Buf()
        self.wring = [mk(f"wr{i}", [128, WSLOT], BF16) for i in range(3)]
        self.wrr = 0
        self.wq = 0
        self.gbc = [mk(f"gbc{i}", [128, D], F32) for i in range(2)]
        self.grr = 0
        self.small = [mk(f"small{i}", [128, 64], F32) for i in range(4)]
        self.tmpf = [mk(f"tmpf{i}", [128, 512], F32) for i in range(2)]
        self.ident, self.ident_b = mk("ident_sb", [128, 128], BF16)
        self.p_f = mk("p_f", [128, NSUB, 256], F32)
        self.p_bf = [mk(f"p_bf{i}", [128, 256], BF16) for i in range(2)]
        self.pT = self.sb("pT", [128, 2, TT], BF16)
        self.pT_b = Buf()
        self.ps = [(self.es.enter_context(nc.psum_tensor(f"ps{i}", [128, 512], F32)), Buf()) for i in range(8)]
        self.prr = 0
        self.out_evs = []
        tr.dma("sp", self.ident[:], a["ident"], writes=[self.ident_b])
        self.ident_b.const = True
        if mode == 1:
            self.v_f = self.sb("v_f", [128, NSUB, D], F32)
            self.v_fb = [Buf() for _ in range(NSUB)]
            self.stats = self.sb("stats", [128, NSUB, 24], F32)
            self.stats_b = [Buf() for _ in range(NSUB)]
            self.wsT, self.wsT_b = mk("wsT_sb", [128, 16, 128], BF16)
            self.bsT = self.hT[:, 16:24, :].rearrange("p k t -> p (k t)").bitcast(F32).rearrange("p (g t) -> p g t", g=16)
            self.bsT_b = self.hT_b2
            wsf0, wsfb = self.gbc[0]
            cm0, cmb = self.gbc[1]
            wsf = wsf0[:].rearrange("p (g t) -> p g t", g=16)
            cm = cm0[:, 0:128]
            tr.dma("sp", wsf, a["wsT"], writes=[wsfb])
            tr.dma("sp", cm, a["cmask"], writes=[cmb])
            tr.op("dve", lambda e: e.tensor_tensor(out=self.wsT[:], in0=wsf, in1=cm.unsqueeze(1).to_broadcast([128, 16, 128]),
                                                   op=ALU.mult), reads=[wsfb, cmb], writes=[self.wsT_b])
            self.wsT_b.const = True
            self.cs = mk("cs", [128, NSUB, 32], F32)
            self.obf = [mk(f"obf{i}", [128, 512], BF16) for i in range(2)]
            self.ropet = [mk(f"ropet{i}", [128, 256], F32) for i in range(2)]

        for ti in range(self.ntiles):
            tok0 = ti * TT
            for s in range(NSUB):
                tr.dma("sp", self.xres[:, s, :], a["x"][tok0 + s * 128:tok0 + (s + 1) * 128, :], writes=[self.xres_b[s]])
            if mode == 1:
                import os
                st = os.environ.get("STAGES", "gmlp,ffn,ple").split(",")
                if "gmlp" in st:
                    self.gmlp()
                if "ffn" in st:
                    self.ffn(0)
                if "ple" in st:
                    self.ple(0, tok0)
                for s in range(NSUB):
                    self.out_evs.append(tr.dma("sp", a["x_out"][tok0 + s * 128:tok0 + (s + 1) * 128, :], self.xres[:, s, :],
                                               reads=[self.xres_b[s]]))
                self.nsa_proj(tok0, 0)
            else:
                self.attn_out(tok0)
                self.ffn(1)
                self.ple(1, tok0)
                self.final_norm(tok0)
        tr.finish(self.out_evs)


QT = 256
NCMP = 1023


def nsa_consts():
    kl = np.arange(128)[:, None]
    ql = np.arange(QT)[None, :]
    masks = np.zeros((128, 13, QT), np.float32)
    masks[:, 0] = (kl <= ql)
    masks[:, 1] = (128 + kl <= ql)
    masks[:, 2] = (kl > ql)
    masks[:, 3] = (kl + 128 > ql)
    for r in range(9):
        masks[:, 4 + r] = (16 * kl + 31 <= 256 * r + ql)
    c = np.arange(1024)[:, None]
    s = np.arange(256)[None, :]
    ov = np.clip(np.minimum(c * 16 + 32, s * 64 + 64) - np.maximum(c * 16, s * 64), 0, None) / 16.0
    ov[1023] = 0
    M = ov.reshape(8, 128, 256).transpose(1, 0, 2)
    R = np.zeros((128, 64, 128), np.float32)
    for j in range(64):
        R[2 * j, j, :64] = 1
        R[2 * j + 1, j, 64:] = 1
    F = np.zeros((128, 512), np.float32)
    qq = np.arange(128)[:, None]
    rel = np.arange(512)[None, :] - 256
    cur = (qq >= 64).astype(np.int64)
    F[(rel > cur)] = -1e30
    F[(rel == cur) | (rel == cur - 1)] = 1e9
    pos = (np.arange(1024) * 16 + 31).astype(np.float32)
    inv = np.power(np.float32(500000.0), -np.arange(16, dtype=np.float32) * 2.0 / 32)
    ang = pos[:, None] * inv[None, :]
    crope = np.concatenate([np.cos(ang), np.sin(ang)], axis=1).astype(np.float32).reshape(8, 128, 32).transpose(1, 0, 2)
    return dict(masks=masks.astype(NPBF), Mtab=np.ascontiguousarray(M).astype(NPBF), Rtab=R.astype(NPBF), Fbase=F,
                crope=np.ascontiguousarray(crope), ident=np.eye(128, dtype=np.float32).astype(NPBF))


class NSA:
    def __init__(self, nq=T // QT, Tk=T):
        self.nq = nq
        self.Tk = Tk
        self.nc = bass.Bass("TRN2", target_bir_lowering=False)
        self.es = contextlib.ExitStack()

    def dram_in(self, name, shape, dt):
        return self.nc.dram_tensor(name, list(shape), dt, kind="ExternalInput").ap()

    def sb(self, name, shape, dt):
        return self.es.enter_context(self.nc.sbuf_tensor(name, list(shape), dt))

    def mk(self, name, shape, dt):
        return self.sb(name, shape, dt), Buf()

    def build(self):
        with self.es:
            self.tr = Tracker(self.nc, self.es)
            self._build()
        return self.nc

    @staticmethod
    def _alias(b):
        n = Buf()
        n.writer = b.writer
        n.readers = dict(b.readers)
        return n

    def load_T(self, src, dstT, dstb, nchunks):
        tr = self.tr
        for c0 in range(0, nchunks, 8):
            n = min(8, nchunks - c0)
            st, stb = self.stage[(c0 // 8) % 2]
            tr.dma("sp", st[:, 0:n, :], src[c0 * 128:(c0 + n) * 128, :].rearrange("(c p) d -> p c d", p=128), writes=[stb])
            pt, pb = self.ps[7] if (c0 // 8) % 2 == 0 else self.ps[6]
            pv = pt[:].bitcast(BF16)
            for j in range(n):
                tr.op("pe", lambda e: e.transpose(out=pv[:, j * 128:(j + 1) * 128], in_=st[:, j, :], identity=self.ident[:]),
                      reads=[stb, self.ident_b], writes=[pb])
            eng = "act" if (c0 // 8) % 2 == 0 else "dve"
            o = dstT[:, c0 * 128:(c0 + n) * 128]
            if eng == "act":
                tr.op("act", lambda e: e.copy(out=o, in_=pv[:, 0:n * 128]), reads=[pb], writes=[dstb])
            else:
                tr.op("dve", lambda e: e.tensor_copy(out=o, in_=pv[:, 0:n * 128]), reads=[pb], writes=[dstb])

    def compress(self, src, w1, w2, peT, is_k):
        tr = self.tr
        nch = self.Tk // 128
        ncmp = (self.Tk - 32) // 16 + 1
        R1, R1b = self.R1
        self.load_T(src, R1, R1b, nch)
        w1s, w1b = self.R2
        w1v = w1s[:].rearrange("p (l h) -> p l h", l=32)
        for l0 in range(0, 32, 8):
            tr.dma("pool", w1v[:, l0:l0 + 8, :], w1[l0 * 128:(l0 + 8) * 128, :].rearrange("(l p) h -> p l h", p=128), writes=[w1b])
        w2s, w2b = self.w2s
        tr.dma("pool", w2s[:], w2.rearrange("(c p) d -> p c d", p=128), writes=[w2b])
        pef, pefb = self.pef
        tr.dma("sp", pef[:], peT, writes=[pefb])
        peb, pebb = self.peb
        tr.op("dve", lambda e: e.tensor_copy(out=peb[:], in_=pef[:]), reads=[pefb], writes=[pebb])
        hid, hidb = self.hid
        tr.op("pool", lambda e: e.memset(hid[:], 0.0), writes=[hidb])
        bias, biasb = self.small[0]
        for hc in range(4):
            pt, pb = self.ps[hc % 2]
            for l in range(32):
                tr.op("pe", lambda e: e.matmul(pt[:, 0:1], lhsT=w1v[:, l, hc * 128:(hc + 1) * 128], rhs=peb[:, l:l + 1],
                                               start=(l == 0), stop=(l == 31)), reads=[w1b, pebb], writes=[pb])
            tr.op("dve", lambda e: e.tensor_copy(out=bias[:, hc:hc + 1], in_=pt[:, 0:1]), reads=[pb], writes=[biasb])
        for hc in range(4):
            for cb in range(0, ncmp, 512):
                n = min(512, ncmp - cb)
                pt, pb = self.ps[2 + ((hc * 2 + cb // 512) % 2)]
                for l in range(32):
                    rhs = R1[:, l + 16 * cb:l + 16 * cb + 16 * (n - 1) + 1:16]
                    tr.op("pe", lambda e: e.matmul(pt[:, 0:n], lhsT=w1v[:, l, hc * 128:(hc + 1) * 128], rhs=rhs,
                                                   start=(l == 0), stop=(l == 31)), reads=[w1b, R1b], writes=[pb])
                tr.op("act", lambda e: e.activation(out=hid[:, hc, cb:cb + n], in_=pt[:, 0:n], func=AF.Gelu_apprx_tanh,
                                                    bias=bias[:, hc:hc + 1]), reads=[pb, biasb], writes=[hidb])
        ncc = (ncmp + 127) // 128
        for j in range(ncc):
            pt, pb = self.ps[4 + j % 2]
            for hc in range(4):
                tr.op("pe", lambda e: e.matmul(pt[:, 0:128], lhsT=hid[:, hc, j * 128:(j + 1) * 128], rhs=w2s[:, hc, :],
                                               start=(hc == 0), stop=(hc == 3)), reads=[hidb, w2b], writes=[pb])
            if not is_k:
                tr.op("act", lambda e: e.copy(out=self.vcmp1[:, j, 0:128], in_=pt[:, 0:128]), reads=[pb], writes=[self.vcmp1_b])
            else:
                tmp, tb = self.tmpc[j % 2]
                tr.op("act", lambda e: e.copy(out=tmp[:], in_=pt[:, 0:128]), reads=[pb], writes=[tb])
                x1, x2 = tmp[:, 0:16], tmp[:, 16:32]
                cos, sin = self.crope[:, j, 0:16], self.crope[:, j, 16:32]
                r, rb = self.small[1]
                tr.op("dve", lambda e: e.tensor_tensor(out=r[:, 0:16], in0=x1, in1=cos, op=ALU.mult), reads=[tb, self.crope_b], writes=[rb])
                tr.op("dve", lambda e: e.tensor_tensor(out=r[:, 16:32], in0=x2, in1=sin, op=ALU.mult), reads=[tb, self.crope_b], writes=[rb])
                tr.op("dve", lambda e: e.tensor_tensor(out=r[:, 32:48], in0=x2, in1=cos, op=ALU.mult), reads=[tb, self.crope_b], writes=[rb])
                tr.op("dve", lambda e: e.tensor_tensor(out=r[:, 48:64], in0=x1, in1=sin, op=ALU.mult), reads=[tb, self.crope_b], writes=[rb])
                tr.op("dve", lambda e: e.tensor_tensor(out=x1, in0=r[:, 0:16], in1=r[:, 16:32], op=ALU.subtract), reads=[rb, tb], writes=[tb])
                tr.op("dve", lambda e: e.tensor_tensor(out=x2, in0=r[:, 32:48], in1=r[:, 48:64], op=ALU.add), reads=[rb, tb], writes=[tb])
                tb16, tb16b = self.tmpc16[j % 2]
                tr.op("dve", lambda e: e.tensor_copy(out=tb16[:], in_=tmp[:]), reads=[tb], writes=[tb16b])
                p2, p2b = self.ps[6 + j % 2]
                pv = p2[:].bitcast(BF16)
                tr.op("pe", lambda e: e.transpose(out=pv[:, 0:128], in_=tb16[:], identity=self.ident[:]),
                      reads=[tb16b, self.ident_b], writes=[p2b])
                tr.op("act", lambda e: e.copy(out=self.kcmpT[:, j * 128:(j + 1) * 128], in_=pv[:, 0:128]), reads=[p2b],
                      writes=[self.kcmpT_b])

    def run_phase(self, batches, accs):
        tr = self.tr
        fib = set()
        prev = None
        for b in list(batches) + [None]:
            cur = None
            if b is not None:
                bi = self.bcount
                self.bcount += 1
                S, Sb = self.sbig[bi % self.nsbuf]
                P, Pb = self.pbig[bi % 2]
                nu = len(b["units"])
                if "pre" in b:
                    b["pre"]()
                for u, (kT, kb, h) in enumerate(b["units"]):
                    neg = b.get("neg")
                    tr.op("pe", lambda e: e.matmul(S[:, u * QT:(u + 1) * QT], lhsT=kT, rhs=self.QTt[:, h, :], start=True, stop=(neg is None)),
                          reads=list(kb) + [self.QT_b], writes=[Sb])
                    if neg is not None:
                        tr.op("pe", lambda e: e.matmul(S[:, u * QT:(u + 1) * QT], lhsT=neg[0], rhs=neg[1], start=False, stop=True),
                              reads=list(neg[2]), writes=[Sb])
                n = nu * QT
                if "post" in b:
                    b["post"]()
                tr.op("act", lambda e: e.activation(out=P[:, 0:n], in_=S[:, 0:n], func=AF.Exp), reads=[Sb], writes=[Pb])
                pv3 = P[:, 0:n].rearrange("p (u q) -> p u q", u=nu)
                for m, mb in b["masks"]:
                    tr.op("dve", lambda e: e.tensor_tensor(out=pv3, in0=pv3, in1=m.unsqueeze(1).to_broadcast([128, nu, QT]), op=ALU.mult),
                          reads=[Pb] + list(mb), writes=[Pb])
                cur = (P, Pb, b)
            if prev is not None:
                P, Pb, pb_ = prev
                v = pb_["v"]
                ncol = v.shape[-1]
                for u, (kT, kb, h) in enumerate(pb_["units"]):
                    for sub in range(2):
                        acc, accb, bank = accs[h][sub]
                        st_flag = bank not in fib
                        fib.add(bank)
                        tr.op("pe", lambda e: e.matmul(acc[:, 0:ncol], lhsT=P[:, u * QT + sub * 128:u * QT + (sub + 1) * 128], rhs=v,
                                                       start=st_flag, stop=True, skip_group_check=True),
                              reads=[Pb] + list(pb_["vb"]), writes=[accb])
            prev = cur

    def _build(self):
        import os
        nc, tr = self.nc, self.tr
        Tk = self.Tk
        nch = Tk // 128
        a = {}
        a["q"] = self.dram_in("q", [Tk, 512], BF16)
        for nm in ("kc", "vc", "ks", "vs", "kw", "vw"):
            a[nm] = self.dram_in(nm, [Tk, 128], BF16)
        a["gates"] = self.dram_in("gates", [Tk, 12], F32)
        for nm in ("kc_w1", "vc_w1"):
            a[nm] = self.dram_in(nm, [4096, 512], F32)
        for nm in ("kc_w2", "vc_w2"):
            a[nm] = self.dram_in(nm, [512, 128], F32)
        for nm in ("kc_peT", "vc_peT"):
            a[nm] = self.dram_in(nm, [128, 32], F32)
        a["masks"] = self.dram_in("masks", [128, 13, QT], BF16)
        a["Mtab"] = self.dram_in("Mtab", [128, 8, 256], BF16)
        a["Rtab"] = self.dram_in("Rtab", [128, 64, 128], BF16)
        a["Fbase"] = self.dram_in("Fbase", [128, 512], F32)
        a["crope"] = self.dram_in("crope", [128, 8, 32], F32)
        a["ident"] = self.dram_in("ident", [128, 128], BF16)
        o_out = self.nc.dram_tensor("o_out", [self.nq * QT, 512], BF16, kind="ExternalOutput").ap()
        mk = self.mk
        self.R1 = mk("R1", [128, Tk], BF16)
        self.R2 = mk("R2", [128, max(Tk, 16384)], BF16)
        self.vs1, self.vs1_b = mk("vs1", [128, nch, 129], BF16)
        self.vw1, self.vw1_b = mk("vw1", [128, nch, 129], BF16)
        self.hid = mk("hid", [128, 4, 1024], BF16)
        self.w2s = mk("w2s", [128, 4, 128], BF16)
        self.pef = mk("pef", [128, 32], F32)
        self.peb = mk("peb", [128, 32], BF16)
        self.stage = [mk(f"stage{i}", [128, 8, 128], BF16) for i in range(2)]
        self.small = [mk(f"small{i}", [128, 64], F32) for i in range(4)]
        self.tmpc = [mk(f"tmpc{i}", [128, 128], F32) for i in range(2)]
        self.tmpc16 = [mk(f"tmpc16{i}", [128, 128], BF16) for i in range(2)]
        self.kcmpT, self.kcmpT_b = mk("kcmpT", [128, 1024], BF16)
        self.vcmp1, self.vcmp1_b = mk("vcmp1", [128, 8, 385], BF16)
        self.ident, self.ident_b = mk("ident_sb", [128, 128], BF16)
        self.masks, self.masks_b = mk("masks_sb", [128, 13, QT], BF16)
        self.Rtab, self.Rtab_b = mk("Rtab_sb", [128, 64, 128], BF16)
        self.Fbase, self.Fbase_b = mk("Fbase_sb", [128, 512], F32)
        self.crope, self.crope_b = mk("crope_sb", [128, 8, 32], F32)
        big0 = self.es.enter_context(nc.psum_tensor("psbig0", [128, 1024], F32))
        big1 = self.es.enter_context(nc.psum_tensor("psbig1", [128, 1024], F32))
        b0, b1 = Buf(), Buf()
        self.ps = [(big0[:, 0:512], b0), (big0[:, 512:1024], b0), (big1[:, 0:512], b1), (big1[:, 512:1024], b1)]
        for i in range(4, 8):
            self.ps.append((self.es.enter_context(nc.psum_tensor(f"ps{i}", [128, 512], F32))[:], Buf()))
        self.sbig = [(big0[:], b0), (big1[:], b1)]
        self.nsbuf = 2
        self.bcount = 0
        for t_, b_, src in ((self.ident, self.ident_b, a["ident"]), (self.masks, self.masks_b, a["masks"]),
                            (self.Rtab, self.Rtab_b, a["Rtab"]), (self.Fbase, self.Fbase_b, a["Fbase"]),
                            (self.crope, self.crope_b, a["crope"])):
            tr.dma("sp", t_[:], src, writes=[b_])
            b_.const = True
        for _ in range(int(os.environ.get("ACT_DUMMY", "0"))):
            tr.op("act", lambda e: e.copy(out=self.small[3][0][:, 0:8], in_=self.small[3][0][:, 8:16]), writes=[self.small[3][1]])
        for _ in range(int(os.environ.get("DVE_DUMMY", "0"))):
            tr.op("dve", lambda e: e.tensor_copy(out=self.small[3][0][:, 16:24], in_=self.small[3][0][:, 24:32]), writes=[self.small[3][1]])
        tr.op("pool", lambda e: e.memset(self.kcmpT[:], 0.0), writes=[self.kcmpT_b])
        tr.op("pool", lambda e: e.memset(self.vcmp1[:], 0.0), writes=[self.vcmp1_b])
        self.compress(a["kc"], a["kc_w1"], a["kc_w2"], a["kc_peT"], True)
        self.compress(a["vc"], a["vc_w1"], a["vc_w2"], a["vc_peT"], False)
        tr.op("pool", lambda e: e.memset(self.vcmp1[:, :, 128:129], 1.0), reads=[], writes=[self.vcmp1_b])
        tr.dma("sp", self.vcmp1[:, :, 129:385], a["Mtab"], writes=[self.vcmp1_b])
        STOP = os.environ.get("NSA_STOP", "")
        if STOP == "compress":
            tr.finish([]); return
        ksT, ksT_b = self.R1
        kwT, kwT_b = self.R2
        self.load_T(a["ks"], ksT, ksT_b, nch)
        self.load_T(a["kw"], kwT, kwT_b, nch)
        tr.op("pool", lambda e: e.memset(self.vs1[:, :, 128:129], 1.0), writes=[self.vs1_b])
        tr.op("pool", lambda e: e.memset(self.vw1[:, :, 128:129], 1.0), writes=[self.vw1_b])
        for c0 in range(0, nch, 8):
            n = min(8, nch - c0)
            tr.dma("sp", self.vs1[:, c0:c0 + n, 0:128], a["vs"][c0 * 128:(c0 + n) * 128, :].rearrange("(c p) d -> p c d", p=128),
                   writes=[self.vs1_b])
            tr.dma("sp", self.vw1[:, c0:c0 + n, 0:128], a["vw"][c0 * 128:(c0 + n) * 128, :].rearrange("(c p) d -> p c d", p=128),
                   writes=[self.vw1_b])
        if STOP == "kv":
            tr.finish([]); return
        q_tm = [mk(f"q_tm{i}", [128, 2, 512], BF16) for i in range(2)]
        self.QTt, self.QT_b = mk("QTt", [128, 4, QT], BF16)
        self.pbig = [mk(f"pbig{i}", [128, 4 * QT], BF16) for i in range(2)]
        o_acc, o_accb = mk("o_acc", [128, 2, 512], F32)
        o_bf = [mk(f"o_bf{i}", [128, 2, 512], BF16) for i in range(2)]
        imp = [mk(f"imp{i}", [128, 256], F32) for i in range(2)]
        score = [mk(f"score{i}", [128, 256], F32) for i in range(2)]
        selb = [mk(f"selb{i}", [128, 256], BF16) for i in range(2)]
        selT, selT_b = mk("selT", [128, 2, QT], BF16)
        gts = [mk(f"gts{i}", [128, 2, 12], F32) for i in range(2)]
        mslot = []
        for i in range(2):
            pt, pbk = self.ps[7]
            mslot.append((pt[:, i * 256:(i + 1) * 256], self._alias(pbk)))
        pb7s = [self.ps[7][1], mslot[0][1], mslot[1][1]]
        msk = [mk(f"msk{i}", [128, QT], BF16) for i in range(2)]
        acc8 = [[None, None] for _ in range(4)]
        for h in range(4):
            for sub in range(2):
                idx = h * 2 + sub
                pt, pbk = self.ps[4 + idx // 3]
                acc8[h][sub] = (pt[:, (idx % 3) * 129:(idx % 3) * 129 + 129], pbk, 4 + idx // 3)
        out_evs = []
        I0 = int(os.environ.get("NSA_I0", "0"))
        for i in range(I0, self.nq):
            t0 = i * QT
            qt_, qtb = q_tm[i % 2]
            tr.dma("sp", qt_[:], a["q"][t0:t0 + QT, :].rearrange("(s p) c -> p s c", p=128), writes=[qtb])
            g, gb = gts[i % 2]
            tr.dma("sp", g[:], a["gates"][t0:t0 + QT, :].rearrange("(s p) c -> p s c", p=128), writes=[gb])
            pt, pb = self.ps[7]
            pv = pt[:].bitcast(BF16)
            for sub in range(2):
                for h in range(4):
                    tr.op("pe", lambda e: e.transpose(out=pv[:, (sub * 4 + h) * 128:(sub * 4 + h + 1) * 128],
                                                      in_=qt_[:, sub, h * 128:(h + 1) * 128], identity=self.ident[:]),
                          reads=[qtb, self.ident_b], writes=pb7s)
            for sub in range(2):
                tr.op("act" if sub == 0 else "dve",
                      (lambda e: e.copy(out=self.QTt[:, :, sub * 128:(sub + 1) * 128],
                                        in_=pv[:, sub * 512:(sub + 1) * 512].rearrange("p (h q) -> p h q", h=4))) if sub == 0 else
                      (lambda e: e.tensor_copy(out=self.QTt[:, :, sub * 128:(sub + 1) * 128],
                                               in_=pv[:, sub * 512:(sub + 1) * 512].rearrange("p (h q) -> p h q", h=4))),
                      reads=pb7s, writes=[self.QT_b])
            jmax = min((16 * i + 14) // 128, 7)
            for hp in range(2):
                accs = {}
                cbanks = [3, 4, 5, 6]
                for hh in range(2):
                    h = hp * 2 + hh
                    accs[h] = []
                    for sub in range(2):
                        bk = cbanks[hh * 2 + sub]
                        pt_, pb_ = self.ps[bk]
                        accs[h].append((pt_, pb_, bk))
                batches = []
                for j in range(jmax + 1):
                    r = i - 8 * j
                    masks = [(self.masks[:, 4 + r, :], [self.masks_b])] if r <= 8 else []
                    batches.append(dict(units=[(self.kcmpT[:, j * 128:(j + 1) * 128], [self.kcmpT_b], hp * 2 + hh) for hh in range(2)],
                                        v=self.vcmp1[:, j, :], vb=[self.vcmp1_b], masks=masks))
                self.nsbuf = 1
                self.run_phase(batches, accs)
                self.nsbuf = 2
                sm, smb = self.small[1]
                for hh in range(2):
                    for sub in range(2):
                        acc, accb, _ = accs[hp * 2 + hh][sub]
                        k = hh * 2 + sub
                        tr.op("dve", lambda e: e.tensor_scalar(out=sm[:, k:k + 1], in0=acc[:, 128:129], scalar1=1e-30, scalar2=1.0,
                                                               op0=ALU.max, op1=ALU.mult), reads=[accb], writes=[smb])
                tr.op("dve", lambda e: e.reciprocal(out=sm[:, 8:12], in_=sm[:, 0:4]), reads=[smb], writes=[smb])
                tr.op("dve", lambda e: e.tensor_tensor(out=sm[:, 16:20].rearrange("p (h s) -> p h s", s=2),
                                                       in0=sm[:, 8:12].rearrange("p (h s) -> p h s", s=2),
                                                       in1=g[:, :, hp * 2:hp * 2 + 2].rearrange("p s h -> p h s"), op=ALU.mult),
                      reads=[smb, gb], writes=[smb])
                for hh in range(2):
                    h = hp * 2 + hh
                    for sub in range(2):
                        acc, accb, _ = accs[h][sub]
                        k = hh * 2 + sub
                        tr.op("dve", lambda e: e.tensor_scalar(out=o_acc[:, sub, h * 128:(h + 1) * 128], in0=acc[:, 0:128],
                                                               scalar1=sm[:, 16 + k:17 + k], scalar2=1.0, op0=ALU.mult, op1=ALU.mult),
                              reads=[accb, smb], writes=[o_accb])
                        im, imb = imp[sub]
                        if h == 0:
                            tr.op("dve", lambda e: e.tensor_scalar(out=im[:], in0=acc[:, 129:385], scalar1=sm[:, 8 + k:9 + k], scalar2=1.0,
                                                                   op0=ALU.mult, op1=ALU.mult), reads=[accb, smb], writes=[imb])
                        else:
                            tr.op("dve", lambda e: e.scalar_tensor_tensor(out=im[:], in0=acc[:, 129:385], scalar=sm[:, 8 + k:9 + k],
                                                                          in1=im[:], op0=ALU.mult, op1=ALU.add),
                                  reads=[accb, smb, imb], writes=[imb])
            if STOP == "cmp":
                continue
            for sub in range(2):
                qt128 = 2 * i + sub
                im, imb = imp[sub]
                sc, scb = score[sub]
                off = 256 - 2 * qt128
                tr.op("dve", lambda e: e.tensor_tensor(out=sc[:], in0=im[:], in1=self.Fbase[:, off:off + 256], op=ALU.add),
                      reads=[imb, self.Fbase_b], writes=[scb])
                tr.op("dve", lambda e: e.memset(sc[:, 0:1], 1e9), writes=[scb], reads=[scb])
                m8, m8b = self.small[2 + sub]
                tr.op("dve", lambda e: e.max(out=m8[:, 0:8], in_=sc[:]), reads=[scb], writes=[m8b])
                tr.op("dve", lambda e: e.match_replace(out=im[:], in_to_replace=m8[:, 0:8], in_values=sc[:], imm_value=-3e38),
                      reads=[scb, m8b], writes=[imb])
                tr.op("dve", lambda e: e.max(out=m8[:, 8:16], in_=im[:]), reads=[imb], writes=[m8b])
                sb_, sbb = selb[sub]
                tr.op("dve", lambda e: e.tensor_scalar(out=sb_[:], in0=sc[:], scalar1=m8[:, 15:16], scalar2=1.0, op0=ALU.is_ge, op1=ALU.mult),
                      reads=[scb, m8b], writes=[sbb])
                pt, pb = self.ps[7]
                pv = pt[:].bitcast(BF16)
                for sc_ in range(2):
                    tr.op("pe", lambda e: e.transpose(out=pv[:, sc_ * 128:(sc_ + 1) * 128], in_=sb_[:, sc_ * 128:(sc_ + 1) * 128],
                                                      identity=self.ident[:]), reads=[sbb, self.ident_b], writes=pb7s)
                tr.op("dve", lambda e: e.tensor_scalar(out=selT[:, :, sub * 128:(sub + 1) * 128],
                                                       in0=pv[:, 0:256].rearrange("p (c q) -> p c q", c=2),
                                                       scalar1=30000.0, scalar2=-30000.0, op0=ALU.mult, op1=ALU.add),
                      reads=pb7s, writes=[selT_b])
            if STOP == "topk":
                continue
            accs = {h: acc8[h] for h in range(4)}
            batches = []
            for kc in range(0, 2 * i + 2):
                r = kc - 2 * i
                masks = [(self.masks[:, r, :], [self.masks_b])] if r >= 0 else []
                batches.append(dict(units=[(ksT[:, kc * 128:(kc + 1) * 128], [ksT_b], h) for h in range(4)],
                                    v=self.vs1[:, kc, :], vb=[self.vs1_b], masks=masks,
                                    neg=(self.Rtab[:, kc % 64, :], selT[:, kc // 64, :], [self.Rtab_b, selT_b])))
            self.run_phase(batches, accs)
            self.evac_branch(acc8, g, gb, 1, o_acc, o_accb)
            if STOP == "sel":
                continue
            batches = []
            for kc in range(max(0, 2 * i - 4), 2 * i + 2):
                r = kc - 2 * i
                mi = {-4: 2, -3: 3, 0: 0, 1: 1}.get(r)
                masks = [(self.masks[:, mi, :], [self.masks_b])] if mi is not None else []
                batches.append(dict(units=[(kwT[:, kc * 128:(kc + 1) * 128], [kwT_b], h) for h in range(4)],
                                    v=self.vw1[:, kc, :], vb=[self.vw1_b], masks=masks))
            self.run_phase(batches, accs)
            if STOP == "win":
                continue
            self.evac_branch(acc8, g, gb, 2, o_acc, o_accb)
            if STOP == "winevac":
                continue
            ob, obb = o_bf[i % 2]
            tr.op("dve", lambda e: e.tensor_copy(out=ob[:], in_=o_acc[:]), reads=[o_accb], writes=[obb])
            for sub in range(2):
                out_evs.append(tr.dma(os.environ.get("NSA_OUTQ", "sp"), o_out[t0 + sub * 128:t0 + (sub + 1) * 128, :], ob[:, sub, :], reads=[obb]))
        tr.finish(out_evs)

    def evac_branch(self, acc8, g, gb, br, o_acc, o_accb):
        tr = self.tr
        sm, smb = self.small[0]
        for bi, (bank, n) in enumerate(((4, 3), (5, 3), (6, 2))):
            pt, pbk = self.ps[bank]
            den = pt[:, 0:n * 129].rearrange("p (a c) -> p a c", c=129)[:, :, 128]
            tr.op("dve", lambda e: e.tensor_scalar(out=sm[:, bi * 3:bi * 3 + n], in0=den, scalar1=1e-30, scalar2=1.0,
                                                   op0=ALU.max, op1=ALU.mult), reads=[pbk], writes=[smb])
        tr.op("dve", lambda e: e.reciprocal(out=sm[:, 8:16], in_=sm[:, 0:8]), reads=[smb], writes=[smb])
        tr.op("dve", lambda e: e.tensor_tensor(out=sm[:, 16:24].rearrange("p (h s) -> p h s", s=2),
                                               in0=sm[:, 8:16].rearrange("p (h s) -> p h s", s=2),
                                               in1=g[:, :, br * 4:br * 4 + 4].rearrange("p s h -> p h s"), op=ALU.mult),
              reads=[smb, gb], writes=[smb])
        for h in range(4):
            for sub in range(2):
                acc, accb, _ = acc8[h][sub]
                idx = h * 2 + sub
                o = o_acc[:, sub, h * 128:(h + 1) * 128]
                tr.op("dve", lambda e: e.scalar_tensor_tensor(out=o, in0=acc[:, 0:128], scalar=sm[:, 16 + idx:17 + idx], in1=o,
                                                              op0=ALU.mult, op1=ALU.add), reads=[accb, smb, o_accb], writes=[o_accb])


_CACHE = {}


def _prog(key, fn):
    if key not in _CACHE:
        _CACHE[key] = fn()
    return _CACHE[key]


def kernel(x, p, norm_mix, norm_ffn, norm_ple, ffn_up, ffn_down, ple_proj, ple_gate,
           gm_in, gm_ln_g, gm_ln_b, gm_ws, gm_bs, gm_out,
           nsa_in, nsa_kc_pe, nsa_kc_w1, nsa_kc_w2, nsa_vc_pe, nsa_vc_w1, nsa_vc_w2, nsa_out, final_norm):
    f32 = np.float32
    A = lambda v: np.ascontiguousarray(np.asarray(v, dtype=f32))
    x = A(x).reshape(B * T, D)
    p = A(p).reshape(2, B * T, 256)
    norm_mix, norm_ffn, norm_ple = A(norm_mix), A(norm_ffn), A(norm_ple)
    ident = np.eye(128, dtype=f32).astype(NPBF)
    NTOK = B * T // NCORES
    pos = np.arange(T, dtype=f32)
    inv = np.power(f32(500000.0), -np.arange(16, dtype=f32) * f32(2.0) / f32(32))
    ang = pos[:, None] * inv[None, :]
    rope = np.concatenate([np.cos(ang), np.sin(ang)], axis=1).astype(f32)
    rope = np.concatenate([rope, rope], axis=0)
    cores = list(range(NCORES))
    nc1 = Dense(1).build()
    common1 = dict(ident=ident, norm_mix=norm_mix, norm_ffn=norm_ffn, norm_ple=norm_ple,
                   ffn_up1=A(ffn_up[0]), ffn_down1=A(ffn_down[0]), ple_proj1=A(ple_proj[0]), ple_gate1=A(ple_gate[0]),
                   gm_in=A(gm_in[0]), gm_out=A(gm_out[0]), gm_ln_g=A(gm_ln_g).reshape(1, D), gm_ln_b=A(gm_ln_b).reshape(1, D),
                   wsT=np.ascontiguousarray(A(gm_ws[0]).transpose(2, 0, 1)), cmask=np.triu(np.ones((128, 128), f32)),
                   bsT=A(gm_bs[0]).reshape(1, 2048), nsa_in=A(nsa_in[0]))
    maps = []
    for c in cores:
        sl = slice(c * NTOK, (c + 1) * NTOK)
        maps.append(dict(common1, x=x[sl], p=p[0, sl], rope=rope[sl]))
    r1 = run_bass_kernel_spmd(nc1, maps, core_ids=cores).results
    x1 = np.concatenate([r["x_out"] for r in r1], axis=0)
    qkv = np.concatenate([r["qkv_out"] for r in r1], axis=0)
    gates = np.concatenate([r["gates_out"] for r in r1], axis=0)
    del r1, maps
    nc2 = NSA().build()
    C = nsa_consts()
    common2 = dict(C, kc_w1=A(nsa_kc_w1[0]), vc_w1=A(nsa_vc_w1[0]), kc_w2=A(nsa_kc_w2[0]), vc_w2=A(nsa_vc_w2[0]),
                   kc_peT=np.ascontiguousarray(A(nsa_kc_pe[0]).T), vc_peT=np.ascontiguousarray(A(nsa_vc_pe[0]).T))
    maps = []
    for c in cores:
        b, g = c // 4, c % 4
        rows = slice(b * T, (b + 1) * T)
        m = dict(common2)
        m["q"] = np.ascontiguousarray(qkv[rows, g * 512:(g + 1) * 512])
        for i, nm in enumerate(("kc", "vc", "ks", "vs", "kw", "vw")):
            m[nm] = np.ascontiguousarray(qkv[rows, 2048 + i * 512 + g * 128:2048 + i * 512 + (g + 1) * 128])
        m["gates"] = np.ascontiguousarray(np.concatenate([gates[rows, br * 16 + g * 4:br * 16 + g * 4 + 4] for br in range(3)], axis=1))
        maps.append(m)
    r2 = run_bass_kernel_spmd(nc2, maps, core_ids=cores).results
    o = np.empty((B * T, D), dtype=NPBF)
    for c in cores:
        b, g = c // 4, c % 4
        o[b * T:(b + 1) * T, g * 512:(g + 1) * 512] = r2[c]["o_out"]
    del r2, maps, qkv
    nc3 = Dense(3).build()
    common3 = dict(ident=ident, norm_mix=norm_mix, norm_ffn=norm_ffn, norm_ple=norm_ple,
                   ffn_up3=A(ffn_up[1]), ffn_down3=A(ffn_down[1]), ple_proj3=A(ple_proj[1]), ple_gate3=A(ple_gate[1]),
                   nsa_out=A(nsa_out[0]), final_norm=A(final_norm).reshape(1, D))
    maps = []
    for c in cores:
        sl = slice(c * NTOK, (c + 1) * NTOK)
        maps.append(dict(common3, x=x1[sl], p=p[1, sl], o_in=np.ascontiguousarray(o[sl])))
    r3 = run_bass_kernel_spmd(nc3, maps, core_ids=cores).results
    out = np.concatenate([r["x_out"] for r in r3], axis=0).reshape(B, T, D).astype(f32)
    return out
```

```python
import bisect
import contextlib
import numpy as np
import ml_dtypes
import concourse.bass as bass
import concourse.mybir as mybir
from concourse.bass_utils import run_bass_kernel_spmd

F32 = mybir.dt.float32
BF16 = mybir.dt.bfloat16
AF = mybir.ActivationFunctionType
ALU = mybir.AluOpType
NPBF = ml_dtypes.bfloat16

D = 2048
DFF = 8192
T = 16384
B = 2
NCORES = 8
EPS = 1e-6
SEM_LIMIT = 1000


class Buf:
    __slots__ = ("writer", "readers", "const")

    def __init__(self):
        self.writer = None
        self.readers = {}
        self.const = False


class Tracker:
    def __init__(self, nc, es):
        self.nc = nc
        self.es = es
        self.engs = {"pe": nc.tensor, "act": nc.scalar, "dve": nc.vector, "pool": nc.gpsimd, "sp": nc.sync}
        self.sems = {k: [] for k in self.engs}
        self.seq = {k: 0 for k in self.engs}
        self.last = {k: None for k in self.engs}
        self.sig_seqs = {k: [] for k in self.engs}
        self.known = {}
        self.dma_sems = {}
        self.dma_uses = {}
        self.dma_rr = {}
        for q, n in (("sp", 20), ("pool", 12), ("act", 6)):
            self.dma_sems[q] = [es.enter_context(nc.semaphore(f"dq_{q}_{i}")) for i in range(n)]
            self.dma_uses[q] = [0] * n
            self.dma_rr[q] = 0
        self.nwaits = 0
        import os
        self.dummy = [es.enter_context(nc.semaphore(f"dummy{i}")) for i in range(int(os.environ.get("DUMMY_SEMS", "0")))]

    def _eng_sem(self, e, epoch):
        while len(self.sems[e]) <= epoch:
            self.sems[e].append(self.es.enter_context(self.nc.semaphore(f"es_{e}_{len(self.sems[e])}")))
        return self.sems[e][epoch]

    def _wait(self, waiter, ev):
        if ev is None:
            return
        if ev[0] == "dma":
            _, q, si, val = ev
            key = (waiter, "dma", q, si)
            if self.known.get(key, 0) >= val:
                return
            self.engs[waiter].wait_ge(self.dma_sems[q][si], val)
            self.known[key] = val
            self.nwaits += 1
            return
        e, seq = ev
        sigs = self.sig_seqs[e]
        i = bisect.bisect_left(sigs, seq)
        if i == len(sigs):
            lseq, lins = self.last[e]
            assert lseq >= seq
            k = len(sigs)
            lins.then_inc(self._eng_sem(e, k // SEM_LIMIT), 1)
            sigs.append(lseq)
        k = i
        epoch, val = k // SEM_LIMIT, k % SEM_LIMIT + 1
        key = (waiter, e, epoch)
        if self.known.get(key, 0) >= val:
            return
        self.engs[waiter].wait_ge(self._eng_sem(e, epoch), val)
        self.known[key] = val
        for ep in range(epoch):
            self.known[(waiter, e, ep)] = SEM_LIMIT
        self.nwaits += 1

    def _deps(self, eng, reads, writes):
        deps = {}

        def add(ev, is_write_dep):
            if ev is None:
                return
            if ev[0] == "dma":
                deps[ev] = ev
            else:
                e, seq = ev
                if e == "pe" and eng == "pe" and is_write_dep:
                    return
                if deps.get(e, (e, 0))[1] < seq:
                    deps[e] = ev

        for b in reads:
            add(b.writer, False)
        for b in writes:
            add(b.writer, True)
            for r in b.readers.values():
                if not (r[0] == eng and eng == "pe" and False):
                    add(r, False)
        return list(deps.values())

    def _record(self, ev, reads, writes):
        for b in reads:
            if b.const:
                continue
            if ev[0] == "dma":
                b.readers[ev] = ev
            else:
                b.readers[ev[0]] = ev
        for b in writes:
            b.writer = ev
            b.readers = {}

    def op(self, eng, fn, reads=(), writes=()):
        for d in self._deps(eng, reads, writes):
            if d[0] == eng and eng == "pe":
                continue
            if False and d[0] == eng and self.seq[eng] - d[1] >= 3:
                continue
            self._wait(eng, d)
        ins = fn(self.engs[eng])
        self.seq[eng] += 1
        ev = (eng, self.seq[eng])
        self.last[eng] = (self.seq[eng], ins)
        self._record(ev, reads, writes)
        return ins

    def dma(self, q, out, in_, reads=(), writes=(), **kw):
        for d in self._deps(q, reads, writes):
            self._wait(q, d)
        n = len(self.dma_sems[q])
        si = self.dma_rr[q]
        self.dma_rr[q] = (si + 1) % n
        uses = self.dma_uses[q][si]
        if uses > 0:
            self._wait(q, ("dma", q, si, 16 * uses))
        ins = self.engs[q].dma_start(out=out, in_=in_, **kw)
        ins.then_inc(self.dma_sems[q][si], 16)
        self.dma_uses[q][si] = uses + 1
        ev = ("dma", q, si, 16 * (uses + 1))
        self._record(ev, reads, writes)
        return ev

    def finish(self, out_events):
        for ev in out_events:
            self._wait("sp", ev)


TT = 512
NSUB = 4
NTILE = 4096 // TT
WSLOT = 8192


class Dense:
    def __init__(self, mode, ntiles=NTILE):
        self.mode = mode
        self.ntiles = ntiles
        nc = self.nc = bass.Bass("TRN2", target_bir_lowering=False)
        self.es = contextlib.ExitStack()

    def dram_in(self, name, shape, dt):
        return self.nc.dram_tensor(name, list(shape), dt, kind="ExternalInput").ap()

    def dram_out(self, name, shape, dt):
        return self.nc.dram_tensor(name, list(shape), dt, kind="ExternalOutput").ap()

    def sb(self, name, shape, dt):
        return self.es.enter_context(self.nc.sbuf_tensor(name, list(shape), dt))

    def build(self):
        nc, es = self.nc, self.es
        with es:
            self.tr = Tracker(nc, es)
            self._build()
        return nc

    def wpiece(self, wap, r0, nk, c0, ncols):
        assert nk * ncols <= WSLOT
        i = self.wrr
        self.wrr = (i + 1) % len(self.wring)
        t, b = self.wring[i]
        dst = t[:, 0:nk * ncols].rearrange("p (k n) -> p k n", k=nk)
        src = wap[r0 * 128:(r0 + nk) * 128, c0:c0 + ncols].rearrange("(k p) n -> p k n", p=128)
        q = "pool"
        self.wq += 1
        self.tr.dma(q, dst, src, writes=[b])
        return dst, b

    def bcast_load(self, vec_ap):
        i = self.grr
        self.grr = (i + 1) % len(self.gbc)
        t, b = self.gbc[i]
        self.tr.dma("sp", t[:], vec_ap.partition_broadcast(128), writes=[b])
        return t, b

    def next_ps(self):
        i = self.prr
        self.prr = (i + 1) % 8
        return self.ps[i]

    def rmsnorm_T(self, gvec_ap):
        tr = self.tr
        g_t, g_b = self.bcast_load(gvec_ap)
        for s in range(NSUB):
            xs = self.xres[:, s, :]
            xb = self.xres_b[s]
            junk, jb = self.xn_tm[s % 2]
            ss, ssb = self.small[s % 2]
            tr.op("act", lambda e: e.activation(out=junk[:], in_=xs, func=AF.Square, accum_out=ss[:, 0:1]),
                  reads=[xb], writes=[jb, ssb])
            tr.op("dve", lambda e: e.tensor_scalar(out=ss[:, 1:2], in0=ss[:, 0:1], scalar1=1.0 / D, scalar2=EPS,
                                                   op0=ALU.mult, op1=ALU.add), reads=[ssb], writes=[ssb])
            tr.op("act", lambda e: e.activation(out=ss[:, 2:3], in_=ss[:, 1:2], func=AF.Sqrt), reads=[ssb], writes=[ssb])
            tr.op("dve", lambda e: e.reciprocal(out=ss[:, 3:4], in_=ss[:, 2:3]), reads=[ssb], writes=[ssb])
            tr.op("dve", lambda e: e.scalar_tensor_tensor(out=junk[:], in0=xs, scalar=ss[:, 3:4], in1=g_t[:],
                                                          op0=ALU.mult, op1=ALU.mult),
                  reads=[xb, ssb, g_b], writes=[jb])
            self.transpose_into(junk, jb, 16, self.xnT, self.xnT_b, s)

    def transpose_into(self, src, srcb, nk, dstT, dstb, s):
        tr = self.tr
        for k0 in range(0, nk, 8):
            n = min(8, nk - k0)
            pt, pb = self.next_ps()
            pv = pt[:].bitcast(BF16)
            for j in range(n):
                k = k0 + j
                tr.op("pe", lambda e: e.transpose(out=pv[:, j * 128:(j + 1) * 128], in_=src[:, k * 128:(k + 1) * 128],
                                                  identity=self.ident[:]),
                      reads=[srcb, self.ident_b], writes=[pb])
            eng = "act" if (k0 // 8) % 2 == 0 else "dve"
            o = dstT[:, k0:k0 + n, s * 128:(s + 1) * 128]
            i = pv[:, 0:n * 128].rearrange("p (k t) -> p k t", k=n)
            if eng == "act":
                tr.op("act", lambda e: e.copy(out=o, in_=i), reads=[pb], writes=[dstb])
            else:
                tr.op("dve", lambda e: e.tensor_copy(out=o, in_=i), reads=[pb], writes=[dstb])

    def proj_fm(self, wap, r0, nk, c0, ncols, srcT, srcb, evac):
        tr = self.tr
        pcols = WSLOT // nk
        for p0 in range(0, ncols, pcols):
            pc = min(pcols, ncols - p0)
            w, wb = self.wpiece(wap, r0, nk, c0 + p0, pc)
            for f in range(pc // 128):
                pt, pb = self.next_ps()
                for k in range(nk):
                    tr.op("pe", lambda e: e.matmul(pt[:], lhsT=w[:, k, f * 128:(f + 1) * 128], rhs=srcT[:, k, :],
                                                   start=(k == 0), stop=(k == nk - 1)),
                          reads=[wb] + list(srcb), writes=[pb])
                evac((p0 // 128) + f, pt, pb)

    def proj_tm(self, wap, r0, nk, c0, ncols, srcT, srcb, evac, blk=512):
        tr = self.tr
        kper = max(1, WSLOT // blk)
        half = 0
        for cb, cc in enumerate(range(0, ncols, blk)):
            w_ = min(blk, ncols - cc)
            pss = [self.ps[(half * 4 + s)] for s in range(NSUB)]
            half ^= 1
            for k0 in range(0, nk, kper):
                kn = min(kper, nk - k0)
                w, wb = self.wpiece(wap, r0 + k0, kn, c0 + cc, w_)
                for kk in range(kn):
                    k = k0 + kk
                    for s in range(NSUB):
                        pt, pb = pss[s]
                        tr.op("pe", lambda e: e.matmul(pt[:, 0:w_], lhsT=srcT[:, k, s * 128:(s + 1) * 128], rhs=w[:, kk, :],
                                                       start=(k == 0), stop=(k == nk - 1)),
                              reads=[wb] + list(srcb), writes=[pb])
            for s in range(NSUB):
                pt, pb = pss[s]
                evac(cb, s, pt, pb, w_)

    def resid_add(self, cb, s, pt, pb, w_):
        xs = self.xres[:, s, cb * 512:cb * 512 + w_]
        self.tr.op("dve", lambda e: e.tensor_tensor(out=xs, in0=xs, in1=pt[:, 0:w_], op=ALU.add),
                   reads=[pb, self.xres_b[s]], writes=[self.xres_b[s]])

    def ffn(self, li):
        tr = self.tr
        self.rmsnorm_T(self.a["norm_ffn"][li:li + 1, :])
        for h in range(2):
            def evac_up(f, pt, pb):
                tmp, tb = self.tmpf[f % 2]
                tr.op("act", lambda e: e.activation(out=tmp[:], in_=pt[:], func=AF.Relu), reads=[pb], writes=[tb])
                eng = "dve"
                tr.op(eng, lambda e: e.tensor_tensor(out=self.hT[:, f, :], in0=tmp[:], in1=tmp[:], op=ALU.mult),
                      reads=[tb], writes=[self.hT_b, self.hT_b2])
            self.proj_fm(self.a["ffn_up"][li], 0, 16, h * 4096, 4096, self.xnT, [self.xnT_b], evac_up)
            self.proj_tm(self.a["ffn_down"][li], h * 32, 32, 0, 2048, self.hT, [self.hT_b, self.hT_b2], self.resid_add)

    def ple(self, li, tok0):
        tr = self.tr
        self.rmsnorm_T(self.a["norm_ple"][li:li + 1, :])
        pf, pfb = self.p_f
        tr.dma("sp", pf[:], self.a["p"][tok0:tok0 + TT, :].rearrange("(s p) c -> p s c", p=128), writes=[pfb])
        for s in range(NSUB):
            pbf, pbb = self.p_bf[s % 2]
            tr.op("dve", lambda e: e.tensor_copy(out=pbf[:], in_=pf[:, s, :]), reads=[pfb], writes=[pbb])
            self.transpose_into(pbf, pbb, 2, self.pT, self.pT_b, s)
        for cb in range(4):
            pg = [self.ps[s] for s in range(4)]
            pp = [self.ps[4 + s] for s in range(4)]
            w, wb = self.wpiece(self.a["ple_gate"][li], 0, 16, cb * 512, 512)
            wple, wpb = self.wpiece(self.a["ple_proj"][li], 0, 2, cb * 512, 512)
            for k in range(16):
                for s in range(NSUB):
                    pt, pb = pg[s]
                    tr.op("pe", lambda e: e.matmul(pt[:], lhsT=self.xnT[:, k, s * 128:(s + 1) * 128], rhs=w[:, k, :],
                                                   start=(k == 0), stop=(k == 15)), reads=[wb, self.xnT_b], writes=[pb])
            for k in range(2):
                for s in range(NSUB):
                    pt, pb = pp[s]
                    tr.op("pe", lambda e: e.matmul(pt[:], lhsT=self.pT[:, k, s * 128:(s + 1) * 128],
                                                   rhs=wple[:, k, :],
                                                   start=(k == 0), stop=(k == 1)), reads=[wpb, self.pT_b], writes=[pb])
            for s in range(NSUB):
                tmp, tb = self.tmpf[s % 2]
                tr.op("act", lambda e: e.activation(out=tmp[:], in_=pg[s][0][:], func=AF.Sigmoid), reads=[pg[s][1]], writes=[tb])
                tr.op("dve", lambda e: e.tensor_tensor(out=tmp[:], in0=tmp[:], in1=pp[s][0][:], op=ALU.mult),
                      reads=[tb, pp[s][1]], writes=[tb])
                xs = self.xres[:, s, cb * 512:(cb + 1) * 512]
                tr.op("dve", lambda e: e.tensor_tensor(out=xs, in0=xs, in1=tmp[:], op=ALU.add),
                      reads=[tb, self.xres_b[s]], writes=[self.xres_b[s]])

    def gmlp(self):
        tr = self.tr
        a = self.a
        self.rmsnorm_T(a["norm_mix"][0:1, :])
        tr.dma("sp", self.hT[:, 16:24, :].rearrange("p k t -> p (k t)").bitcast(F32), a["bsT"].partition_broadcast(128),
               writes=[self.hT_b2])

        def evac_u(f, pt, pb):
            tr.op("act", lambda e: e.activation(out=self.hT[:, f, :], in_=pt[:], func=AF.Gelu_apprx_tanh),
                  reads=[pb], writes=[self.hT_b])
        self.proj_fm(a["gm_in"], 0, 16, 0, 2048, self.xnT, [self.xnT_b], evac_u)

        def evac_v(cb, s, pt, pb, w_):
            vs = self.v_f[:, s, cb * 512:(cb + 1) * 512]
            tr.op("act", lambda e: e.activation(out=vs, in_=pt[:], func=AF.Gelu_apprx_tanh), reads=[pb], writes=[self.v_fb[s]])
            tr.op("dve", lambda e: e.bn_stats(out=self.stats[:, s, cb * 6:(cb + 1) * 6], in_=vs), reads=[self.v_fb[s]],
                  writes=[self.stats_b[s]])
        self.proj_tm(a["gm_in"], 0, 16, 2048, 2048, self.xnT, [self.xnT_b], evac_v)
        lg_t, lg_b = self.bcast_load(a["gm_ln_g"])
        lb_t, lb_b = self.bcast_load(a["gm_ln_b"])
        for s in range(NSUB):
            mv, mvb = self.small[s % 2]
            tr.op("dve", lambda e: e.bn_aggr(out=mv[:, 0:2], in_=self.stats[:, s, :]), reads=[self.stats_b[s]], writes=[mvb])
            tr.op("dve", lambda e: e.tensor_scalar(out=mv[:, 2:3], in0=mv[:, 1:2], scalar1=EPS, scalar2=1.0, op0=ALU.add, op1=ALU.mult),
                  reads=[mvb], writes=[mvb])
            tr.op("act", lambda e: e.activation(out=mv[:, 3:4], in_=mv[:, 2:3], func=AF.Sqrt), reads=[mvb], writes=[mvb])
            tr.op("dve", lambda e: e.reciprocal(out=mv[:, 4:5], in_=mv[:, 3:4]), reads=[mvb], writes=[mvb])
            vs = self.v_f[:, s, :]
            tr.op("dve", lambda e: e.tensor_scalar(out=vs, in0=vs, scalar1=mv[:, 0:1], scalar2=mv[:, 4:5],
                                                   op0=ALU.subtract, op1=ALU.mult), reads=[mvb, self.v_fb[s]], writes=[self.v_fb[s]])
            tr.op("dve", lambda e: e.tensor_tensor(out=vs, in0=vs, in1=lg_t[:], op=ALU.mult), reads=[self.v_fb[s], lg_b],
                  writes=[self.v_fb[s]])
            vl, vlb = self.xn_tm[s % 2]
            tr.op("dve", lambda e: e.tensor_tensor(out=vl[:], in0=vs, in1=lb_t[:], op=ALU.add), reads=[self.v_fb[s], lb_b],
                  writes=[vlb])
            for g4 in range(4):
                pt, pb = self.next_ps()
                for gg in range(4):
                    g = g4 * 4 + gg
                    tr.op("pe", lambda e: e.matmul(pt[:, gg * 128:(gg + 1) * 128], lhsT=vl[:, g * 128:(g + 1) * 128],
                                                   rhs=self.wsT[:, g, :], start=True, stop=True),
                          reads=[vlb, self.wsT_b], writes=[pb])
                tmp, tb = self.tmpf[g4 % 2]
                tv = tmp[:].rearrange("p (g t) -> p g t", g=4)
                tr.op("dve", lambda e: e.tensor_tensor(out=tv, in0=pt[:].rearrange("p (g t) -> p g t", g=4),
                                                       in1=self.bsT[:, g4 * 4:(g4 + 1) * 4, :], op=ALU.add),
                      reads=[pb, self.bsT_b], writes=[tb])
                u = self.hT[:, g4 * 4:(g4 + 1) * 4, s * 128:(s + 1) * 128]
                tr.op("dve", lambda e: e.tensor_tensor(out=u, in0=tv, in1=u, op=ALU.mult), reads=[tb, self.hT_b],
                      writes=[self.hT_b])
        mT = self.hT[:, 0:16, :]
        self.proj_tm(a["gm_out"], 0, 16, 0, 2048, mT, [self.hT_b], self.resid_add)

    def nsa_proj(self, tok0, pos0):
        tr = self.tr
        a = self.a
        self.rmsnorm_T(a["norm_mix"][1:2, :])
        cs, csb = self.cs
        tr.dma("sp", cs[:], a["rope"][tok0:tok0 + TT, :].rearrange("(s p) c -> p s c", p=128), writes=[csb])
        out_evs = self.out_evs

        def evac(cb, s, pt, pb, w_):
            if cb == 10:
                g, gb = self.small[2 + s % 2]
                tr.op("act", lambda e: e.activation(out=g[:, 0:48], in_=pt[:, 0:48], func=AF.Sigmoid), reads=[pb], writes=[gb])
                out_evs.append(tr.dma("sp", a["gates_out"][tok0 + s * 128:tok0 + (s + 1) * 128, :], g[:, 0:48], reads=[gb]))
                return
            tmp, tb = self.tmpf[s % 2]
            is_q = cb < 4
            rope = is_q or cb in (6, 8)
            ob, obb = self.obf[(cb * 4 + s) % 2]
            if not rope:
                tr.op("act", lambda e: e.copy(out=ob[:], in_=pt[:]), reads=[pb], writes=[obb])
            else:
                sc = (128.0 ** -0.5) if is_q else 1.0
                tr.op("act", lambda e: e.mul(out=tmp[:], in_=pt[:], mul=sc), reads=[pb], writes=[tb])
                v = tmp[:].rearrange("p (h d) -> p h d", h=4)
                x1, x2 = v[:, :, 0:16], v[:, :, 16:32]
                cos = cs[:, s, 0:16].unsqueeze(1).to_broadcast([128, 4, 16])
                sin = cs[:, s, 16:32].unsqueeze(1).to_broadcast([128, 4, 16])
                r, rb = self.ropet[s % 2]
                rv = r[:].rearrange("p (j h d) -> p j h d", j=4, h=4)
                tr.op("dve", lambda e: e.tensor_tensor(out=rv[:, 0], in0=x1, in1=cos, op=ALU.mult), reads=[tb, csb], writes=[rb])
                tr.op("dve", lambda e: e.tensor_tensor(out=rv[:, 1], in0=x2, in1=sin, op=ALU.mult), reads=[tb, csb], writes=[rb])
                tr.op("dve", lambda e: e.tensor_tensor(out=rv[:, 2], in0=x2, in1=cos, op=ALU.mult), reads=[tb, csb], writes=[rb])
                tr.op("dve", lambda e: e.tensor_tensor(out=rv[:, 3], in0=x1, in1=sin, op=ALU.mult), reads=[tb, csb], writes=[rb])
                tr.op("dve", lambda e: e.tensor_tensor(out=x1, in0=rv[:, 0], in1=rv[:, 1], op=ALU.subtract), reads=[rb, tb], writes=[tb])
                tr.op("dve", lambda e: e.tensor_tensor(out=x2, in0=rv[:, 2], in1=rv[:, 3], op=ALU.add), reads=[rb, tb], writes=[tb])
                tr.op("act", lambda e: e.copy(out=ob[:], in_=tmp[:]), reads=[tb], writes=[obb])
            out_evs.append(tr.dma("sp", a["qkv_out"][tok0 + s * 128:tok0 + (s + 1) * 128, cb * 512:(cb + 1) * 512], ob[:], reads=[obb]))
        self.proj_tm(a["nsa_in"], 0, 16, 0, 5168, self.xnT, [self.xnT_b], evac)

    def attn_out(self, tok0):
        tr = self.tr
        a = self.a
        for s in range(NSUB):
            ot, otb = self.xn_tm[s % 2]
            tr.dma("sp", ot[:], a["o_in"][tok0 + s * 128:tok0 + (s + 1) * 128, :], writes=[otb])
            self.transpose_into(ot, otb, 16, self.xnT, self.xnT_b, s)
        self.proj_tm(a["nsa_out"], 0, 16, 0, 2048, self.xnT, [self.xnT_b], self.resid_add)

    def final_norm(self, tok0):
        tr = self.tr
        g_t, g_b = self.bcast_load(self.a["final_norm"])
        for s in range(NSUB):
            xs = self.xres[:, s, :]
            xb = self.xres_b[s]
            junk, jb = self.xn_tm[s % 2]
            ss, ssb = self.small[s % 2]
            tr.op("act", lambda e: e.activation(out=junk[:], in_=xs, func=AF.Square, accum_out=ss[:, 0:1]),
                  reads=[xb], writes=[jb, ssb])
            tr.op("dve", lambda e: e.tensor_scalar(out=ss[:, 1:2], in0=ss[:, 0:1], scalar1=1.0 / D, scalar2=EPS,
                                                   op0=ALU.mult, op1=ALU.add), reads=[ssb], writes=[ssb])
            tr.op("act", lambda e: e.activation(out=ss[:, 2:3], in_=ss[:, 1:2], func=AF.Sqrt), reads=[ssb], writes=[ssb])
            tr.op("dve", lambda e: e.reciprocal(out=ss[:, 3:4], in_=ss[:, 2:3]), reads=[ssb], writes=[ssb])
            tr.op("dve", lambda e: e.scalar_tensor_tensor(out=xs, in0=xs, scalar=ss[:, 3:4], in1=g_t[:],
                                                          op0=ALU.mult, op1=ALU.mult),
                  reads=[xb, ssb, g_b], writes=[xb])
            self.out_evs.append(tr.dma("sp", self.a["x_out"][tok0 + s * 128:tok0 + (s + 1) * 128, :], xs, reads=[xb]))

    def _build(self):
        nc, tr = self.nc, self.tr
        mode = self.mode
        ntok = self.ntiles * TT
        a = self.a = {}
        a["x"] = self.dram_in("x", [ntok, D], F32)
        a["p"] = self.dram_in("p", [ntok, 256], F32)
        a["ident"] = self.dram_in("ident", [128, 128], BF16)
        for nm in ("norm_mix", "norm_ffn", "norm_ple"):
            a[nm] = self.dram_in(nm, [2, D], F32)
        a["ffn_up"] = [self.dram_in(f"ffn_up{mode}", [D, DFF], F32)] * 2
        a["ffn_down"] = [self.dram_in(f"ffn_down{mode}", [DFF, D], F32)] * 2
        a["ple_proj"] = [self.dram_in(f"ple_proj{mode}", [256, D], F32)] * 2
        a["ple_gate"] = [self.dram_in(f"ple_gate{mode}", [D, D], F32)] * 2
        li = 0 if mode == 1 else 1
        if mode == 1:
            a["gm_in"] = self.dram_in("gm_in", [D, 4096], F32)
            a["gm_out"] = self.dram_in("gm_out", [D, D], F32)
            a["gm_ln_g"] = self.dram_in("gm_ln_g", [1, D], F32)
            a["gm_ln_b"] = self.dram_in("gm_ln_b", [1, D], F32)
            a["wsT"] = self.dram_in("wsT", [128, 16, 128], F32)
            a["cmask"] = self.dram_in("cmask", [128, 128], F32)
            a["bsT"] = self.dram_in("bsT", [1, 2048], F32)
            a["nsa_in"] = self.dram_in("nsa_in", [D, 5168], F32)
            a["rope"] = self.dram_in("rope", [ntok, 32], F32)
            a["x_out"] = self.dram_out("x_out", [ntok, D], F32)
            a["qkv_out"] = self.dram_out("qkv_out", [ntok, 5120], BF16)
            a["gates_out"] = self.dram_out("gates_out", [ntok, 48], F32)
        else:
            a["o_in"] = self.dram_in("o_in", [ntok, D], BF16)
            a["nsa_out"] = self.dram_in("nsa_out", [D, D], F32)
            a["final_norm"] = self.dram_in("final_norm", [1, D], F32)
            a["x_out"] = self.dram_out("x_out", [ntok, D], F32)

        def mk(name, shape, dt):
            return self.sb(name, shape, dt), Buf()
        self.xres = self.sb("xres", [128, NSUB, D], F32)
        self.xres_b = [Buf() for _ in range(NSUB)]
        self.xn_tm = [mk(f"xn_tm{i}", [128, D], BF16) for i in range(2)]
        self.xnT = self.sb("xnT", [128, 16, TT], BF16)
        self.xnT_b = Buf()
        self.hT = self.sb("hT", [128, 32, TT], BF16)
        self.hT_b = Buf()
        self.hT_b2 = Buf()
        self.wring = [mk(f"wr{i}", [128, WSLOT], BF16) for i in range(3)]
        self.wrr = 0
        self.wq = 0
        self.gbc = [mk(f"gbc{i}", [128, D], F32) for i in range(2)]
        self.grr = 0
        self.small = [mk(f"small{i}", [128, 64], F32) for i in range(4)]
        self.tmpf = [mk(f"tmpf{i}", [128, 512], F32) for i in range(2)]
        self.ident, self.ident_b = mk("ident_sb", [128, 128], BF16)
        self.p_f = mk("p_f", [128, NSUB, 256], F32)
        self.p_bf = [mk(f"p_bf{i}", [128, 256], BF16) for i in range(2)]
        self.pT = self.sb("pT", [128, 2, TT], BF16)
        self.pT_b = Buf()
        self.ps = [(self.es.enter_context(nc.psum_tensor(f"ps{i}", [128, 512], F32)), Buf()) for i in range(8)]
        self.prr = 0
        self.out_evs = []
        tr.dma("sp", self.ident[:], a["ident"], writes=[self.ident_b])
        self.ident_b.const = True
        if mode == 1:
            self.v_f = self.sb("v_f", [128, NSUB, D], F32)
            self.v_fb = [Buf() for _ in range(NSUB)]
            self.stats = self.sb("stats", [128, NSUB, 24], F32)
            self.stats_b = [Buf() for _ in range(NSUB)]
            self.wsT, self.wsT_b = mk("wsT_sb", [128, 16, 128], BF16)
            self.bsT = self.hT[:, 16:24, :].rearrange("p k t -> p (k t)").bitcast(F32).rearrange("p (g t) -> p g t", g=16)
            self.bsT_b = self.hT_b2
            wsf0, wsfb = self.gbc[0]
            cm0, cmb = self.gbc[1]
            wsf = wsf0[:].rearrange("p (g t) -> p g t", g=16)
            cm = cm0[:, 0:128]
            tr.dma("sp", wsf, a["wsT"], writes=[wsfb])
            tr.dma("sp", cm, a["cmask"], writes=[cmb])
            tr.op("dve", lambda e: e.tensor_tensor(out=self.wsT[:], in0=wsf, in1=cm.unsqueeze(1).to_broadcast([128, 16, 128]),
                                                   op=ALU.mult), reads=[wsfb, cmb], writes=[self.wsT_b])
            self.wsT_b.const = True
            self.cs = mk("cs", [128, NSUB, 32], F32)
            self.obf = [mk(f"obf{i}", [128, 512], BF16) for i in range(2)]
            self.ropet = [mk(f"ropet{i}", [128, 256], F32) for i in range(2)]

        for ti in range(self.ntiles):
            tok0 = ti * TT
            for s in range(NSUB):
                tr.dma("sp", self.xres[:, s, :], a["x"][tok0 + s * 128:tok0 + (s + 1) * 128, :], writes=[self.xres_b[s]])
            if mode == 1:
                import os
                st = os.environ.get("STAGES", "gmlp,ffn,ple").split(",")
                if "gmlp" in st:
                    self.gmlp()
                if "ffn" in st:
                    self.ffn(0)
                if "ple" in st:
                    self.ple(0, tok0)
                for s in range(NSUB):
                    self.out_evs.append(tr.dma("sp", a["x_out"][tok0 + s * 128:tok0 + (s + 1) * 128, :], self.xres[:, s, :],
                                               reads=[self.xres_b[s]]))
                self.nsa_proj(tok0, 0)
            else:
                self.attn_out(tok0)
                self.ffn(1)
                self.ple(1, tok0)
                self.final_norm(tok0)
        tr.finish(self.out_evs)


QT = 256
NCMP = 1023


def nsa_consts():
    kl = np.arange(128)[:, None]
    ql = np.arange(QT)[None, :]
    masks = np.zeros((128, 13, QT), np.float32)
    masks[:, 0] = (kl <= ql)
    masks[:, 1] = (128 + kl <= ql)
    masks[:, 2] = (kl > ql)
    masks[:, 3] = (kl + 128 > ql)
    for r in range(9):
        masks[:, 4 + r] = (16 * kl + 31 <= 256 * r + ql)
    c = np.arange(1024)[:, None]
    s = np.arange(256)[None, :]
    ov = np.clip(np.minimum(c * 16 + 32, s * 64 + 64) - np.maximum(c * 16, s * 64), 0, None) / 16.0
    ov[1023] = 0
    M = ov.reshape(8, 128, 256).transpose(1, 0, 2)
    R = np.zeros((128, 64, 128), np.float32)
    for j in range(64):
        R[2 * j, j, :64] = 1
        R[2 * j + 1, j, 64:] = 1
    F = np.zeros((128, 512), np.float32)
    qq = np.arange(128)[:, None]
    rel = np.arange(512)[None, :] - 256
    cur = (qq >= 64).astype(np.int64)
    F[(rel > cur)] = -1e30
    F[(rel == cur) | (rel == cur - 1)] = 1e9
    pos = (np.arange(1024) * 16 + 31).astype(np.float32)
    inv = np.power(np.float32(500000.0), -np.arange(16, dtype=np.float32) * 2.0 / 32)
    ang = pos[:, None] * inv[None, :]
    crope = np.concatenate([np.cos(ang), np.sin(ang)], axis=1).astype(np.float32).reshape(8, 128, 32).transpose(1, 0, 2)
    return dict(masks=masks.astype(NPBF), Mtab=np.ascontiguousarray(M).astype(NPBF), Rtab=R.astype(NPBF), Fbase=F,
                crope=np.ascontiguousarray(crope), ident=np.eye(128, dtype=np.float32).astype(NPBF))


class NSA:
    def __init__(self, nq=T // QT, Tk=T):
        self.nq = nq
        self.Tk = Tk
        self.nc = bass.Bass("TRN2", target_bir_lowering=False)
        self.es = contextlib.ExitStack()

    def dram_in(self, name, shape, dt):
        return self.nc.dram_tensor(name, list(shape), dt, kind="ExternalInput").ap()

    def sb(self, name, shape, dt):
        return self.es.enter_context(self.nc.sbuf_tensor(name, list(shape), dt))

    def mk(self, name, shape, dt):
        return self.sb(name, shape, dt), Buf()

    def build(self):
        with self.es:
            self.tr = Tracker(self.nc, self.es)
            self._build()
        return self.nc

    @staticmethod
    def _alias(b):
        n = Buf()
        n.writer = b.writer
        n.readers = dict(b.readers)
        return n

    def load_T(self, src, dstT, dstb, nchunks):
        tr = self.tr
        for c0 in range(0, nchunks, 8):
            n = min(8, nchunks - c0)
            st, stb = self.stage[(c0 // 8) % 2]
            tr.dma("sp", st[:, 0:n, :], src[c0 * 128:(c0 + n) * 128, :].rearrange("(c p) d -> p c d", p=128), writes=[stb])
            pt, pb = self.ps[7] if (c0 // 8) % 2 == 0 else self.ps[6]
            pv = pt[:].bitcast(BF16)
            for j in range(n):
                tr.op("pe", lambda e: e.transpose(out=pv[:, j * 128:(j + 1) * 128], in_=st[:, j, :], identity=self.ident[:]),
                      reads=[stb, self.ident_b], writes=[pb])
            eng = "act" if (c0 // 8) % 2 == 0 else "dve"
            o = dstT[:, c0 * 128:(c0 + n) * 128]
            if eng == "act":
                tr.op("act", lambda e: e.copy(out=o, in_=pv[:, 0:n * 128]), reads=[pb], writes=[dstb])
            else:
                tr.op("dve", lambda e: e.tensor_copy(out=o, in_=pv[:, 0:n * 128]), reads=[pb], writes=[dstb])

    def compress(self, src, w1, w2, peT, is_k):
        tr = self.tr
        nch = self.Tk // 128
        ncmp = (self.Tk - 32) // 16 + 1
        R1, R1b = self.R1
        self.load_T(src, R1, R1b, nch)
        w1s, w1b = self.R2
        w1v = w1s[:].rearrange("p (l h) -> p l h", l=32)
        for l0 in range(0, 32, 8):
            tr.dma("pool", w1v[:, l0:l0 + 8, :], w1[l0 * 128:(l0 + 8) * 128, :].rearrange("(l p) h -> p l h", p=128), writes=[w1b])
        w2s, w2b = self.w2s
        tr.dma("pool", w2s[:], w2.rearrange("(c p) d -> p c d", p=128), writes=[w2b])
        pef, pefb = self.pef
        tr.dma("sp", pef[:], peT, writes=[pefb])
        peb, pebb = self.peb
        tr.op("dve", lambda e: e.tensor_copy(out=peb[:], in_=pef[:]), reads=[pefb], writes=[pebb])
        hid, hidb = self.hid
        tr.op("pool", lambda e: e.memset(hid[:], 0.0), writes=[hidb])
        bias, biasb = self.small[0]
        for hc in range(4):
            pt, pb = self.ps[hc % 2]
            for l in range(32):
                tr.op("pe", lambda e: e.matmul(pt[:, 0:1], lhsT=w1v[:, l, hc * 128:(hc + 1) * 128], rhs=peb[:, l:l + 1],
                                               start=(l == 0), stop=(l == 31)), reads=[w1b, pebb], writes=[pb])
            tr.op("dve", lambda e: e.tensor_copy(out=bias[:, hc:hc + 1], in_=pt[:, 0:1]), reads=[pb], writes=[biasb])
        for hc in range(4):
            for cb in range(0, ncmp, 512):
                n = min(512, ncmp - cb)
                pt, pb = self.ps[2 + ((hc * 2 + cb // 512) % 2)]
                for l in range(32):
                    rhs = R1[:, l + 16 * cb:l + 16 * cb + 16 * (n - 1) + 1:16]
                    tr.op("pe", lambda e: e.matmul(pt[:, 0:n], lhsT=w1v[:, l, hc * 128:(hc + 1) * 128], rhs=rhs,
                                                   start=(l == 0), stop=(l == 31)), reads=[w1b, R1b], writes=[pb])
                tr.op("act", lambda e: e.activation(out=hid[:, hc, cb:cb + n], in_=pt[:, 0:n], func=AF.Gelu_apprx_tanh,
                                                    bias=bias[:, hc:hc + 1]), reads=[pb, biasb], writes=[hidb])
        ncc = (ncmp + 127) // 128
        for j in range(ncc):
            pt, pb = self.ps[4 + j % 2]
            for hc in range(4):
                tr.op("pe", lambda e: e.matmul(pt[:, 0:128], lhsT=hid[:, hc, j * 128:(j + 1) * 128], rhs=w2s[:, hc, :],
                                               start=(hc == 0), stop=(hc == 3)), reads=[hidb, w2b], writes=[pb])
            if not is_k:
                tr.op("act", lambda e: e.copy(out=self.vcmp1[:, j, 0:128], in_=pt[:, 0:128]), reads=[pb], writes=[self.vcmp1_b])
            else:
                tmp, tb = self.tmpc[j % 2]
                tr.op("act", lambda e: e.copy(out=tmp[:], in_=pt[:, 0:128]), reads=[pb], writes=[tb])
                x1, x2 = tmp[:, 0:16], tmp[:, 16:32]
                cos, sin = self.crope[:, j, 0:16], self.crope[:, j, 16:32]
                r, rb = self.small[1]
                tr.op("dve", lambda e: e.tensor_tensor(out=r[:, 0:16], in0=x1, in1=cos, op=ALU.mult), reads=[tb, self.crope_b], writes=[rb])
                tr.op("dve", lambda e: e.tensor_tensor(out=r[:, 16:32], in0=x2, in1=sin, op=ALU.mult), reads=[tb, self.crope_b], writes=[rb])
                tr.op("dve", lambda e: e.tensor_tensor(out=r[:, 32:48], in0=x2, in1=cos, op=ALU.mult), reads=[tb, self.crope_b], writes=[rb])
                tr.op("dve", lambda e: e.tensor_tensor(out=r[:, 48:64], in0=x1, in1=sin, op=ALU.mult), reads=[tb, self.crope_b], writes=[rb])
                tr.op("dve", lambda e: e.tensor_tensor(out=x1, in0=r[:, 0:16], in1=r[:, 16:32], op=ALU.subtract), reads=[rb, tb], writes=[tb])
                tr.op("dve", lambda e: e.tensor_tensor(out=x2, in0=r[:, 32:48], in1=r[:, 48:64], op=ALU.add), reads=[rb, tb], writes=[tb])
                tb16, tb16b = self.tmpc16[j % 2]
                tr.op("dve", lambda e: e.tensor_copy(out=tb16[:], in_=tmp[:]), reads=[tb], writes=[tb16b])
                p2, p2b = self.ps[6 + j % 2]
                pv = p2[:].bitcast(BF16)
                tr.op("pe", lambda e: e.transpose(out=pv[:, 0:128], in_=tb16[:], identity=self.ident[:]),
                      reads=[tb16b, self.ident_b], writes=[p2b])
                tr.op("act", lambda e: e.copy(out=self.kcmpT[:, j * 128:(j + 1) * 128], in_=pv[:, 0:128]), reads=[p2b],
                      writes=[self.kcmpT_b])

    def run_phase(self, batches, accs):
        tr = self.tr
        fib = set()
        LAG = 2
        pend = []
        for b in list(batches) + [None] * LAG:
            cur = None
            prev = None
            if b is not None:
                bi = self.bcount
                self.bcount += 1
                S, Sb = self.sbig[bi % self.nsbuf]
                P, Pb = self.pbig[bi % 3]
                nu = len(b["units"])
                if "pre" in b:
                    b["pre"]()
                for u, (kT, kb, h) in enumerate(b["units"]):
                    neg = b.get("neg")
                    tr.op("pe", lambda e: e.matmul(S[:, u * QT:(u + 1) * QT], lhsT=kT, rhs=self.QTt[:, h, :], start=(neg is None or u % 2 == 0), stop=(neg is None),
                                                   skip_group_check=(neg is not None)),
                          reads=list(kb) + [self.QT_b], writes=[Sb])
                    if neg is not None and u % 2 == 1:
                        rhs2 = neg[1].unsqueeze(1).to_broadcast([128, 2, QT])
                        tr.op("pe", lambda e: e.matmul(S[:, (u - 1) * QT:(u + 1) * QT].rearrange("p (a q) -> p a q", a=2), lhsT=neg[0], rhs=rhs2,
                                                       start=False, stop=True, skip_group_check=True),
                              reads=list(neg[2]), writes=[Sb])
                n = nu * QT
                if "post" in b:
                    b["post"]()
                tr.op("act", lambda e: e.activation(out=P[:, 0:n], in_=S[:, 0:n], func=AF.Exp), reads=[Sb], writes=[Pb])
                pv3 = P[:, 0:n].rearrange("p (u q) -> p u q", u=nu)
                for m, mb in b["masks"]:
                    tr.op("dve", lambda e: e.tensor_tensor(out=pv3, in0=pv3, in1=m.unsqueeze(1).to_broadcast([128, nu, QT]), op=ALU.mult),
                          reads=[Pb] + list(mb), writes=[Pb])
                cur = (P, Pb, b)
            pend.append(cur)
            if len(pend) > LAG:
                prev = pend.pop(0)
            if prev is not None:
                P, Pb, pb_ = prev
                v = pb_["v"]
                ncol = v.shape[-1]
                for u, (kT, kb, h) in enumerate(pb_["units"]):
                    for sub in range(2):
                        acc, accb, bank = accs[h][sub]
                        st_flag = bank not in fib
                        fib.add(bank)
                        tr.op("pe", lambda e: e.matmul(acc[:, 0:ncol], lhsT=P[:, u * QT + sub * 128:u * QT + (sub + 1) * 128], rhs=v,
                                                       start=st_flag, stop=True, skip_group_check=True),
                              reads=[Pb] + list(pb_["vb"]), writes=[accb])

    def _build(self):
        import os
        nc, tr = self.nc, self.tr
        Tk = self.Tk
        nch = Tk // 128
        a = {}
        a["q"] = self.dram_in("q", [Tk, 512], BF16)
        for nm in ("kc", "vc", "ks", "vs", "kw", "vw"):
            a[nm] = self.dram_in(nm, [Tk, 128], BF16)
        a["gates"] = self.dram_in("gates", [Tk, 12], F32)
        for nm in ("kc_w1", "vc_w1"):
            a[nm] = self.dram_in(nm, [4096, 512], F32)
        for nm in ("kc_w2", "vc_w2"):
            a[nm] = self.dram_in(nm, [512, 128], F32)
        for nm in ("kc_peT", "vc_peT"):
            a[nm] = self.dram_in(nm, [128, 32], F32)
        a["masks"] = self.dram_in("masks", [128, 13, QT], BF16)
        a["Mtab"] = self.dram_in("Mtab", [128, 8, 256], BF16)
        a["Rtab"] = self.dram_in("Rtab", [128, 64, 128], BF16)
        a["Fbase"] = self.dram_in("Fbase", [128, 512], F32)
        a["crope"] = self.dram_in("crope", [128, 8, 32], F32)
        a["ident"] = self.dram_in("ident", [128, 128], BF16)
        o_out = self.nc.dram_tensor("o_out", [self.nq * QT, 512], BF16, kind="ExternalOutput").ap()
        mk = self.mk
        self.R1 = mk("R1", [128, Tk], BF16)
        self.R2 = mk("R2", [128, max(Tk, 16384)], BF16)
        self.vs1, self.vs1_b = mk("vs1", [128, nch, 129], BF16)
        self.vw1, self.vw1_b = mk("vw1", [128, nch, 129], BF16)
        self.hid = mk("hid", [128, 4, 1024], BF16)
        self.w2s = mk("w2s", [128, 4, 128], BF16)
        self.pef = mk("pef", [128, 32], F32)
        self.peb = mk("peb", [128, 32], BF16)
        self.stage = [mk(f"stage{i}", [128, 8, 128], BF16) for i in range(2)]
        self.small = [mk(f"small{i}", [128, 64], F32) for i in range(4)]
        self.tmpc = [mk(f"tmpc{i}", [128, 128], F32) for i in range(2)]
        self.tmpc16 = [mk(f"tmpc16{i}", [128, 128], BF16) for i in range(2)]
        self.kcmpT, self.kcmpT_b = mk("kcmpT", [128, 1024], BF16)
        self.vcmp1, self.vcmp1_b = mk("vcmp1", [128, 8, 385], BF16)
        self.ident, self.ident_b = mk("ident_sb", [128, 128], BF16)
        self.masks, self.masks_b = mk("masks_sb", [128, 13, QT], BF16)
        self.Rtab, self.Rtab_b = mk("Rtab_sb", [128, 64, 128], BF16)
        self.Fbase, self.Fbase_b = mk("Fbase_sb", [128, 512], F32)
        self.crope, self.crope_b = mk("crope_sb", [128, 8, 32], F32)
        big0 = self.es.enter_context(nc.psum_tensor("psbig0", [128, 1024], F32))
        big1 = self.es.enter_context(nc.psum_tensor("psbig1", [128, 1024], F32))
        b0, b1 = Buf(), Buf()
        self.ps = [(big0[:, 0:512], b0), (big0[:, 512:1024], b0), (big1[:, 0:512], b1), (big1[:, 512:1024], b1)]
        for i in range(4, 8):
            self.ps.append((self.es.enter_context(nc.psum_tensor(f"ps{i}", [128, 512], F32))[:], Buf()))
        self.sbig = [(big0[:], b0), (big1[:], b1)]
        self.nsbuf = 2
        self.bcount = 0
        for t_, b_, src in ((self.ident, self.ident_b, a["ident"]), (self.masks, self.masks_b, a["masks"]),
                            (self.Rtab, self.Rtab_b, a["Rtab"]), (self.Fbase, self.Fbase_b, a["Fbase"]),
                            (self.crope, self.crope_b, a["crope"])):
            tr.dma("sp", t_[:], src, writes=[b_])
            b_.const = True
        for _ in range(int(os.environ.get("ACT_DUMMY", "0"))):
            tr.op("act", lambda e: e.copy(out=self.small[3][0][:, 0:8], in_=self.small[3][0][:, 8:16]), writes=[self.small[3][1]])
        for _ in range(int(os.environ.get("DVE_DUMMY", "0"))):
            tr.op("dve", lambda e: e.tensor_copy(out=self.small[3][0][:, 16:24], in_=self.small[3][0][:, 24:32]), writes=[self.small[3][1]])
        tr.op("pool", lambda e: e.memset(self.kcmpT[:], 0.0), writes=[self.kcmpT_b])
        tr.op("pool", lambda e: e.memset(self.vcmp1[:], 0.0), writes=[self.vcmp1_b])
        self.compress(a["kc"], a["kc_w1"], a["kc_w2"], a["kc_peT"], True)
        self.compress(a["vc"], a["vc_w1"], a["vc_w2"], a["vc_peT"], False)
        tr.op("pool", lambda e: e.memset(self.vcmp1[:, :, 128:129], 1.0), reads=[], writes=[self.vcmp1_b])
        tr.dma("sp", self.vcmp1[:, :, 129:385], a["Mtab"], writes=[self.vcmp1_b])
        STOP = os.environ.get("NSA_STOP", "")
        if STOP == "compress":
            tr.finish([]); return
        ksT, ksT_b = self.R1
        kwT, kwT_b = self.R2
        self.load_T(a["ks"], ksT, ksT_b, nch)
        self.load_T(a["kw"], kwT, kwT_b, nch)
        tr.op("pool", lambda e: e.memset(self.vs1[:, :, 128:129], 1.0), writes=[self.vs1_b])
        tr.op("pool", lambda e: e.memset(self.vw1[:, :, 128:129], 1.0), writes=[self.vw1_b])
        for c0 in range(0, nch, 8):
            n = min(8, nch - c0)
            tr.dma("sp", self.vs1[:, c0:c0 + n, 0:128], a["vs"][c0 * 128:(c0 + n) * 128, :].rearrange("(c p) d -> p c d", p=128),
                   writes=[self.vs1_b])
            tr.dma("sp", self.vw1[:, c0:c0 + n, 0:128], a["vw"][c0 * 128:(c0 + n) * 128, :].rearrange("(c p) d -> p c d", p=128),
                   writes=[self.vw1_b])
        if STOP == "kv":
            tr.finish([]); return
        q_tm = [mk(f"q_tm{i}", [128, 2, 512], BF16) for i in range(2)]
        self.QTt, self.QT_b = mk("QTt", [128, 4, QT], BF16)
        self.pbig = [mk(f"pbig{i}", [128, 4 * QT], BF16) for i in range(3)]
        o_acc, o_accb = mk("o_acc", [128, 2, 512], F32)
        o_bf = [mk(f"o_bf{i}", [128, 2, 512], BF16) for i in range(2)]
        imp = [mk(f"imp{i}", [128, 256], F32) for i in range(2)]
        score = [mk(f"score{i}", [128, 256], F32) for i in range(2)]
        selb = [mk(f"selb{i}", [128, 256], BF16) for i in range(2)]
        selT, selT_b = mk("selT", [128, 2, QT], BF16)
        gts = [mk(f"gts{i}", [128, 2, 12], F32) for i in range(2)]
        mslot = []
        for i in range(2):
            pt, pbk = self.ps[7]
            mslot.append((pt[:, i * 256:(i + 1) * 256], self._alias(pbk)))
        pb7s = [self.ps[7][1], mslot[0][1], mslot[1][1]]
        msk = [mk(f"msk{i}", [128, QT], BF16) for i in range(2)]
        acc8 = [[None, None] for _ in range(4)]
        for h in range(4):
            for sub in range(2):
                idx = h * 2 + sub
                pt, pbk = self.ps[4 + idx // 3]
                acc8[h][sub] = (pt[:, (idx % 3) * 129:(idx % 3) * 129 + 129], pbk, 4 + idx // 3)
        out_evs = []
        I0 = int(os.environ.get("NSA_I0", "0"))
        for i in range(I0, self.nq):
            t0 = i * QT
            qt_, qtb = q_tm[i % 2]
            tr.dma("sp", qt_[:], a["q"][t0:t0 + QT, :].rearrange("(s p) c -> p s c", p=128), writes=[qtb])
            g, gb = gts[i % 2]
            tr.dma("sp", g[:], a["gates"][t0:t0 + QT, :].rearrange("(s p) c -> p s c", p=128), writes=[gb])
            pt, pb = self.ps[7]
            pv = pt[:].bitcast(BF16)
            for sub in range(2):
                for h in range(4):
                    tr.op("pe", lambda e: e.transpose(out=pv[:, (sub * 4 + h) * 128:(sub * 4 + h + 1) * 128],
                                                      in_=qt_[:, sub, h * 128:(h + 1) * 128], identity=self.ident[:]),
                          reads=[qtb, self.ident_b], writes=pb7s)
            for sub in range(2):
                tr.op("act" if sub == 0 else "dve",
                      (lambda e: e.copy(out=self.QTt[:, :, sub * 128:(sub + 1) * 128],
                                        in_=pv[:, sub * 512:(sub + 1) * 512].rearrange("p (h q) -> p h q", h=4))) if sub == 0 else
                      (lambda e: e.tensor_copy(out=self.QTt[:, :, sub * 128:(sub + 1) * 128],
                                               in_=pv[:, sub * 512:(sub + 1) * 512].rearrange("p (h q) -> p h q", h=4))),
                      reads=pb7s, writes=[self.QT_b])
            jmax = min((16 * i + 14) // 128, 7)
            for hp in range(2):
                accs = {}
                cbanks = [3, 4, 5, 6]
                for hh in range(2):
                    h = hp * 2 + hh
                    accs[h] = []
                    for sub in range(2):
                        bk = cbanks[hh * 2 + sub]
                        pt_, pb_ = self.ps[bk]
                        accs[h].append((pt_, pb_, bk))
                batches = []
                for j in range(jmax + 1):
                    r = i - 8 * j
                    masks = [(self.masks[:, 4 + r, :], [self.masks_b])] if r <= 8 else []
                    batches.append(dict(units=[(self.kcmpT[:, j * 128:(j + 1) * 128], [self.kcmpT_b], hp * 2 + hh) for hh in range(2)],
                                        v=self.vcmp1[:, j, :], vb=[self.vcmp1_b], masks=masks))
                self.nsbuf = 1
                self.run_phase(batches, accs)
                self.nsbuf = 2
                sm, smb = self.small[1]
                for hh in range(2):
                    for sub in range(2):
                        acc, accb, _ = accs[hp * 2 + hh][sub]
                        k = hh * 2 + sub
                        tr.op("dve", lambda e: e.tensor_scalar(out=sm[:, k:k + 1], in0=acc[:, 128:129], scalar1=1e-30, scalar2=1.0,
                                                               op0=ALU.max, op1=ALU.mult), reads=[accb], writes=[smb])
                tr.op("dve", lambda e: e.reciprocal(out=sm[:, 8:12], in_=sm[:, 0:4]), reads=[smb], writes=[smb])
                tr.op("dve", lambda e: e.tensor_tensor(out=sm[:, 16:20].rearrange("p (h s) -> p h s", s=2),
                                                       in0=sm[:, 8:12].rearrange("p (h s) -> p h s", s=2),
                                                       in1=g[:, :, hp * 2:hp * 2 + 2].rearrange("p s h -> p h s"), op=ALU.mult),
                      reads=[smb, gb], writes=[smb])
                for hh in range(2):
                    h = hp * 2 + hh
                    for sub in range(2):
                        acc, accb, _ = accs[h][sub]
                        k = hh * 2 + sub
                        tr.op("dve", lambda e: e.tensor_scalar(out=o_acc[:, sub, h * 128:(h + 1) * 128], in0=acc[:, 0:128],
                                                               scalar1=sm[:, 16 + k:17 + k], scalar2=1.0, op0=ALU.mult, op1=ALU.mult),
                              reads=[accb, smb], writes=[o_accb])
                        im, imb = imp[sub]
                        if h == 0:
                            tr.op("dve", lambda e: e.tensor_scalar(out=im[:], in0=acc[:, 129:385], scalar1=sm[:, 8 + k:9 + k], scalar2=1.0,
                                                                   op0=ALU.mult, op1=ALU.mult), reads=[accb, smb], writes=[imb])
                        else:
                            tr.op("dve", lambda e: e.scalar_tensor_tensor(out=im[:], in0=acc[:, 129:385], scalar=sm[:, 8 + k:9 + k],
                                                                          in1=im[:], op0=ALU.mult, op1=ALU.add),
                                  reads=[accb, smb, imb], writes=[imb])
            if STOP == "cmp":
                continue
            for sub in range(2):
                qt128 = 2 * i + sub
                im, imb = imp[sub]
                sc, scb = score[sub]
                off = 256 - 2 * qt128
                tr.op("dve", lambda e: e.tensor_tensor(out=sc[:], in0=im[:], in1=self.Fbase[:, off:off + 256], op=ALU.add),
                      reads=[imb, self.Fbase_b], writes=[scb])
                tr.op("dve", lambda e: e.memset(sc[:, 0:1], 1e9), writes=[scb], reads=[scb])
                m8, m8b = self.small[2 + sub]
                tr.op("dve", lambda e: e.max(out=m8[:, 0:8], in_=sc[:]), reads=[scb], writes=[m8b])
                tr.op("dve", lambda e: e.match_replace(out=im[:], in_to_replace=m8[:, 0:8], in_values=sc[:], imm_value=-3e38),
                      reads=[scb, m8b], writes=[imb])
                tr.op("dve", lambda e: e.max(out=m8[:, 8:16], in_=im[:]), reads=[imb], writes=[m8b])
                sb_, sbb = selb[sub]
                tr.op("dve", lambda e: e.tensor_scalar(out=sb_[:], in0=sc[:], scalar1=m8[:, 15:16], scalar2=1.0, op0=ALU.is_ge, op1=ALU.mult),
                      reads=[scb, m8b], writes=[sbb])
                pt, pb = self.ps[7]
                pv = pt[:].bitcast(BF16)
                for sc_ in range(2):
                    tr.op("pe", lambda e: e.transpose(out=pv[:, sc_ * 128:(sc_ + 1) * 128], in_=sb_[:, sc_ * 128:(sc_ + 1) * 128],
                                                      identity=self.ident[:]), reads=[sbb, self.ident_b], writes=pb7s)
                tr.op("dve", lambda e: e.tensor_scalar(out=selT[:, :, sub * 128:(sub + 1) * 128],
                                                       in0=pv[:, 0:256].rearrange("p (c q) -> p c q", c=2),
                                                       scalar1=30000.0, scalar2=-30000.0, op0=ALU.mult, op1=ALU.add),
                      reads=pb7s, writes=[selT_b])
            if STOP == "topk":
                continue
            accs = {h: acc8[h] for h in range(4)}
            batches = []
            for kc in range(0, 2 * i + 2):
                r = kc - 2 * i
                masks = [(self.masks[:, r, :], [self.masks_b])] if r >= 0 else []
                batches.append(dict(units=[(ksT[:, kc * 128:(kc + 1) * 128], [ksT_b], h) for h in range(4)],
                                    v=self.vs1[:, kc, :], vb=[self.vs1_b], masks=masks,
                                    neg=(self.Rtab[:, kc % 64, :], selT[:, kc // 64, :], [self.Rtab_b, selT_b])))
            self.run_phase(batches, accs)
            self.evac_branch(acc8, g, gb, 1, o_acc, o_accb)
            if STOP == "sel":
                continue
            batches = []
            for kc in range(max(0, 2 * i - 4), 2 * i + 2):
                r = kc - 2 * i
                mi = {-4: 2, -3: 3, 0: 0, 1: 1}.get(r)
                masks = [(self.masks[:, mi, :], [self.masks_b])] if mi is not None else []
                batches.append(dict(units=[(kwT[:, kc * 128:(kc + 1) * 128], [kwT_b], h) for h in range(4)],
                                    v=self.vw1[:, kc, :], vb=[self.vw1_b], masks=masks))
            self.run_phase(batches, accs)
            if STOP == "win":
                continue
            self.evac_branch(acc8, g, gb, 2, o_acc, o_accb)
            if STOP == "winevac":
                continue
            ob, obb = o_bf[i % 2]
            tr.op("dve", lambda e: e.tensor_copy(out=ob[:], in_=o_acc[:]), reads=[o_accb], writes=[obb])
            for sub in range(2):
                out_evs.append(tr.dma(os.environ.get("NSA_OUTQ", "sp"), o_out[t0 + sub * 128:t0 + (sub + 1) * 128, :], ob[:, sub, :], reads=[obb]))
        tr.finish(out_evs)

    def evac_branch(self, acc8, g, gb, br, o_acc, o_accb):
        tr = self.tr
        sm, smb = self.small[0]
        for bi, (bank, n) in enumerate(((4, 3), (5, 3), (6, 2))):
            pt, pbk = self.ps[bank]
            den = pt[:, 0:n * 129].rearrange("p (a c) -> p a c", c=129)[:, :, 128]
            tr.op("dve", lambda e: e.tensor_scalar(out=sm[:, bi * 3:bi * 3 + n], in0=den, scalar1=1e-30, scalar2=1.0,
                                                   op0=ALU.max, op1=ALU.mult), reads=[pbk], writes=[smb])
        tr.op("dve", lambda e: e.reciprocal(out=sm[:, 8:16], in_=sm[:, 0:8]), reads=[smb], writes=[smb])
        tr.op("dve", lambda e: e.tensor_tensor(out=sm[:, 16:24].rearrange("p (h s) -> p h s", s=2),
                                               in0=sm[:, 8:16].rearrange("p (h s) -> p h s", s=2),
                                               in1=g[:, :, br * 4:br * 4 + 4].rearrange("p s h -> p h s"), op=ALU.mult),
              reads=[smb, gb], writes=[smb])
        for h in range(4):
            for sub in range(2):
                acc, accb, _ = acc8[h][sub]
                idx = h * 2 + sub
                o = o_acc[:, sub, h * 128:(h + 1) * 128]
                tr.op("dve", lambda e: e.scalar_tensor_tensor(out=o, in0=acc[:, 0:128], scalar=sm[:, 16 + idx:17 + idx], in1=o,
                                                              op0=ALU.mult, op1=ALU.add), reads=[accb, smb, o_accb], writes=[o_accb])


_CACHE = {}


def _prog(key, fn):
    if key not in _CACHE:
        _CACHE[key] = fn()
    return _CACHE[key]


def kernel(x, p, norm_mix, norm_ffn, norm_ple, ffn_up, ffn_down, ple_proj, ple_gate,
           gm_in, gm_ln_g, gm_ln_b, gm_ws, gm_bs, gm_out,
           nsa_in, nsa_kc_pe, nsa_kc_w1, nsa_kc_w2, nsa_vc_pe, nsa_vc_w1, nsa_vc_w2, nsa_out, final_norm):
    f32 = np.float32
    A = lambda v: np.ascontiguousarray(np.asarray(v, dtype=f32))
    x = A(x).reshape(B * T, D)
    p = A(p).reshape(2, B * T, 256)
    norm_mix, norm_ffn, norm_ple = A(norm_mix), A(norm_ffn), A(norm_ple)
    ident = np.eye(128, dtype=f32).astype(NPBF)
    NTOK = B * T // NCORES
    pos = np.arange(T, dtype=f32)
    inv = np.power(f32(500000.0), -np.arange(16, dtype=f32) * f32(2.0) / f32(32))
    ang = pos[:, None] * inv[None, :]
    rope = np.concatenate([np.cos(ang), np.sin(ang)], axis=1).astype(f32)
    rope = np.concatenate([rope, rope], axis=0)
    cores = list(range(NCORES))
    nc1 = Dense(1).build()
    common1 = dict(ident=ident, norm_mix=norm_mix, norm_ffn=norm_ffn, norm_ple=norm_ple,
                   ffn_up1=A(ffn_up[0]), ffn_down1=A(ffn_down[0]), ple_proj1=A(ple_proj[0]), ple_gate1=A(ple_gate[0]),
                   gm_in=A(gm_in[0]), gm_out=A(gm_out[0]), gm_ln_g=A(gm_ln_g).reshape(1, D), gm_ln_b=A(gm_ln_b).reshape(1, D),
                   wsT=np.ascontiguousarray(A(gm_ws[0]).transpose(2, 0, 1)), cmask=np.triu(np.ones((128, 128), f32)),
                   bsT=A(gm_bs[0]).reshape(1, 2048), nsa_in=A(nsa_in[0]))
    maps = []
    for c in cores:
        sl = slice(c * NTOK, (c + 1) * NTOK)
        maps.append(dict(common1, x=x[sl], p=p[0, sl], rope=rope[sl]))
    r1 = run_bass_kernel_spmd(nc1, maps, core_ids=cores).results
    x1 = np.concatenate([r["x_out"] for r in r1], axis=0)
    qkv = np.concatenate([r["qkv_out"] for r in r1], axis=0)
    gates = np.concatenate([r["gates_out"] for r in r1], axis=0)
    del r1, maps
    nc2 = NSA().build()
    C = nsa_consts()
    common2 = dict(C, kc_w1=A(nsa_kc_w1[0]), vc_w1=A(nsa_vc_w1[0]), kc_w2=A(nsa_kc_w2[0]), vc_w2=A(nsa_vc_w2[0]),
                   kc_peT=np.ascontiguousarray(A(nsa_kc_pe[0]).T), vc_peT=np.ascontiguousarray(A(nsa_vc_pe[0]).T))
    maps = []
    for c in cores:
        b, g = c // 4, c % 4
        rows = slice(b * T, (b + 1) * T)
        m = dict(common2)
        m["q"] = np.ascontiguousarray(qkv[rows, g * 512:(g + 1) * 512])
        for i, nm in enumerate(("kc", "vc", "ks", "vs", "kw", "vw")):
            m[nm] = np.ascontiguousarray(qkv[rows, 2048 + i * 512 + g * 128:2048 + i * 512 + (g + 1) * 128])
        m["gates"] = np.ascontiguousarray(np.concatenate([gates[rows, br * 16 + g * 4:br * 16 + g * 4 + 4] for br in range(3)], axis=1))
        maps.append(m)
    r2 = run_bass_kernel_spmd(nc2, maps, core_ids=cores).results
    o = np.empty((B * T, D), dtype=NPBF)
    for c in cores:
        b, g = c // 4, c % 4
        o[b * T:(b + 1) * T, g * 512:(g + 1) * 512] = r2[c]["o_out"]
    del r2, maps, qkv
    nc3 = Dense(3).build()
    common3 = dict(ident=ident, norm_mix=norm_mix, norm_ffn=norm_ffn, norm_ple=norm_ple,
                   ffn_up3=A(ffn_up[1]), ffn_down3=A(ffn_down[1]), ple_proj3=A(ple_proj[1]), ple_gate3=A(ple_gate[1]),
                   nsa_out=A(nsa_out[0]), final_norm=A(final_norm).reshape(1, D))
    maps = []
    for c in cores:
        sl = slice(c * NTOK, (c + 1) * NTOK)
        maps.append(dict(common3, x=x1[sl], p=p[1, sl], o_in=np.ascontiguousarray(o[sl])))
    r3 = run_bass_kernel_spmd(nc3, maps, core_ids=cores).results
    out = np.concatenate([r["x_out"] for r in r3], axis=0).reshape(B, T, D).astype(f32)
    return out
```

```python
import bisect
import contextlib
import numpy as np
import ml_dtypes
import concourse.bass as bass
import concourse.mybir as mybir
from concourse.bass_utils import run_bass_kernel_spmd

F32 = mybir.dt.float32
BF16 = mybir.dt.bfloat16
AF = mybir.ActivationFunctionType
ALU = mybir.AluOpType
NPBF = ml_dtypes.bfloat16

D = 2048
DFF = 8192
T = 16384
B = 2
NCORES = 8
EPS = 1e-6
SEM_LIMIT = 1000


class Buf:
    __slots__ = ("writer", "readers", "const")

    def __init__(self):
        self.writer = None
        self.readers = {}
        self.const = False


class Tracker:
    def __init__(self, nc, es):
        self.nc = nc
        self.es = es
        self.engs = {"pe": nc.tensor, "act": nc.scalar, "dve": nc.vector, "pool": nc.gpsimd, "sp": nc.sync}
        self.sems = {k: [] for k in self.engs}
        self.seq = {k: 0 for k in self.engs}
        self.last = {k: None for k in self.engs}
        self.sig_seqs = {k: [] for k in self.engs}
        self.known = {}
        self.dma_sems = {}
        self.dma_uses = {}
        self.dma_rr = {}
        for q, n in (("sp", 20), ("pool", 12), ("act", 6)):
            self.dma_sems[q] = [es.enter_context(nc.semaphore(f"dq_{q}_{i}")) for i in range(n)]
            self.dma_uses[q] = [0] * n
            self.dma_rr[q] = 0
        self.nwaits = 0

    def _eng_sem(self, e, epoch):
        while len(self.sems[e]) <= epoch:
            self.sems[e].append(self.es.enter_context(self.nc.semaphore(f"es_{e}_{len(self.sems[e])}")))
        return self.sems[e][epoch]

    def _wait(self, waiter, ev):
        if ev is None:
            return
        if ev[0] == "dma":
            _, q, si, val = ev
            key = (waiter, "dma", q, si)
            if self.known.get(key, 0) >= val:
                return
            self.engs[waiter].wait_ge(self.dma_sems[q][si], val)
            self.known[key] = val
            self.nwaits += 1
            return
        e, seq = ev
        sigs = self.sig_seqs[e]
        i = bisect.bisect_left(sigs, seq)
        if i == len(sigs):
            lseq, lins = self.last[e]
            assert lseq >= seq
            k = len(sigs)
            lins.then_inc(self._eng_sem(e, k // SEM_LIMIT), 1)
            sigs.append(lseq)
        k = i
        epoch, val = k // SEM_LIMIT, k % SEM_LIMIT + 1
        key = (waiter, e, epoch)
        if self.known.get(key, 0) >= val:
            return
        self.engs[waiter].wait_ge(self._eng_sem(e, epoch), val)
        self.known[key] = val
        for ep in range(epoch):
            self.known[(waiter, e, ep)] = SEM_LIMIT
        self.nwaits += 1

    def _deps(self, eng, reads, writes):
        deps = {}

        def add(ev, is_write_dep):
            if ev is None:
                return
            if ev[0] == "dma":
                deps[ev] = ev
            else:
                e, seq = ev
                if e == "pe" and eng == "pe" and is_write_dep:
                    return
                if deps.get(e, (e, 0))[1] < seq:
                    deps[e] = ev

        for b in reads:
            add(b.writer, False)
        for b in writes:
            add(b.writer, True)
            for r in b.readers.values():
                if not (r[0] == eng and eng == "pe" and False):
                    add(r, False)
        return list(deps.values())

    def _record(self, ev, reads, writes):
        for b in reads:
            if b.const:
                continue
            if ev[0] == "dma":
                b.readers[ev] = ev
            else:
                b.readers[ev[0]] = ev
        for b in writes:
            b.writer = ev
            b.readers = {}

    def op(self, eng, fn, reads=(), writes=()):
        for d in self._deps(eng, reads, writes):
            if d[0] == eng and eng == "pe":
                continue
            if False and d[0] == eng and self.seq[eng] - d[1] >= 3:
                continue
            self._wait(eng, d)
        ins = fn(self.engs[eng])
        self.seq[eng] += 1
        ev = (eng, self.seq[eng])
        self.last[eng] = (self.seq[eng], ins)
        self._record(ev, reads, writes)
        return ins

    def dma(self, q, out, in_, reads=(), writes=(), **kw):
        for d in self._deps(q, reads, writes):
            self._wait(q, d)
        n = len(self.dma_sems[q])
        si = self.dma_rr[q]
        self.dma_rr[q] = (si + 1) % n
        uses = self.dma_uses[q][si]
        if uses > 0:
            self._wait(q, ("dma", q, si, 16 * uses))
        ins = self.engs[q].dma_start(out=out, in_=in_, **kw)
        ins.then_inc(self.dma_sems[q][si], 16)
        self.dma_uses[q][si] = uses + 1
        ev = ("dma", q, si, 16 * (uses + 1))
        self._record(ev, reads, writes)
        return ev

    def finish(self, out_events):
        for ev in out_events:
            self._wait("sp", ev)


TT = 512
NSUB = 4
NTILE = 4096 // TT
WSLOT = 8192


class Dense:
    def __init__(self, mode, ntiles=NTILE):
        self.mode = mode
        self.ntiles = ntiles
        nc = self.nc = bass.Bass("TRN2", target_bir_lowering=False)
        self.es = contextlib.ExitStack()

    def dram_in(self, name, shape, dt):
        return self.nc.dram_tensor(name, list(shape), dt, kind="ExternalInput").ap()

    def dram_out(self, name, shape, dt):
        return self.nc.dram_tensor(name, list(shape), dt, kind="ExternalOutput").ap()

    def sb(self, name, shape, dt):
        return self.es.enter_context(self.nc.sbuf_tensor(name, list(shape), dt))

    def build(self):
        nc, es = self.nc, self.es
        with es:
            self.tr = Tracker(nc, es)
            self._build()
        return nc

    def wpiece(self, wap, r0, nk, c0, ncols):
        assert nk * ncols <= WSLOT
        i = self.wrr
        self.wrr = (i + 1) % len(self.wring)
        t, b = self.wring[i]
        dst = t[:, 0:nk * ncols].rearrange("p (k n) -> p k n", k=nk)
        src = wap[r0 * 128:(r0 + nk) * 128, c0:c0 + ncols].rearrange("(k p) n -> p k n", p=128)
        self.wq += 1
        tok = self.bf_tok.get(id(wap))
        if tok is not None:
            self.tr.dma("sp", dst, src, reads=[tok], writes=[b])
        else:
            self.tr.dma("pool", dst, src, writes=[b])
        return dst, b

    def preconvert(self, name, wap, rows, cols):
        dst = self.nc.dram_tensor(name + "_bf", [rows, cols], BF16, kind="Internal").ap()
        tok = Buf()
        rstep = 512
        for r in range(0, rows, rstep):
            for c in range(0, cols, 2048):
                self.tr.dma("pool", dst[r:r + rstep, c:c + 2048], wap[r:r + rstep, c:c + 2048], writes=[tok])
        self.bf_tok[id(dst)] = tok
        return dst

    def bcast_load(self, vec_ap):
        i = self.grr
        self.grr = (i + 1) % len(self.gbc)
        t, b = self.gbc[i]
        self.tr.dma("sp", t[:], vec_ap.partition_broadcast(128), writes=[b])
        return t, b

    def next_ps(self):
        i = self.prr
        self.prr = (i + 1) % 8
        return self.ps[i]

    def rmsnorm_T(self, gvec_ap):
        tr = self.tr
        g_t, g_b = self.bcast_load(gvec_ap)
        for s in range(NSUB):
            xs = self.xres[:, s, :]
            xb = self.xres_b[s]
            junk, jb = self.xn_tm[s % 2]
            ss, ssb = self.small[s % 2]
            tr.op("act", lambda e: e.activation(out=junk[:], in_=xs, func=AF.Square, accum_out=ss[:, 0:1]),
                  reads=[xb], writes=[jb, ssb])
            tr.op("dve", lambda e: e.tensor_scalar(out=ss[:, 1:2], in0=ss[:, 0:1], scalar1=1.0 / D, scalar2=EPS,
                                                   op0=ALU.mult, op1=ALU.add), reads=[ssb], writes=[ssb])
            tr.op("act", lambda e: e.activation(out=ss[:, 2:3], in_=ss[:, 1:2], func=AF.Sqrt), reads=[ssb], writes=[ssb])
            tr.op("dve", lambda e: e.reciprocal(out=ss[:, 3:4], in_=ss[:, 2:3]), reads=[ssb], writes=[ssb])
            tr.op("dve", lambda e: e.scalar_tensor_tensor(out=junk[:], in0=xs, scalar=ss[:, 3:4], in1=g_t[:],
                                                          op0=ALU.mult, op1=ALU.mult),
                  reads=[xb, ssb, g_b], writes=[jb])
            self.transpose_into(junk, jb, 16, self.xnT, self.xnT_b, s)

    def transpose_into(self, src, srcb, nk, dstT, dstb, s):
        tr = self.tr
        for k0 in range(0, nk, 8):
            n = min(8, nk - k0)
            pt, pb = self.next_ps()
            pv = pt[:].bitcast(BF16)
            for j in range(n):
                k = k0 + j
                tr.op("pe", lambda e: e.transpose(out=pv[:, j * 128:(j + 1) * 128], in_=src[:, k * 128:(k + 1) * 128],
                                                  identity=self.ident[:]),
                      reads=[srcb, self.ident_b], writes=[pb])
            eng = "act" if (k0 // 8) % 2 == 0 else "dve"
            o = dstT[:, k0:k0 + n, s * 128:(s + 1) * 128]
            i = pv[:, 0:n * 128].rearrange("p (k t) -> p k t", k=n)
            if eng == "act":
                tr.op("act", lambda e: e.copy(out=o, in_=i), reads=[pb], writes=[dstb])
            else:
                tr.op("dve", lambda e: e.tensor_copy(out=o, in_=i), reads=[pb], writes=[dstb])

    def proj_fm(self, wap, r0, nk, c0, ncols, srcT, srcb, evac):
        tr = self.tr
        pcols = WSLOT // nk
        for p0 in range(0, ncols, pcols):
            pc = min(pcols, ncols - p0)
            w, wb = self.wpiece(wap, r0, nk, c0 + p0, pc)
            for f in range(pc // 128):
                pt, pb = self.next_ps()
                for k in range(nk):
                    tr.op("pe", lambda e: e.matmul(pt[:], lhsT=w[:, k, f * 128:(f + 1) * 128], rhs=srcT[:, k, :],
                                                   start=(k == 0), stop=(k == nk - 1)),
                          reads=[wb] + list(srcb), writes=[pb])
                evac((p0 // 128) + f, pt, pb)

    def proj_tm(self, wap, r0, nk, c0, ncols, srcT, srcb, evac, blk=512):
        tr = self.tr
        kper = max(1, WSLOT // blk)
        half = 0
        for cb, cc in enumerate(range(0, ncols, blk)):
            w_ = min(blk, ncols - cc)
            pss = [self.ps[(half * 4 + s)] for s in range(NSUB)]
            half ^= 1
            for k0 in range(0, nk, kper):
                kn = min(kper, nk - k0)
                w, wb = self.wpiece(wap, r0 + k0, kn, c0 + cc, w_)
                for kk in range(kn):
                    k = k0 + kk
                    for s in range(NSUB):
                        pt, pb = pss[s]
                        tr.op("pe", lambda e: e.matmul(pt[:, 0:w_], lhsT=srcT[:, k, s * 128:(s + 1) * 128], rhs=w[:, kk, :],
                                                       start=(k == 0), stop=(k == nk - 1)),
                              reads=[wb] + list(srcb), writes=[pb])
            for s in range(NSUB):
                pt, pb = pss[s]
                evac(cb, s, pt, pb, w_)

    def resid_add(self, cb, s, pt, pb, w_):
        xs = self.xres[:, s, cb * 512:cb * 512 + w_]
        self.tr.op("dve", lambda e: e.tensor_tensor(out=xs, in0=xs, in1=pt[:, 0:w_], op=ALU.add),
                   reads=[pb, self.xres_b[s]], writes=[self.xres_b[s]])

    def ffn(self, li):
        tr = self.tr
        self.rmsnorm_T(self.a["norm_ffn"][li:li + 1, :])
        for h in range(2):
            def evac_up(f, pt, pb):
                tmp, tb = self.tmpf[f % 2]
                tr.op("act", lambda e: e.activation(out=tmp[:], in_=pt[:], func=AF.Relu), reads=[pb], writes=[tb])
                eng = "dve"
                tr.op(eng, lambda e: e.tensor_tensor(out=self.hT[:, f, :], in0=tmp[:], in1=tmp[:], op=ALU.mult),
                      reads=[tb], writes=[self.hT_b, self.hT_b2])
            self.proj_fm(self.a["ffn_up"][li], 0, 16, h * 4096, 4096, self.xnT, [self.xnT_b], evac_up)
            self.proj_tm(self.a["ffn_down"][li], h * 32, 32, 0, 2048, self.hT, [self.hT_b, self.hT_b2], self.resid_add)

    def ple(self, li, tok0):
        tr = self.tr
        self.rmsnorm_T(self.a["norm_ple"][li:li + 1, :])
        pf, pfb = self.p_f
        tr.dma("sp", pf[:], self.a["p"][tok0:tok0 + TT, :].rearrange("(s p) c -> p s c", p=128), writes=[pfb])
        for s in range(NSUB):
            pbf, pbb = self.p_bf[s % 2]
            tr.op("dve", lambda e: e.tensor_copy(out=pbf[:], in_=pf[:, s, :]), reads=[pfb], writes=[pbb])
            self.transpose_into(pbf, pbb, 2, self.pT, self.pT_b, s)
        for cb in range(4):
            pg = [self.ps[s] for s in range(4)]
            pp = [self.ps[4 + s] for s in range(4)]
            w, wb = self.wpiece(self.a["ple_gate"][li], 0, 16, cb * 512, 512)
            wple, wpb = self.wpiece(self.a["ple_proj"][li], 0, 2, cb * 512, 512)
            for k in range(16):
                for s in range(NSUB):
                    pt, pb = pg[s]
                    tr.op("pe", lambda e: e.matmul(pt[:], lhsT=self.xnT[:, k, s * 128:(s + 1) * 128], rhs=w[:, k, :],
                                                   start=(k == 0), stop=(k == 15)), reads=[wb, self.xnT_b], writes=[pb])
            for k in range(2):
                for s in range(NSUB):
                    pt, pb = pp[s]
                    tr.op("pe", lambda e: e.matmul(pt[:], lhsT=self.pT[:, k, s * 128:(s + 1) * 128],
                                                   rhs=wple[:, k, :],
                                                   start=(k == 0), stop=(k == 1)), reads=[wpb, self.pT_b], writes=[pb])
            for s in range(NSUB):
                tmp, tb = self.tmpf[s % 2]
                tr.op("act", lambda e: e.activation(out=tmp[:], in_=pg[s][0][:], func=AF.Sigmoid), reads=[pg[s][1]], writes=[tb])
                tr.op("dve", lambda e: e.tensor_tensor(out=tmp[:], in0=tmp[:], in1=pp[s][0][:], op=ALU.mult),
                      reads=[tb, pp[s][1]], writes=[tb])
                xs = self.xres[:, s, cb * 512:(cb + 1) * 512]
                tr.op("dve", lambda e: e.tensor_tensor(out=xs, in0=xs, in1=tmp[:], op=ALU.add),
                      reads=[tb, self.xres_b[s]], writes=[self.xres_b[s]])

    def gmlp(self):
        tr = self.tr
        a = self.a
        self.rmsnorm_T(a["norm_mix"][0:1, :])
        tr.dma("sp", self.hT[:, 16:24, :].rearrange("p k t -> p (k t)").bitcast(F32), a["bsT"].partition_broadcast(128),
               writes=[self.hT_b2])

        def evac_u(f, pt, pb):
            tr.op("act", lambda e: e.activation(out=self.hT[:, f, :], in_=pt[:], func=AF.Gelu_apprx_tanh),
                  reads=[pb], writes=[self.hT_b])
        self.proj_fm(a["gm_in"], 0, 16, 0, 2048, self.xnT, [self.xnT_b], evac_u)

        def evac_v(cb, s, pt, pb, w_):
            vs = self.v_f[:, s, cb * 512:(cb + 1) * 512]
            tr.op("act", lambda e: e.activation(out=vs, in_=pt[:], func=AF.Gelu_apprx_tanh), reads=[pb], writes=[self.v_fb[s]])
            tr.op("dve", lambda e: e.bn_stats(out=self.stats[:, s, cb * 6:(cb + 1) * 6], in_=vs), reads=[self.v_fb[s]],
                  writes=[self.stats_b[s]])
        self.proj_tm(a["gm_in"], 0, 16, 2048, 2048, self.xnT, [self.xnT_b], evac_v)
        lg_t, lg_b = self.bcast_load(a["gm_ln_g"])
        lb_t, lb_b = self.bcast_load(a["gm_ln_b"])
        for s in range(NSUB):
            mv, mvb = self.small[s % 2]
            tr.op("dve", lambda e: e.bn_aggr(out=mv[:, 0:2], in_=self.stats[:, s, :]), reads=[self.stats_b[s]], writes=[mvb])
            tr.op("dve", lambda e: e.tensor_scalar(out=mv[:, 2:3], in0=mv[:, 1:2], scalar1=EPS, scalar2=1.0, op0=ALU.add, op1=ALU.mult),
                  reads=[mvb], writes=[mvb])
            tr.op("act", lambda e: e.activation(out=mv[:, 3:4], in_=mv[:, 2:3], func=AF.Sqrt), reads=[mvb], writes=[mvb])
            tr.op("dve", lambda e: e.reciprocal(out=mv[:, 4:5], in_=mv[:, 3:4]), reads=[mvb], writes=[mvb])
            vs = self.v_f[:, s, :]
            tr.op("dve", lambda e: e.tensor_scalar(out=vs, in0=vs, scalar1=mv[:, 0:1], scalar2=mv[:, 4:5],
                                                   op0=ALU.subtract, op1=ALU.mult), reads=[mvb, self.v_fb[s]], writes=[self.v_fb[s]])
            tr.op("dve", lambda e: e.tensor_tensor(out=vs, in0=vs, in1=lg_t[:], op=ALU.mult), reads=[self.v_fb[s], lg_b],
                  writes=[self.v_fb[s]])
            vl, vlb = self.xn_tm[s % 2]
            tr.op("dve", lambda e: e.tensor_tensor(out=vl[:], in0=vs, in1=lb_t[:], op=ALU.add), reads=[self.v_fb[s], lb_b],
                  writes=[vlb])
            for g4 in range(4):
                pt, pb = self.next_ps()
                for gg in range(4):
                    g = g4 * 4 + gg
                    tr.op("pe", lambda e: e.matmul(pt[:, gg * 128:(gg + 1) * 128], lhsT=vl[:, g * 128:(g + 1) * 128],
                                                   rhs=self.wsT[:, g, :], start=True, stop=True),
                          reads=[vlb, self.wsT_b], writes=[pb])
                tmp, tb = self.tmpf[g4 % 2]
                tv = tmp[:].rearrange("p (g t) -> p g t", g=4)
                tr.op("dve", lambda e: e.tensor_tensor(out=tv, in0=pt[:].rearrange("p (g t) -> p g t", g=4),
                                                       in1=self.bsT[:, g4 * 4:(g4 + 1) * 4, :], op=ALU.add),
                      reads=[pb, self.bsT_b], writes=[tb])
                u = self.hT[:, g4 * 4:(g4 + 1) * 4, s * 128:(s + 1) * 128]
                tr.op("dve", lambda e: e.tensor_tensor(out=u, in0=tv, in1=u, op=ALU.mult), reads=[tb, self.hT_b],
                      writes=[self.hT_b])
        mT = self.hT[:, 0:16, :]
        self.proj_tm(a["gm_out"], 0, 16, 0, 2048, mT, [self.hT_b], self.resid_add)

    def nsa_proj(self, tok0, pos0):
        tr = self.tr
        a = self.a
        self.rmsnorm_T(a["norm_mix"][1:2, :])
        cs, csb = self.cs
        tr.dma("sp", cs[:], a["rope"][tok0:tok0 + TT, :].rearrange("(s p) c -> p s c", p=128), writes=[csb])
        out_evs = self.out_evs

        def evac(cb, s, pt, pb, w_):
            if cb == 10:
                g, gb = self.small[2 + s % 2]
                tr.op("act", lambda e: e.activation(out=g[:, 0:48], in_=pt[:, 0:48], func=AF.Sigmoid), reads=[pb], writes=[gb])
                out_evs.append(tr.dma("sp", a["gates_out"][tok0 + s * 128:tok0 + (s + 1) * 128, :], g[:, 0:48], reads=[gb]))
                return
            tmp, tb = self.tmpf[s % 2]
            is_q = cb < 4
            rope = is_q or cb in (6, 8)
            ob, obb = self.obf[(cb * 4 + s) % 2]
            if not rope:
                tr.op("act", lambda e: e.copy(out=ob[:], in_=pt[:]), reads=[pb], writes=[obb])
            else:
                sc = (128.0 ** -0.5) if is_q else 1.0
                tr.op("act", lambda e: e.mul(out=tmp[:], in_=pt[:], mul=sc), reads=[pb], writes=[tb])
                v = tmp[:].rearrange("p (h d) -> p h d", h=4)
                x1, x2 = v[:, :, 0:16], v[:, :, 16:32]
                cos = cs[:, s, 0:16].unsqueeze(1).to_broadcast([128, 4, 16])
                sin = cs[:, s, 16:32].unsqueeze(1).to_broadcast([128, 4, 16])
                r, rb = self.ropet[s % 2]
                rv = r[:].rearrange("p (j h d) -> p j h d", j=4, h=4)
                tr.op("dve", lambda e: e.tensor_tensor(out=rv[:, 0], in0=x1, in1=cos, op=ALU.mult), reads=[tb, csb], writes=[rb])
                tr.op("dve", lambda e: e.tensor_tensor(out=rv[:, 1], in0=x2, in1=sin, op=ALU.mult), reads=[tb, csb], writes=[rb])
                tr.op("dve", lambda e: e.tensor_tensor(out=rv[:, 2], in0=x2, in1=cos, op=ALU.mult), reads=[tb, csb], writes=[rb])
                tr.op("dve", lambda e: e.tensor_tensor(out=rv[:, 3], in0=x1, in1=sin, op=ALU.mult), reads=[tb, csb], writes=[rb])
                tr.op("dve", lambda e: e.tensor_tensor(out=x1, in0=rv[:, 0], in1=rv[:, 1], op=ALU.subtract), reads=[rb, tb], writes=[tb])
                tr.op("dve", lambda e: e.tensor_tensor(out=x2, in0=rv[:, 2], in1=rv[:, 3], op=ALU.add), reads=[rb, tb], writes=[tb])
                tr.op("act", lambda e: e.copy(out=ob[:], in_=tmp[:]), reads=[tb], writes=[obb])
            out_evs.append(tr.dma("sp", a["qkv_out"][tok0 + s * 128:tok0 + (s + 1) * 128, cb * 512:(cb + 1) * 512], ob[:], reads=[obb]))
        self.proj_tm(a["nsa_in"], 0, 16, 0, 5168, self.xnT, [self.xnT_b], evac)

    def attn_out(self, tok0):
        tr = self.tr
        a = self.a
        for s in range(NSUB):
            ot, otb = self.xn_tm[s % 2]
            tr.dma("sp", ot[:], a["o_in"][tok0 + s * 128:tok0 + (s + 1) * 128, :], writes=[otb])
            self.transpose_into(ot, otb, 16, self.xnT, self.xnT_b, s)
        self.proj_tm(a["nsa_out"], 0, 16, 0, 2048, self.xnT, [self.xnT_b], self.resid_add)

    def final_norm(self, tok0):
        tr = self.tr
        g_t, g_b = self.bcast_load(self.a["final_norm"])
        for s in range(NSUB):
            xs = self.xres[:, s, :]
            xb = self.xres_b[s]
            junk, jb = self.xn_tm[s % 2]
            ss, ssb = self.small[s % 2]
            tr.op("act", lambda e: e.activation(out=junk[:], in_=xs, func=AF.Square, accum_out=ss[:, 0:1]),
                  reads=[xb], writes=[jb, ssb])
            tr.op("dve", lambda e: e.tensor_scalar(out=ss[:, 1:2], in0=ss[:, 0:1], scalar1=1.0 / D, scalar2=EPS,
                                                   op0=ALU.mult, op1=ALU.add), reads=[ssb], writes=[ssb])
            tr.op("act", lambda e: e.activation(out=ss[:, 2:3], in_=ss[:, 1:2], func=AF.Sqrt), reads=[ssb], writes=[ssb])
            tr.op("dve", lambda e: e.reciprocal(out=ss[:, 3:4], in_=ss[:, 2:3]), reads=[ssb], writes=[ssb])
            tr.op("dve", lambda e: e.scalar_tensor_tensor(out=xs, in0=xs, scalar=ss[:, 3:4], in1=g_t[:],
                                                          op0=ALU.mult, op1=ALU.mult),
                  reads=[xb, ssb, g_b], writes=[xb])
            self.out_evs.append(tr.dma("sp", self.a["x_out"][tok0 + s * 128:tok0 + (s + 1) * 128, :], xs, reads=[xb]))

    def _build(self):
        nc, tr = self.nc, self.tr
        mode = self.mode
        ntok = self.ntiles * TT
        a = self.a = {}
        a["x"] = self.dram_in("x", [ntok, D], F32)
        a["p"] = self.dram_in("p", [ntok, 256], F32)
        a["ident"] = self.dram_in("ident", [128, 128], BF16)
        for nm in ("norm_mix", "norm_ffn", "norm_ple"):
            a[nm] = self.dram_in(nm, [2, D], F32)
        a["ffn_up"] = [self.dram_in(f"ffn_up{mode}", [D, DFF], F32)] * 2
        a["ffn_down"] = [self.dram_in(f"ffn_down{mode}", [DFF, D], F32)] * 2
        a["ple_proj"] = [self.dram_in(f"ple_proj{mode}", [256, D], F32)] * 2
        a["ple_gate"] = [self.dram_in(f"ple_gate{mode}", [D, D], F32)] * 2
        li = 0 if mode == 1 else 1
        if mode == 1:
            a["gm_in"] = self.dram_in("gm_in", [D, 4096], F32)
            a["gm_out"] = self.dram_in("gm_out", [D, D], F32)
            a["gm_ln_g"] = self.dram_in("gm_ln_g", [1, D], F32)
            a["gm_ln_b"] = self.dram_in("gm_ln_b", [1, D], F32)
            a["wsT"] = self.dram_in("wsT", [128, 16, 128], F32)
            a["cmask"] = self.dram_in("cmask", [128, 128], F32)
            a["bsT"] = self.dram_in("bsT", [1, 2048], F32)
            a["nsa_in"] = self.dram_in("nsa_in", [D, 5168], F32)
            a["rope"] = self.dram_in("rope", [ntok, 32], F32)
            a["x_out"] = self.dram_out("x_out", [ntok, D], F32)
            a["qkv_out"] = self.dram_out("qkv_out", [ntok, 5120], BF16)
            a["gates_out"] = self.dram_out("gates_out", [ntok, 48], F32)
        else:
            a["o_in"] = self.dram_in("o_in", [ntok, D], BF16)
            a["nsa_out"] = self.dram_in("nsa_out", [D, D], F32)
            a["final_norm"] = self.dram_in("final_norm", [1, D], F32)
            a["x_out"] = self.dram_out("x_out", [ntok, D], F32)

        def mk(name, shape, dt):
            return self.sb(name, shape, dt), Buf()
        self.xres = self.sb("xres", [128, NSUB, D], F32)
        self.xres_b = [Buf() for _ in range(NSUB)]
        self.xn_tm = [mk(f"xn_tm{i}", [128, D], BF16) for i in range(2)]
        self.xnT = self.sb("xnT", [128, 16, TT], BF16)
        self.xnT_b = Buf()
        self.hT = self.sb("hT", [128, 32, TT], BF16)
        self.hT_b = Buf()
        self.hT_b2 = Buf()
        self.wring = [mk(f"wr{i}", [128, WSLOT], BF16) for i in range(3)]
        self.wrr = 0
        self.wq = 0
        self.gbc = [mk(f"gbc{i}", [128, D], F32) for i in range(2)]
        self.grr = 0
        self.small = [mk(f"small{i}", [128, 64], F32) for i in range(4)]
        self.tmpf = [mk(f"tmpf{i}", [128, 512], F32) for i in range(2)]
        self.ident, self.ident_b = mk("ident_sb", [128, 128], BF16)
        self.p_f = mk("p_f", [128, NSUB, 256], F32)
        self.p_bf = [mk(f"p_bf{i}", [128, 256], BF16) for i in range(2)]
        self.pT = self.sb("pT", [128, 2, TT], BF16)
        self.pT_b = Buf()
        self.ps = [(self.es.enter_context(nc.psum_tensor(f"ps{i}", [128, 512], F32)), Buf()) for i in range(8)]
        self.prr = 0
        self.out_evs = []
        tr.dma("sp", self.ident[:], a["ident"], writes=[self.ident_b])
        self.ident_b.const = True
        self.bf_tok = {}
        if mode == 1:
            self.v_f = self.sb("v_f", [128, NSUB, D], F32)
            self.v_fb = [Buf() for _ in range(NSUB)]
            self.stats = self.sb("stats", [128, NSUB, 24], F32)
            self.stats_b = [Buf() for _ in range(NSUB)]
            self.wsT, self.wsT_b = mk("wsT_sb", [128, 16, 128], BF16)
            self.bsT = self.hT[:, 16:24, :].rearrange("p k t -> p (k t)").bitcast(F32).rearrange("p (g t) -> p g t", g=16)
            self.bsT_b = self.hT_b2
            wsf0, wsfb = self.gbc[0]
            cm0, cmb = self.gbc[1]
            wsf = wsf0[:].rearrange("p (g t) -> p g t", g=16)
            cm = cm0[:, 0:128]
            tr.dma("sp", wsf, a["wsT"], writes=[wsfb])
            tr.dma("sp", cm, a["cmask"], writes=[cmb])
            tr.op("dve", lambda e: e.tensor_tensor(out=self.wsT[:], in0=wsf, in1=cm.unsqueeze(1).to_broadcast([128, 16, 128]),
                                                   op=ALU.mult), reads=[wsfb, cmb], writes=[self.wsT_b])
            self.wsT_b.const = True
            self.cs = mk("cs", [128, NSUB, 32], F32)
            self.obf = [mk(f"obf{i}", [128, 512], BF16) for i in range(2)]
            self.ropet = [mk(f"ropet{i}", [128, 256], F32) for i in range(2)]

        for ti in range(self.ntiles):
            tok0 = ti * TT
            for s in range(NSUB):
                tr.dma("sp", self.xres[:, s, :], a["x"][tok0 + s * 128:tok0 + (s + 1) * 128, :], writes=[self.xres_b[s]])
            if mode == 1:
                self.gmlp()
                self.ffn(0)
                self.ple(0, tok0)
                for s in range(NSUB):
                    self.out_evs.append(tr.dma("sp", a["x_out"][tok0 + s * 128:tok0 + (s + 1) * 128, :], self.xres[:, s, :],
                                               reads=[self.xres_b[s]]))
                self.nsa_proj(tok0, 0)
            else:
                self.attn_out(tok0)
                self.ffn(1)
                self.ple(1, tok0)
                self.final_norm(tok0)
        tr.finish(self.out_evs)


QT = 256
NCMP = 1023


def nsa_consts():
    kl = np.arange(128)[:, None]
    ql = np.arange(QT)[None, :]
    masks = np.zeros((128, 13, QT), np.float32)
    masks[:, 0] = (kl <= ql)
    masks[:, 1] = (128 + kl <= ql)
    masks[:, 2] = (kl > ql)
    masks[:, 3] = (kl + 128 > ql)
    for r in range(9):
        masks[:, 4 + r] = (16 * kl + 31 <= 256 * r + ql)
    c = np.arange(1024)[:, None]
    s = np.arange(256)[None, :]
    ov = np.clip(np.minimum(c * 16 + 32, s * 64 + 64) - np.maximum(c * 16, s * 64), 0, None) / 16.0
    ov[1023] = 0
    M = ov.reshape(8, 128, 256).transpose(1, 0, 2)
    R = np.zeros((128, 64, 128), np.float32)
    for j in range(64):
        R[2 * j, j, :64] = 1
        R[2 * j + 1, j, 64:] = 1
    F = np.zeros((128, 512), np.float32)
    qq = np.arange(128)[:, None]
    rel = np.arange(512)[None, :] - 256
    cur = (qq >= 64).astype(np.int64)
    F[(rel > cur)] = -1e30
    F[(rel == cur) | (rel == cur - 1)] = 1e9
    pos = (np.arange(1024) * 16 + 31).astype(np.float32)
    inv = np.power(np.float32(500000.0), -np.arange(16, dtype=np.float32) * 2.0 / 32)
    ang = pos[:, None] * inv[None, :]
    crope = np.concatenate([np.cos(ang), np.sin(ang)], axis=1).astype(np.float32).reshape(8, 128, 32).transpose(1, 0, 2)
    return dict(masks=masks.astype(NPBF), Mtab=np.ascontiguousarray(M).astype(NPBF), Rtab=R.astype(NPBF), Fbase=F,
                crope=np.ascontiguousarray(crope), ident=np.eye(128, dtype=np.float32).astype(NPBF))


class NSA:
    def __init__(self, nq=T // QT, Tk=T):
        self.nq = nq
        self.Tk = Tk
        self.nc = bass.Bass("TRN2", target_bir_lowering=False)
        self.es = contextlib.ExitStack()

    def dram_in(self, name, shape, dt):
        return self.nc.dram_tensor(name, list(shape), dt, kind="ExternalInput").ap()

    def sb(self, name, shape, dt):
        return self.es.enter_context(self.nc.sbuf_tensor(name, list(shape), dt))

    def mk(self, name, shape, dt):
        return self.sb(name, shape, dt), Buf()

    def build(self):
        with self.es:
            self.tr = Tracker(self.nc, self.es)
            self._build()
        return self.nc

    @staticmethod
    def _alias(b):
        n = Buf()
        n.writer = b.writer
        n.readers = dict(b.readers)
        return n

    def load_T(self, src, dstT, dstb, nchunks):
        tr = self.tr
        for c0 in range(0, nchunks, 8):
            n = min(8, nchunks - c0)
            st, stb = self.stage[(c0 // 8) % 2]
            tr.dma("sp", st[:, 0:n, :], src[c0 * 128:(c0 + n) * 128, :].rearrange("(c p) d -> p c d", p=128), writes=[stb])
            pt, pb = self.ps[7] if (c0 // 8) % 2 == 0 else self.ps[6]
            pv = pt[:].bitcast(BF16)
            for j in range(n):
                tr.op("pe", lambda e: e.transpose(out=pv[:, j * 128:(j + 1) * 128], in_=st[:, j, :], identity=self.ident[:]),
                      reads=[stb, self.ident_b], writes=[pb])
            eng = "act" if (c0 // 8) % 2 == 0 else "dve"
            o = dstT[:, c0 * 128:(c0 + n) * 128]
            if eng == "act":
                tr.op("act", lambda e: e.copy(out=o, in_=pv[:, 0:n * 128]), reads=[pb], writes=[dstb])
            else:
                tr.op("dve", lambda e: e.tensor_copy(out=o, in_=pv[:, 0:n * 128]), reads=[pb], writes=[dstb])

    def compress(self, src, w1, w2, peT, is_k):
        tr = self.tr
        nch = self.Tk // 128
        ncmp = (self.Tk - 32) // 16 + 1
        R1, R1b = self.R1
        self.load_T(src, R1, R1b, nch)
        w1s, w1b = self.R2
        w1v = w1s[:].rearrange("p (l h) -> p l h", l=32)
        for l0 in range(0, 32, 8):
            tr.dma("pool", w1v[:, l0:l0 + 8, :], w1[l0 * 128:(l0 + 8) * 128, :].rearrange("(l p) h -> p l h", p=128), writes=[w1b])
        w2s, w2b = self.w2s
        tr.dma("pool", w2s[:], w2.rearrange("(c p) d -> p c d", p=128), writes=[w2b])
        pef, pefb = self.pef
        tr.dma("sp", pef[:], peT, writes=[pefb])
        peb, pebb = self.peb
        tr.op("dve", lambda e: e.tensor_copy(out=peb[:], in_=pef[:]), reads=[pefb], writes=[pebb])
        hid, hidb = self.hid
        tr.op("pool", lambda e: e.memset(hid[:], 0.0), writes=[hidb])
        bias, biasb = self.small[0]
        for hc in range(4):
            pt, pb = self.ps[hc % 2]
            for l in range(32):
                tr.op("pe", lambda e: e.matmul(pt[:, 0:1], lhsT=w1v[:, l, hc * 128:(hc + 1) * 128], rhs=peb[:, l:l + 1],
                                               start=(l == 0), stop=(l == 31)), reads=[w1b, pebb], writes=[pb])
            tr.op("dve", lambda e: e.tensor_copy(out=bias[:, hc:hc + 1], in_=pt[:, 0:1]), reads=[pb], writes=[biasb])
        for hc in range(4):
            for cb in range(0, ncmp, 512):
                n = min(512, ncmp - cb)
                pt, pb = self.ps[2 + ((hc * 2 + cb // 512) % 2)]
                for l in range(32):
                    rhs = R1[:, l + 16 * cb:l + 16 * cb + 16 * (n - 1) + 1:16]
                    tr.op("pe", lambda e: e.matmul(pt[:, 0:n], lhsT=w1v[:, l, hc * 128:(hc + 1) * 128], rhs=rhs,
                                                   start=(l == 0), stop=(l == 31)), reads=[w1b, R1b], writes=[pb])
                tr.op("act", lambda e: e.activation(out=hid[:, hc, cb:cb + n], in_=pt[:, 0:n], func=AF.Gelu_apprx_tanh,
                                                    bias=bias[:, hc:hc + 1]), reads=[pb, biasb], writes=[hidb])
        ncc = (ncmp + 127) // 128
        for j in range(ncc):
            pt, pb = self.ps[4 + j % 2]
            for hc in range(4):
                tr.op("pe", lambda e: e.matmul(pt[:, 0:128], lhsT=hid[:, hc, j * 128:(j + 1) * 128], rhs=w2s[:, hc, :],
                                               start=(hc == 0), stop=(hc == 3)), reads=[hidb, w2b], writes=[pb])
            if not is_k:
                tr.op("act", lambda e: e.copy(out=self.vcmp1[:, j, 0:128], in_=pt[:, 0:128]), reads=[pb], writes=[self.vcmp1_b])
            else:
                tmp, tb = self.tmpc[j % 2]
                tr.op("act", lambda e: e.copy(out=tmp[:], in_=pt[:, 0:128]), reads=[pb], writes=[tb])
                x1, x2 = tmp[:, 0:16], tmp[:, 16:32]
                cos, sin = self.crope[:, j, 0:16], self.crope[:, j, 16:32]
                r, rb = self.small[1]
                tr.op("dve", lambda e: e.tensor_tensor(out=r[:, 0:16], in0=x1, in1=cos, op=ALU.mult), reads=[tb, self.crope_b], writes=[rb])
                tr.op("dve", lambda e: e.tensor_tensor(out=r[:, 16:32], in0=x2, in1=sin, op=ALU.mult), reads=[tb, self.crope_b], writes=[rb])
                tr.op("dve", lambda e: e.tensor_tensor(out=r[:, 32:48], in0=x2, in1=cos, op=ALU.mult), reads=[tb, self.crope_b], writes=[rb])
                tr.op("dve", lambda e: e.tensor_tensor(out=r[:, 48:64], in0=x1, in1=sin, op=ALU.mult), reads=[tb, self.crope_b], writes=[rb])
                tr.op("dve", lambda e: e.tensor_tensor(out=x1, in0=r[:, 0:16], in1=r[:, 16:32], op=ALU.subtract), reads=[rb, tb], writes=[tb])
                tr.op("dve", lambda e: e.tensor_tensor(out=x2, in0=r[:, 32:48], in1=r[:, 48:64], op=ALU.add), reads=[rb, tb], writes=[tb])
                tb16, tb16b = self.tmpc16[j % 2]
                tr.op("dve", lambda e: e.tensor_copy(out=tb16[:], in_=tmp[:]), reads=[tb], writes=[tb16b])
                p2, p2b = self.ps[6 + j % 2]
                pv = p2[:].bitcast(BF16)
                tr.op("pe", lambda e: e.transpose(out=pv[:, 0:128], in_=tb16[:], identity=self.ident[:]),
                      reads=[tb16b, self.ident_b], writes=[p2b])
                tr.op("act", lambda e: e.copy(out=self.kcmpT[:, j * 128:(j + 1) * 128], in_=pv[:, 0:128]), reads=[p2b],
                      writes=[self.kcmpT_b])

    def run_phase(self, batches, accs):
        tr = self.tr
        fib = set()
        LAG = 2
        pend = []
        for b in list(batches) + [None] * LAG:
            cur = None
            prev = None
            if b is not None:
                bi = self.bcount
                self.bcount += 1
                S, Sb = self.sbig[bi % self.nsbuf]
                P, Pb = self.pbig[bi % 3]
                nu = len(b["units"])
                if "pre" in b:
                    b["pre"]()
                for u, (kT, kb, h) in enumerate(b["units"]):
                    neg = b.get("neg")
                    tr.op("pe", lambda e: e.matmul(S[:, u * QT:(u + 1) * QT], lhsT=kT, rhs=self.QTt[:, h, :], start=(neg is None or u % 2 == 0), stop=(neg is None),
                                                   skip_group_check=(neg is not None)),
                          reads=list(kb) + [self.QT_b], writes=[Sb])
                    if neg is not None and u % 2 == 1:
                        rhs2 = neg[1].unsqueeze(1).to_broadcast([128, 2, QT])
                        tr.op("pe", lambda e: e.matmul(S[:, (u - 1) * QT:(u + 1) * QT].rearrange("p (a q) -> p a q", a=2), lhsT=neg[0], rhs=rhs2,
                                                       start=False, stop=True, skip_group_check=True),
                              reads=list(neg[2]), writes=[Sb])
                n = nu * QT
                if "post" in b:
                    b["post"]()
                tr.op("act", lambda e: e.activation(out=P[:, 0:n], in_=S[:, 0:n], func=AF.Exp), reads=[Sb], writes=[Pb])
                pv3 = P[:, 0:n].rearrange("p (u q) -> p u q", u=nu)
                for m, mb in b["masks"]:
                    tr.op("dve", lambda e: e.tensor_tensor(out=pv3, in0=pv3, in1=m.unsqueeze(1).to_broadcast([128, nu, QT]), op=ALU.mult),
                          reads=[Pb] + list(mb), writes=[Pb])
                cur = (P, Pb, b)
            pend.append(cur)
            if len(pend) > LAG:
                prev = pend.pop(0)
            if prev is not None:
                P, Pb, pb_ = prev
                v = pb_["v"]
                ncol = v.shape[-1]
                for u, (kT, kb, h) in enumerate(pb_["units"]):
                    for sub in range(2):
                        acc, accb, bank = accs[h][sub]
                        st_flag = bank not in fib
                        fib.add(bank)
                        tr.op("pe", lambda e: e.matmul(acc[:, 0:ncol], lhsT=P[:, u * QT + sub * 128:u * QT + (sub + 1) * 128], rhs=v,
                                                       start=st_flag, stop=True, skip_group_check=True),
                              reads=[Pb] + list(pb_["vb"]), writes=[accb])

    def _build(self):
        import os
        nc, tr = self.nc, self.tr
        Tk = self.Tk
        nch = Tk // 128
        a = {}
        a["q"] = self.dram_in("q", [Tk, 512], BF16)
        for nm in ("kc", "vc", "ks", "vs", "kw", "vw"):
            a[nm] = self.dram_in(nm, [Tk, 128], BF16)
        a["gates"] = self.dram_in("gates", [Tk, 12], F32)
        for nm in ("kc_w1", "vc_w1"):
            a[nm] = self.dram_in(nm, [4096, 512], F32)
        for nm in ("kc_w2", "vc_w2"):
            a[nm] = self.dram_in(nm, [512, 128], F32)
        for nm in ("kc_peT", "vc_peT"):
            a[nm] = self.dram_in(nm, [128, 32], F32)
        a["masks"] = self.dram_in("masks", [128, 13, QT], BF16)
        a["Mtab"] = self.dram_in("Mtab", [128, 8, 256], BF16)
        a["Rtab"] = self.dram_in("Rtab", [128, 64, 128], BF16)
        a["Fbase"] = self.dram_in("Fbase", [128, 512], F32)
        a["crope"] = self.dram_in("crope", [128, 8, 32], F32)
        a["ident"] = self.dram_in("ident", [128, 128], BF16)
        o_out = self.nc.dram_tensor("o_out", [self.nq * QT, 512], BF16, kind="ExternalOutput").ap()
        mk = self.mk
        self.R1 = mk("R1", [128, Tk], BF16)
        self.R2 = mk("R2", [128, max(Tk, 16384)], BF16)
        self.vs1, self.vs1_b = mk("vs1", [128, nch, 129], BF16)
        self.vw1, self.vw1_b = mk("vw1", [128, nch, 129], BF16)
        self.hid = mk("hid", [128, 4, 1024], BF16)
        self.w2s = mk("w2s", [128, 4, 128], BF16)
        self.pef = mk("pef", [128, 32], F32)
        self.peb = mk("peb", [128, 32], BF16)
        self.stage = [mk(f"stage{i}", [128, 8, 128], BF16) for i in range(2)]
        self.small = [mk(f"small{i}", [128, 64], F32) for i in range(4)]
        self.tmpc = [mk(f"tmpc{i}", [128, 128], F32) for i in range(2)]
        self.tmpc16 = [mk(f"tmpc16{i}", [128, 128], BF16) for i in range(2)]
        self.kcmpT, self.kcmpT_b = mk("kcmpT", [128, 1024], BF16)
        self.vcmp1, self.vcmp1_b = mk("vcmp1", [128, 8, 385], BF16)
        self.ident, self.ident_b = mk("ident_sb", [128, 128], BF16)
        self.masks, self.masks_b = mk("masks_sb", [128, 13, QT], BF16)
        self.Rtab, self.Rtab_b = mk("Rtab_sb", [128, 64, 128], BF16)
        self.Fbase, self.Fbase_b = mk("Fbase_sb", [128, 512], F32)
        self.crope, self.crope_b = mk("crope_sb", [128, 8, 32], F32)
        big0 = self.es.enter_context(nc.psum_tensor("psbig0", [128, 1024], F32))
        big1 = self.es.enter_context(nc.psum_tensor("psbig1", [128, 1024], F32))
        b0, b1 = Buf(), Buf()
        self.ps = [(big0[:, 0:512], b0), (big0[:, 512:1024], b0), (big1[:, 0:512], b1), (big1[:, 512:1024], b1)]
        for i in range(4, 8):
            self.ps.append((self.es.enter_context(nc.psum_tensor(f"ps{i}", [128, 512], F32))[:], Buf()))
        self.sbig = [(big0[:], b0), (big1[:], b1)]
        self.nsbuf = 2
        self.bcount = 0
        for t_, b_, src in ((self.ident, self.ident_b, a["ident"]), (self.masks, self.masks_b, a["masks"]),
                            (self.Rtab, self.Rtab_b, a["Rtab"]), (self.Fbase, self.Fbase_b, a["Fbase"]),
                            (self.crope, self.crope_b, a["crope"])):
            tr.dma("sp", t_[:], src, writes=[b_])
            b_.const = True
        tr.op("pool", lambda e: e.memset(self.kcmpT[:], 0.0), writes=[self.kcmpT_b])
        tr.op("pool", lambda e: e.memset(self.vcmp1[:], 0.0), writes=[self.vcmp1_b])
        self.compress(a["kc"], a["kc_w1"], a["kc_w2"], a["kc_peT"], True)
        self.compress(a["vc"], a["vc_w1"], a["vc_w2"], a["vc_peT"], False)
        tr.op("pool", lambda e: e.memset(self.vcmp1[:, :, 128:129], 1.0), reads=[], writes=[self.vcmp1_b])
        tr.dma("sp", self.vcmp1[:, :, 129:385], a["Mtab"], writes=[self.vcmp1_b])
        STOP = ""
        if STOP == "compress":
            tr.finish([]); return
        ksT, ksT_b = self.R1
        kwT, kwT_b = self.R2
        self.load_T(a["ks"], ksT, ksT_b, nch)
        self.load_T(a["kw"], kwT, kwT_b, nch)
        tr.op("pool", lambda e: e.memset(self.vs1[:, :, 128:129], 1.0), writes=[self.vs1_b])
        tr.op("pool", lambda e: e.memset(self.vw1[:, :, 128:129], 1.0), writes=[self.vw1_b])
        for c0 in range(0, nch, 8):
            n = min(8, nch - c0)
            tr.dma("sp", self.vs1[:, c0:c0 + n, 0:128], a["vs"][c0 * 128:(c0 + n) * 128, :].rearrange("(c p) d -> p c d", p=128),
                   writes=[self.vs1_b])
            tr.dma("sp", self.vw1[:, c0:c0 + n, 0:128], a["vw"][c0 * 128:(c0 + n) * 128, :].rearrange("(c p) d -> p c d", p=128),
                   writes=[self.vw1_b])
        if STOP == "kv":
            tr.finish([]); return
        q_tm = [mk(f"q_tm{i}", [128, 2, 512], BF16) for i in range(2)]
        self.QTt, self.QT_b = mk("QTt", [128, 4, QT], BF16)
        self.pbig = [mk(f"pbig{i}", [128, 4 * QT], BF16) for i in range(3)]
        o_acc, o_accb = mk("o_acc", [128, 2, 512], F32)
        o_bf = [mk(f"o_bf{i}", [128, 2, 512], BF16) for i in range(2)]
        imp = [mk(f"imp{i}", [128, 256], F32) for i in range(2)]
        score = [mk(f"score{i}", [128, 256], F32) for i in range(2)]
        selb = [mk(f"selb{i}", [128, 256], BF16) for i in range(2)]
        selT, selT_b = mk("selT", [128, 2, QT], BF16)
        gts = [mk(f"gts{i}", [128, 2, 12], F32) for i in range(2)]
        mslot = []
        for i in range(2):
            pt, pbk = self.ps[7]
            mslot.append((pt[:, i * 256:(i + 1) * 256], self._alias(pbk)))
        pb7s = [self.ps[7][1], mslot[0][1], mslot[1][1]]
        msk = [mk(f"msk{i}", [128, QT], BF16) for i in range(2)]
        acc8 = [[None, None] for _ in range(4)]
        for h in range(4):
            for sub in range(2):
                idx = h * 2 + sub
                pt, pbk = self.ps[4 + idx // 3]
                acc8[h][sub] = (pt[:, (idx % 3) * 129:(idx % 3) * 129 + 129], pbk, 4 + idx // 3)
        out_evs = []
        I0 = 0
        for i in range(I0, self.nq):
            t0 = i * QT
            qt_, qtb = q_tm[i % 2]
            tr.dma("sp", qt_[:], a["q"][t0:t0 + QT, :].rearrange("(s p) c -> p s c", p=128), writes=[qtb])
            g, gb = gts[i % 2]
            tr.dma("sp", g[:], a["gates"][t0:t0 + QT, :].rearrange("(s p) c -> p s c", p=128), writes=[gb])
            pt, pb = self.ps[7]
            pv = pt[:].bitcast(BF16)
            for sub in range(2):
                for h in range(4):
                    tr.op("pe", lambda e: e.transpose(out=pv[:, (sub * 4 + h) * 128:(sub * 4 + h + 1) * 128],
                                                      in_=qt_[:, sub, h * 128:(h + 1) * 128], identity=self.ident[:]),
                          reads=[qtb, self.ident_b], writes=pb7s)
            for sub in range(2):
                tr.op("act" if sub == 0 else "dve",
                      (lambda e: e.copy(out=self.QTt[:, :, sub * 128:(sub + 1) * 128],
                                        in_=pv[:, sub * 512:(sub + 1) * 512].rearrange("p (h q) -> p h q", h=4))) if sub == 0 else
                      (lambda e: e.tensor_copy(out=self.QTt[:, :, sub * 128:(sub + 1) * 128],
                                               in_=pv[:, sub * 512:(sub + 1) * 512].rearrange("p (h q) -> p h q", h=4))),
                      reads=pb7s, writes=[self.QT_b])
            jmax = min((16 * i + 14) // 128, 7)
            for hp in range(2):
                accs = {}
                cbanks = [3, 4, 5, 6]
                for hh in range(2):
                    h = hp * 2 + hh
                    accs[h] = []
                    for sub in range(2):
                        bk = cbanks[hh * 2 + sub]
                        pt_, pb_ = self.ps[bk]
                        accs[h].append((pt_, pb_, bk))
                batches = []
                for j in range(jmax + 1):
                    r = i - 8 * j
                    masks = [(self.masks[:, 4 + r, :], [self.masks_b])] if r <= 8 else []
                    batches.append(dict(units=[(self.kcmpT[:, j * 128:(j + 1) * 128], [self.kcmpT_b], hp * 2 + hh) for hh in range(2)],
                                        v=self.vcmp1[:, j, :], vb=[self.vcmp1_b], masks=masks))
                self.nsbuf = 1
                self.run_phase(batches, accs)
                self.nsbuf = 2
                sm, smb = self.small[1]
                for hh in range(2):
                    for sub in range(2):
                        acc, accb, _ = accs[hp * 2 + hh][sub]
                        k = hh * 2 + sub
                        tr.op("dve", lambda e: e.tensor_scalar(out=sm[:, k:k + 1], in0=acc[:, 128:129], scalar1=1e-30, scalar2=1.0,
                                                               op0=ALU.max, op1=ALU.mult), reads=[accb], writes=[smb])
                tr.op("dve", lambda e: e.reciprocal(out=sm[:, 8:12], in_=sm[:, 0:4]), reads=[smb], writes=[smb])
                tr.op("dve", lambda e: e.tensor_tensor(out=sm[:, 16:20].rearrange("p (h s) -> p h s", s=2),
                                                       in0=sm[:, 8:12].rearrange("p (h s) -> p h s", s=2),
                                                       in1=g[:, :, hp * 2:hp * 2 + 2].rearrange("p s h -> p h s"), op=ALU.mult),
                      reads=[smb, gb], writes=[smb])
                for hh in range(2):
                    h = hp * 2 + hh
                    for sub in range(2):
                        acc, accb, _ = accs[h][sub]
                        k = hh * 2 + sub
                        tr.op("dve", lambda e: e.tensor_scalar(out=o_acc[:, sub, h * 128:(h + 1) * 128], in0=acc[:, 0:128],
                                                               scalar1=sm[:, 16 + k:17 + k], scalar2=1.0, op0=ALU.mult, op1=ALU.mult),
                              reads=[accb, smb], writes=[o_accb])
                        im, imb = imp[sub]
                        if h == 0:
                            tr.op("dve", lambda e: e.tensor_scalar(out=im[:], in0=acc[:, 129:385], scalar1=sm[:, 8 + k:9 + k], scalar2=1.0,
                                                                   op0=ALU.mult, op1=ALU.mult), reads=[accb, smb], writes=[imb])
                        else:
                            tr.op("dve", lambda e: e.scalar_tensor_tensor(out=im[:], in0=acc[:, 129:385], scalar=sm[:, 8 + k:9 + k],
                                                                          in1=im[:], op0=ALU.mult, op1=ALU.add),
                                  reads=[accb, smb, imb], writes=[imb])
            if STOP == "cmp":
                continue
            for sub in range(2):
                qt128 = 2 * i + sub
                im, imb = imp[sub]
                sc, scb = score[sub]
                off = 256 - 2 * qt128
                tr.op("dve", lambda e: e.tensor_tensor(out=sc[:], in0=im[:], in1=self.Fbase[:, off:off + 256], op=ALU.add),
                      reads=[imb, self.Fbase_b], writes=[scb])
                tr.op("dve", lambda e: e.memset(sc[:, 0:1], 1e9), writes=[scb], reads=[scb])
                m8, m8b = self.small[2 + sub]
                tr.op("dve", lambda e: e.max(out=m8[:, 0:8], in_=sc[:]), reads=[scb], writes=[m8b])
                tr.op("dve", lambda e: e.match_replace(out=im[:], in_to_replace=m8[:, 0:8], in_values=sc[:], imm_value=-3e38),
                      reads=[scb, m8b], writes=[imb])
                tr.op("dve", lambda e: e.max(out=m8[:, 8:16], in_=im[:]), reads=[imb], writes=[m8b])
                sb_, sbb = selb[sub]
                tr.op("dve", lambda e: e.tensor_scalar(out=sb_[:], in0=sc[:], scalar1=m8[:, 15:16], scalar2=1.0, op0=ALU.is_ge, op1=ALU.mult),
                      reads=[scb, m8b], writes=[sbb])
                pt, pb = self.ps[7]
                pv = pt[:].bitcast(BF16)
                for sc_ in range(2):
                    tr.op("pe", lambda e: e.transpose(out=pv[:, sc_ * 128:(sc_ + 1) * 128], in_=sb_[:, sc_ * 128:(sc_ + 1) * 128],
                                                      identity=self.ident[:]), reads=[sbb, self.ident_b], writes=pb7s)
                tr.op("dve", lambda e: e.tensor_scalar(out=selT[:, :, sub * 128:(sub + 1) * 128],
                                                       in0=pv[:, 0:256].rearrange("p (c q) -> p c q", c=2),
                                                       scalar1=30000.0, scalar2=-30000.0, op0=ALU.mult, op1=ALU.add),
                      reads=pb7s, writes=[selT_b])
            if STOP == "topk":
                continue
            accs = {h: acc8[h] for h in range(4)}
            batches = []
            for kc in range(0, 2 * i + 2):
                r = kc - 2 * i
                masks = [(self.masks[:, r, :], [self.masks_b])] if r >= 0 else []
                batches.append(dict(units=[(ksT[:, kc * 128:(kc + 1) * 128], [ksT_b], h) for h in range(4)],
                                    v=self.vs1[:, kc, :], vb=[self.vs1_b], masks=masks,
                                    neg=(self.Rtab[:, kc % 64, :], selT[:, kc // 64, :], [self.Rtab_b, selT_b])))
            self.run_phase(batches, accs)
            self.evac_branch(acc8, g, gb, 1, o_acc, o_accb)
            if STOP == "sel":
                continue
            batches = []
            for kc in range(max(0, 2 * i - 4), 2 * i + 2):
                r = kc - 2 * i
                mi = {-4: 2, -3: 3, 0: 0, 1: 1}.get(r)
                masks = [(self.masks[:, mi, :], [self.masks_b])] if mi is not None else []
                batches.append(dict(units=[(kwT[:, kc * 128:(kc + 1) * 128], [kwT_b], h) for h in range(4)],
                                    v=self.vw1[:, kc, :], vb=[self.vw1_b], masks=masks))
            self.run_phase(batches, accs)
            if STOP == "win":
                continue
            self.evac_branch(acc8, g, gb, 2, o_acc, o_accb)
            if STOP == "winevac":
                continue
            ob, obb = o_bf[i % 2]
            tr.op("dve", lambda e: e.tensor_copy(out=ob[:], in_=o_acc[:]), reads=[o_accb], writes=[obb])
            for sub in range(2):
                out_evs.append(tr.dma("sp", o_out[t0 + sub * 128:t0 + (sub + 1) * 128, :], ob[:, sub, :], reads=[obb]))
        tr.finish(out_evs)

    def evac_branch(self, acc8, g, gb, br, o_acc, o_accb):
        tr = self.tr
        sm, smb = self.small[0]
        for bi, (bank, n) in enumerate(((4, 3), (5, 3), (6, 2))):
            pt, pbk = self.ps[bank]
            den = pt[:, 0:n * 129].rearrange("p (a c) -> p a c", c=129)[:, :, 128]
            tr.op("dve", lambda e: e.tensor_scalar(out=sm[:, bi * 3:bi * 3 + n], in0=den, scalar1=1e-30, scalar2=1.0,
                                                   op0=ALU.max, op1=ALU.mult), reads=[pbk], writes=[smb])
        tr.op("dve", lambda e: e.reciprocal(out=sm[:, 8:16], in_=sm[:, 0:8]), reads=[smb], writes=[smb])
        tr.op("dve", lambda e: e.tensor_tensor(out=sm[:, 16:24].rearrange("p (h s) -> p h s", s=2),
                                               in0=sm[:, 8:16].rearrange("p (h s) -> p h s", s=2),
                                               in1=g[:, :, br * 4:br * 4 + 4].rearrange("p s h -> p h s"), op=ALU.mult),
              reads=[smb, gb], writes=[smb])
        for h in range(4):
            for sub in range(2):
                acc, accb, _ = acc8[h][sub]
                idx = h * 2 + sub
                o = o_acc[:, sub, h * 128:(h + 1) * 128]
                tr.op("dve", lambda e: e.scalar_tensor_tensor(out=o, in0=acc[:, 0:128], scalar=sm[:, 16 + idx:17 + idx], in1=o,
                                                              op0=ALU.mult, op1=ALU.add), reads=[accb, smb, o_accb], writes=[o_accb])


_CACHE = {}


def _prog(key, fn):
    if key not in _CACHE:
        _CACHE[key] = fn()
    return _CACHE[key]


def kernel(x, p, norm_mix, norm_ffn, norm_ple, ffn_up, ffn_down, ple_proj, ple_gate,
           gm_in, gm_ln_g, gm_ln_b, gm_ws, gm_bs, gm_out,
           nsa_in, nsa_kc_pe, nsa_kc_w1, nsa_kc_w2, nsa_vc_pe, nsa_vc_w1, nsa_vc_w2, nsa_out, final_norm):
    f32 = np.float32
    A = lambda v: np.ascontiguousarray(np.asarray(v, dtype=f32))
    x = A(x).reshape(B * T, D)
    p = A(p).reshape(2, B * T, 256)
    norm_mix, norm_ffn, norm_ple = A(norm_mix), A(norm_ffn), A(norm_ple)
    ident = np.eye(128, dtype=f32).astype(NPBF)
    NTOK = B * T // NCORES
    pos = np.arange(T, dtype=f32)
    inv = np.power(f32(500000.0), -np.arange(16, dtype=f32) * f32(2.0) / f32(32))
    ang = pos[:, None] * inv[None, :]
    rope = np.concatenate([np.cos(ang), np.sin(ang)], axis=1).astype(f32)
    rope = np.concatenate([rope, rope], axis=0)
    cores = list(range(NCORES))
    nc1 = Dense(1).build()
    common1 = dict(ident=ident, norm_mix=norm_mix, norm_ffn=norm_ffn, norm_ple=norm_ple,
                   ffn_up1=A(ffn_up[0]), ffn_down1=A(ffn_down[0]), ple_proj1=A(ple_proj[0]), ple_gate1=A(ple_gate[0]),
                   gm_in=A(gm_in[0]), gm_out=A(gm_out[0]), gm_ln_g=A(gm_ln_g).reshape(1, D), gm_ln_b=A(gm_ln_b).reshape(1, D),
                   wsT=np.ascontiguousarray(A(gm_ws[0]).transpose(2, 0, 1)), cmask=np.triu(np.ones((128, 128), f32)),
                   bsT=A(gm_bs[0]).reshape(1, 2048), nsa_in=A(nsa_in[0]))
    maps = []
    for c in cores:
        sl = slice(c * NTOK, (c + 1) * NTOK)
        maps.append(dict(common1, x=x[sl], p=p[0, sl], rope=rope[sl]))
    r1 = run_bass_kernel_spmd(nc1, maps, core_ids=cores).results
    x1 = np.concatenate([r["x_out"] for r in r1], axis=0)
    qkv = np.concatenate([r["qkv_out"] for r in r1], axis=0)
    gates = np.concatenate([r["gates_out"] for r in r1], axis=0)
    del r1, maps
    nc2 = NSA().build()
    C = nsa_consts()
    common2 = dict(C, kc_w1=A(nsa_kc_w1[0]), vc_w1=A(nsa_vc_w1[0]), kc_w2=A(nsa_kc_w2[0]), vc_w2=A(nsa_vc_w2[0]),
                   kc_peT=np.ascontiguousarray(A(nsa_kc_pe[0]).T), vc_peT=np.ascontiguousarray(A(nsa_vc_pe[0]).T))
    maps = []
    for c in cores:
        b, g = c // 4, c % 4
        rows = slice(b * T, (b + 1) * T)
        m = dict(common2)
        m["q"] = np.ascontiguousarray(qkv[rows, g * 512:(g + 1) * 512])
        for i, nm in enumerate(("kc", "vc", "ks", "vs", "kw", "vw")):
            m[nm] = np.ascontiguousarray(qkv[rows, 2048 + i * 512 + g * 128:2048 + i * 512 + (g + 1) * 128])
        m["gates"] = np.ascontiguousarray(np.concatenate([gates[rows, br * 16 + g * 4:br * 16 + g * 4 + 4] for br in range(3)], axis=1))
        maps.append(m)
    r2 = run_bass_kernel_spmd(nc2, maps, core_ids=cores).results
    o = np.empty((B * T, D), dtype=NPBF)
    for c in cores:
        b, g = c // 4, c % 4
        o[b * T:(b + 1) * T, g * 512:(g + 1) * 512] = r2[c]["o_out"]
    del r2, maps, qkv
    nc3 = Dense(3).build()
    common3 = dict(ident=ident, norm_mix=norm_mix, norm_ffn=norm_ffn, norm_ple=norm_ple,
                   ffn_up3=A(ffn_up[1]), ffn_down3=A(ffn_down[1]), ple_proj3=A(ple_proj[1]), ple_gate3=A(ple_gate[1]),
                   nsa_out=A(nsa_out[0]), final_norm=A(final_norm).reshape(1, D))
    maps = []
    for c in cores:
        sl = slice(c * NTOK, (c + 1) * NTOK)
        maps.append(dict(common3, x=x1[sl], p=p[1, sl], o_in=np.ascontiguousarray(o[sl])))
    r3 = run_bass_kernel_spmd(nc3, maps, core_ids=cores).results
    out = np.concatenate([r["x_out"] for r in r3], axis=0).reshape(B, T, D).astype(f32)
    return out
```
